# Optimizing a Trainium2 kernel written in Bass

```python
import math
import jax, jax.numpy as jnp
from jax import lax
import numpy as np

D_MODEL = 1024
BATCH = 8
SEQ = 2048
DEPTH = 4
DEC_BATCH = 128
DEC_SEQ = 1
PAST_LEN = 2048
PAGE_SIZE = 128

N_MIXERS = 3
N_SCONV_LAYERS = (DEPTH + 2) // 3
N_SSD_LAYERS = (DEPTH + 1) // 3
N_ATTN_LAYERS = DEPTH // 3
RMS_EPS = 1e-6
SCONV_WIDTH = 3
SSD_EXPAND = 2
D_INNER = SSD_EXPAND * D_MODEL
SSD_HEAD_DIM = 64
SSD_HEADS = D_INNER // SSD_HEAD_DIM
SSD_GROUPS = 4
SSD_HEADS_PER_GROUP = SSD_HEADS // SSD_GROUPS
SSD_STATE = 128
SSD_CONV_WIDTH = 4
SSD_CONV_DIM = D_INNER + 2 * SSD_GROUPS * SSD_STATE
SSD_IN_DIM = D_INNER + SSD_CONV_DIM + SSD_HEADS
SSD_CHUNK = 128
ATTN_HEAD_DIM = 64
ATTN_HEADS = D_MODEL // ATTN_HEAD_DIM
MOBA_BLOCK = 256
MOBA_TOP_K = 3
MOBA_Q_BLOCK = 16
ROPE_THETA = 500000.0
ROT_DIM = ATTN_HEAD_DIM // 4
D_FF = 2816
FFN_CONV_WIDTH = 3

kernel_name = "hybrid_sconv_ssd_moba_convffn_step"


def rmsnorm(x, w):
    xf = x.astype(jnp.float32)
    inv = lax.rsqrt(jnp.mean(xf * xf, axis=-1, keepdims=True) + RMS_EPS)
    return (xf * inv).astype(x.dtype) * w


def causal_dwconv(u, w, past):
    width = w.shape[0]
    L = u.shape[1]
    full = jnp.concatenate([past.astype(u.dtype), u], axis=1)
    out = sum(full[:, k:k + L] * w[k] for k in range(width))
    return out, full[:, L:]


def short_conv_mixer(h, w_in, conv_w, w_out, past):
    gate_out, gate_in, val = jnp.split(h @ w_in, 3, axis=-1)
    c, new_past = causal_dwconv(gate_in * val, conv_w, past)
    return (gate_out * c) @ w_out, new_past


def ssd_scan(x, dt, a, b_in, c_in, init_state):
    bsz, L = x.shape[:2]
    q = min(SSD_CHUNK, L)
    pad = (-L) % q
    if pad:
        padf = lambda t: jnp.pad(t, [(0, 0), (0, pad)] + [(0, 0)] * (t.ndim - 2))
        x, dt, b_in, c_in = padf(x), padf(dt), padf(b_in), padf(c_in)
    nc = (L + pad) // q
    G, E, P, N = SSD_GROUPS, SSD_HEADS_PER_GROUP, SSD_HEAD_DIM, SSD_STATE
    xc = x.reshape(bsz, nc, q, G, E, P).astype(jnp.float32)
    dtc = dt.reshape(bsz, nc, q, G, E).astype(jnp.float32)
    bc = b_in.reshape(bsz, nc, q, G, N).astype(jnp.float32)
    cc = c_in.reshape(bsz, nc, q, G, N).astype(jnp.float32)
    a_cum = jnp.cumsum(dtc * a.reshape(G, E), axis=2)
    xdt = xc * dtc[..., None]
    seg = a_cum[:, :, :, None] - a_cum[:, :, None, :]
    causal = jnp.tril(jnp.ones((q, q), bool))[:, :, None, None]
    decay = jnp.exp(jnp.where(causal, seg, -jnp.inf))
    cb = jnp.einsum('bclgn,bcsgn->bclsg', cc, bc)
    y_diag = jnp.einsum('bclsg,bclsge,bcsgep->bclgep', cb, decay, xdt)
    decay_to_end = jnp.exp(a_cum[:, :, -1:] - a_cum)
    states = jnp.einsum('bcsgn,bcsge,bcsgep->bcgepn', bc, decay_to_end, xdt)
    chunk_decay = jnp.exp(a_cum[:, :, -1])

    def step(carry, inp):
        st, dec = inp
        return carry * dec[..., None, None] + st, carry

    init = init_state.astype(jnp.float32).reshape(bsz, G, E, P, N)
    final, prev = lax.scan(step, init, (jnp.moveaxis(states, 1, 0), jnp.moveaxis(chunk_decay, 1, 0)))
    prev = jnp.moveaxis(prev, 0, 1)
    y_off = jnp.einsum('bclgn,bcgepn,bclge->bclgep', cc, prev, jnp.exp(a_cum))
    y = (y_diag + y_off).reshape(bsz, nc * q, SSD_HEADS, P)[:, :L]
    return y, final.reshape(bsz, SSD_HEADS, P, N)


def ssd_mixer(h, w_in, conv_w, conv_b, dt_bias, a_log, d_skip, norm_w, w_out, ssm_state, conv_past):
    bsz, L, _ = h.shape
    z, xbc, dt = jnp.split(h @ w_in, [D_INNER, D_INNER + SSD_CONV_DIM], axis=-1)
    xbc_c, new_conv = causal_dwconv(xbc, conv_w, conv_past)
    xbc_c = jax.nn.silu(xbc_c + conv_b)
    xs, b_in, c_in = jnp.split(xbc_c, [D_INNER, D_INNER + SSD_GROUPS * SSD_STATE], axis=-1)
    xs = xs.reshape(bsz, L, SSD_HEADS, SSD_HEAD_DIM)
    b_in = b_in.reshape(bsz, L, SSD_GROUPS, SSD_STATE)
    c_in = c_in.reshape(bsz, L, SSD_GROUPS, SSD_STATE)
    dt = jax.nn.softplus((dt + dt_bias).astype(jnp.float32))
    a = -jnp.exp(a_log.astype(jnp.float32))
    y, new_state = ssd_scan(xs, dt, a, b_in, c_in, ssm_state)
    y = (y + xs * d_skip[:, None]).astype(h.dtype)
    y = y.reshape(bsz, L, D_INNER) * jax.nn.silu(z)
    y = rmsnorm(y.reshape(bsz, L, SSD_GROUPS, D_INNER // SSD_GROUPS),
                norm_w.reshape(SSD_GROUPS, D_INNER // SSD_GROUPS)).reshape(bsz, L, D_INNER)
    return y @ w_out, new_conv, new_state


def partial_rotary(t, pos):
    half = ROT_DIM // 2
    inv_freq = ROPE_THETA ** (-(jnp.arange(half, dtype=jnp.float32) * 2.0) / ROT_DIM)
    ang = pos.astype(jnp.float32)[:, None] * inv_freq
    cos = jnp.cos(ang)[None, :, None, :]
    sin = jnp.sin(ang)[None, :, None, :]
    t1 = t[..., :half].astype(jnp.float32)
    t2 = t[..., half:ROT_DIM].astype(jnp.float32)
    r1 = (t1 * cos - t2 * sin).astype(t.dtype)
    r2 = (t2 * cos + t1 * sin).astype(t.dtype)
    return jnp.concatenate([r1, r2, t[..., ROT_DIM:]], axis=-1)


def moba_attention(q, k, v, q_start):
    bsz, lq, H, Dh = q.shape
    lk = k.shape[1]
    qb = min(MOBA_Q_BLOCK, lq)
    lq_pad = -(-lq // qb) * qb
    nblk = -(-(q_start + lq_pad) // MOBA_BLOCK)
    kpad = [(0, 0), (0, nblk * MOBA_BLOCK - lk), (0, 0), (0, 0)]
    k_blk = jnp.pad(k, kpad).reshape(bsz, nblk, MOBA_BLOCK, H, Dh).transpose(0, 3, 1, 2, 4)
    v_blk = jnp.pad(v, kpad).reshape(bsz, nblk, MOBA_BLOCK, H, Dh).transpose(0, 3, 1, 2, 4)
    n_score = max(nblk, MOBA_TOP_K)
    k_mean = jnp.mean(k_blk.astype(jnp.float32), axis=3)
    k_mean = jnp.pad(k_mean, [(0, 0), (0, 0), (0, n_score - nblk), (0, 0)])
    q = jnp.pad(q, [(0, 0), (0, lq_pad - lq), (0, 0), (0, 0)])
    nsub = lq_pad // qb
    q_sub = q.reshape(bsz, nsub, qb, H, Dh).transpose(1, 0, 3, 2, 4)
    pos_sub = (q_start + jnp.arange(lq_pad)).reshape(nsub, qb)
    scale = Dh ** -0.5
    bi = jnp.arange(bsz)[:, None, None, None]
    hi = jnp.arange(H)[None, :, None, None]

    def attend_block(args):
        qs, ps = args
        own = ps // MOBA_BLOCK
        gate = jnp.einsum('bhqd,bhnd->bhqn', qs.astype(jnp.float32), k_mean)
        fully_past = jnp.arange(n_score)[None, :] < own[:, None]
        gate = jnp.where(fully_past, gate, -jnp.inf)
        _, top = lax.top_k(gate, MOBA_TOP_K)
        sel_valid = top < own[:, None]
        own_idx = jnp.broadcast_to(own[:, None], (bsz, H, qb, 1))
        sel = jnp.concatenate([jnp.minimum(top, nblk - 1), own_idx], axis=-1)
        k_sel = k_blk[bi, hi, sel]
        v_sel = v_blk[bi, hi, sel]
        logits = jnp.einsum('bhqd,bhqskd->bhqsk', qs, k_sel).astype(jnp.float32) * scale
        key_pos = own[:, None] * MOBA_BLOCK + jnp.arange(MOBA_BLOCK)[None, :]
        own_ok = (key_pos <= ps[:, None])[None, None, :, None, :]
        mask = jnp.concatenate([
            jnp.broadcast_to(sel_valid[..., None], (bsz, H, qb, MOBA_TOP_K, MOBA_BLOCK)),
            jnp.broadcast_to(own_ok, (bsz, H, qb, 1, MOBA_BLOCK))], axis=3)
        logits = jnp.where(mask, logits, -jnp.inf)
        p = jax.nn.softmax(logits.reshape(bsz, H, qb, -1), axis=-1).reshape(logits.shape)
        return jnp.einsum('bhqsk,bhqskd->bhqd', p.astype(v_sel.dtype), v_sel)

    out = lax.map(attend_block, (q_sub, pos_sub))
    return out.transpose(1, 0, 3, 2, 4).reshape(bsz, lq_pad, H, Dh)[:, :lq]


def moba_mixer(h, w_qkv, w_o, pos, q_start, k_past, v_past):
    bsz, L, _ = h.shape
    q, k, v = jnp.split(h @ w_qkv, 3, axis=-1)
    q = partial_rotary(q.reshape(bsz, L, ATTN_HEADS, ATTN_HEAD_DIM), pos)
    k = partial_rotary(k.reshape(bsz, L, ATTN_HEADS, ATTN_HEAD_DIM), pos)
    v = v.reshape(bsz, L, ATTN_HEADS, ATTN_HEAD_DIM)
    if k_past is None:
        k_all, v_all = k, v
    else:
        k_all = jnp.concatenate([k_past.astype(k.dtype), k], axis=1)
        v_all = jnp.concatenate([v_past.astype(v.dtype), v], axis=1)
    o = moba_attention(q, k_all, v_all, q_start)
    return o.reshape(bsz, L, D_MODEL) @ w_o, k, v


def conv_ffn(h, w_up, conv_w, conv_b, w_down, past):
    u = h @ w_up
    c, new_past = causal_dwconv(u, conv_w, past)
    a, g = jnp.split(c + conv_b, 2, axis=-1)
    return (jax.nn.silu(g) * a) @ w_down, new_past


def setup_inputs(seed: int = 0) -> dict:
    key = jax.random.key(seed)
    ks = jax.random.split(key, 40)
    f32 = jnp.float32
    nrm = lambda k, shape, s: jax.random.normal(k, shape, f32) * s
    n_pages = PAST_LEN // PAGE_SIZE
    n_pool = (5 * DEC_BATCH * n_pages) // 4
    perm = jax.random.permutation(ks[0], n_pool)
    page_table = perm[:DEC_BATCH * n_pages].reshape(DEC_BATCH, n_pages).astype(jnp.int32)
    dt0 = jnp.exp(jax.random.uniform(ks[1], (N_SSD_LAYERS, SSD_HEADS), f32, math.log(1e-3), math.log(1e-1)))
    return {
        "x_prompt": nrm(ks[2], (BATCH, SEQ, D_MODEL), 1.0),
        "x_sample": nrm(ks[3], (DEC_BATCH, DEC_SEQ, D_MODEL), 1.0),
        "state_sconv": nrm(ks[4], (N_SCONV_LAYERS, DEC_BATCH, SCONV_WIDTH - 1, D_MODEL), 0.5),
        "state_ssm": nrm(ks[5], (N_SSD_LAYERS, DEC_BATCH, SSD_HEADS, SSD_HEAD_DIM, SSD_STATE), 0.1),
        "state_ssm_conv": nrm(ks[6], (N_SSD_LAYERS, DEC_BATCH, SSD_CONV_WIDTH - 1, SSD_CONV_DIM), 0.5),
        "cache_k": nrm(ks[7], (N_ATTN_LAYERS, n_pool, PAGE_SIZE, ATTN_HEADS, ATTN_HEAD_DIM), 1.0),
        "cache_v": nrm(ks[8], (N_ATTN_LAYERS, n_pool, PAGE_SIZE, ATTN_HEADS, ATTN_HEAD_DIM), 1.0),
        "state_ffn_conv": nrm(ks[9], (DEPTH, DEC_BATCH, FFN_CONV_WIDTH - 1, 2 * D_FF), 0.5),
        "page_table": page_table,
        "norm_mix_w": 1.0 + nrm(ks[10], (DEPTH, D_MODEL), 0.02),
        "norm_ffn_w": 1.0 + nrm(ks[11], (DEPTH, D_MODEL), 0.02),
        "norm_final_w": 1.0 + nrm(ks[12], (D_MODEL,), 0.02),
        "sconv_w_in": nrm(ks[13], (N_SCONV_LAYERS, D_MODEL, 3 * D_MODEL), D_MODEL ** -0.5),
        "sconv_conv_w": nrm(ks[14], (N_SCONV_LAYERS, SCONV_WIDTH, D_MODEL), SCONV_WIDTH ** -0.5),
        "sconv_w_out": nrm(ks[15], (N_SCONV_LAYERS, D_MODEL, D_MODEL), D_MODEL ** -0.5),
        "ssd_w_in": nrm(ks[16], (N_SSD_LAYERS, D_MODEL, SSD_IN_DIM), D_MODEL ** -0.5),
        "ssd_conv_w": nrm(ks[17], (N_SSD_LAYERS, SSD_CONV_WIDTH, SSD_CONV_DIM), SSD_CONV_WIDTH ** -0.5),
        "ssd_conv_b": nrm(ks[18], (N_SSD_LAYERS, SSD_CONV_DIM), 0.02),
        "ssd_dt_bias": dt0 + jnp.log(-jnp.expm1(-dt0)),
        "ssd_a_log": jnp.log(jax.random.uniform(ks[19], (N_SSD_LAYERS, SSD_HEADS), f32, 1.0, 16.0)),
        "ssd_d": 1.0 + nrm(ks[20], (N_SSD_LAYERS, SSD_HEADS), 0.02),
        "ssd_norm_w": 1.0 + nrm(ks[21], (N_SSD_LAYERS, D_INNER), 0.02),
        "ssd_w_out": nrm(ks[22], (N_SSD_LAYERS, D_INNER, D_MODEL), D_INNER ** -0.5),
        "attn_w_qkv": nrm(ks[23], (N_ATTN_LAYERS, D_MODEL, 3 * D_MODEL), D_MODEL ** -0.5),
        "attn_w_o": nrm(ks[24], (N_ATTN_LAYERS, D_MODEL, D_MODEL), D_MODEL ** -0.5),
        "ffn_w_up": nrm(ks[25], (DEPTH, D_MODEL, 2 * D_FF), D_MODEL ** -0.5),
        "ffn_conv_w": nrm(ks[26], (DEPTH, FFN_CONV_WIDTH, 2 * D_FF), FFN_CONV_WIDTH ** -0.5),
        "ffn_conv_b": nrm(ks[27], (DEPTH, 2 * D_FF), 0.02),
        "ffn_w_down": nrm(ks[28], (DEPTH, D_FF, D_MODEL), D_FF ** -0.5),
    }


def reference(x_prompt, x_sample, state_sconv, state_ssm, state_ssm_conv, cache_k, cache_v,
              state_ffn_conv, page_table, norm_mix_w, norm_ffn_w, norm_final_w,
              sconv_w_in, sconv_conv_w, sconv_w_out,
              ssd_w_in, ssd_conv_w, ssd_conv_b, ssd_dt_bias, ssd_a_log, ssd_d, ssd_norm_w, ssd_w_out,
              attn_w_qkv, attn_w_o, ffn_w_up, ffn_conv_w, ffn_conv_b, ffn_w_down):
    n_pages = PAST_LEN // PAGE_SIZE
    pos_p = jnp.arange(SEQ)
    pos_s = PAST_LEN + jnp.arange(DEC_SEQ)
    xp, xs = x_prompt, x_sample
    dt = xp.dtype
    sconv_p, sconv_s, ssm_p, ssm_s, ssmc_p, ssmc_s = [], [], [], [], [], []
    k_p, v_p, k_s, v_s, ffnc_p, ffnc_s = [], [], [], [], [], []
    for i in range(DEPTH):
        kind, j = i % N_MIXERS, i // N_MIXERS
        hp = rmsnorm(xp, norm_mix_w[i])
        hs = rmsnorm(xs, norm_mix_w[i])
        if kind == 0:
            zero = jnp.zeros((BATCH, SCONV_WIDTH - 1, D_MODEL), dt)
            op, st_p = short_conv_mixer(hp, sconv_w_in[j], sconv_conv_w[j], sconv_w_out[j], zero)
            os_, st_s = short_conv_mixer(hs, sconv_w_in[j], sconv_conv_w[j], sconv_w_out[j], state_sconv[j])
            sconv_p.append(st_p)
            sconv_s.append(st_s)
        elif kind == 1:
            zc = jnp.zeros((BATCH, SSD_CONV_WIDTH - 1, SSD_CONV_DIM), dt)
            zs = jnp.zeros((BATCH, SSD_HEADS, SSD_HEAD_DIM, SSD_STATE), jnp.float32)
            args = (ssd_w_in[j], ssd_conv_w[j], ssd_conv_b[j], ssd_dt_bias[j], ssd_a_log[j], ssd_d[j],
                    ssd_norm_w[j], ssd_w_out[j])
            op, cp, sp = ssd_mixer(hp, *args, zs, zc)
            os_, cs, ss = ssd_mixer(hs, *args, state_ssm[j], state_ssm_conv[j])
            ssmc_p.append(cp)
            ssmc_s.append(cs)
            ssm_p.append(sp)
            ssm_s.append(ss)
        else:
            kp_past = cache_k[j][page_table].reshape(DEC_BATCH, n_pages * PAGE_SIZE, ATTN_HEADS, ATTN_HEAD_DIM)
            vp_past = cache_v[j][page_table].reshape(DEC_BATCH, n_pages * PAGE_SIZE, ATTN_HEADS, ATTN_HEAD_DIM)
            op, kp, vp = moba_mixer(hp, attn_w_qkv[j], attn_w_o[j], pos_p, 0, None, None)
            os_, ks_, vs_ = moba_mixer(hs, attn_w_qkv[j], attn_w_o[j], pos_s, PAST_LEN, kp_past, vp_past)
            k_p.append(kp)
            v_p.append(vp)
            k_s.append(ks_)
            v_s.append(vs_)
        xp = xp + op
        xs = xs + os_
        hp = rmsnorm(xp, norm_ffn_w[i])
        hs = rmsnorm(xs, norm_ffn_w[i])
        zf = jnp.zeros((BATCH, FFN_CONV_WIDTH - 1, 2 * D_FF), dt)
        fp, fcp = conv_ffn(hp, ffn_w_up[i], ffn_conv_w[i], ffn_conv_b[i], ffn_w_down[i], zf)
        fs, fcs = conv_ffn(hs, ffn_w_up[i], ffn_conv_w[i], ffn_conv_b[i], ffn_w_down[i], state_ffn_conv[i])
        ffnc_p.append(fcp)
        ffnc_s.append(fcs)
        xp = xp + fp
        xs = xs + fs
    y_prompt = rmsnorm(xp, norm_final_w)
    y_sample = rmsnorm(xs, norm_final_w)
    return (y_prompt, y_sample,
            jnp.stack(sconv_p), jnp.stack(sconv_s),
            jnp.stack(ssm_p), jnp.stack(ssm_s),
            jnp.stack(ssmc_p), jnp.stack(ssmc_s),
            jnp.stack(k_p), jnp.stack(v_p), jnp.stack(k_s), jnp.stack(v_s),
            jnp.stack(ffnc_p), jnp.stack(ffnc_s))
```

```python
import contextlib
import numpy as np
import concourse.bass as bass
import concourse.mybir as mybir
from concourse.bass_utils import run_bass_kernel_spmd
from concourse.ap import AP

F32 = mybir.dt.float32
BF16 = mybir.dt.bfloat16
I32 = mybir.dt.int32
AF = mybir.ActivationFunctionType
ALU = mybir.AluOpType
AX = mybir.AxisListType

NCORES = 8
D = 1024
NP = 2048
NS = 16
T = NP + NS
PADL = 3
TP = T + PADL + 1
DFF = 2816
NFC = 22
NPOOL = 2560
EPS = 1e-6
TT = [(0, 512), (512, 512), (1024, 512), (1536, 512), (2048, 16)]


def conv_tiles(halo):
    n = 512 - halo
    out = []
    s = 0
    while s < NP:
        out.append((s, min(n, NP - s)))
        s += n
    return out


class Op:
    __slots__ = ("eng", "fn", "dma", "deps", "inc", "sem", "val", "idx")


class Prog:
    ENGS = ("pe", "act", "dve", "pool", "sp")
    NDMASEM = {"sp": 8, "pool": 8, "act": 4}

    def __init__(self, nc):
        self.nc = nc
        self.ops = []
        self.last_w = {}
        self.readers = {}

    BAR = tuple(("bar", e) for e in ("pe", "act", "dve", "pool", "sp"))

    def add(self, eng, fn, r=(), w=(), dma=False):
        r = list(r) + list(self.BAR)
        op = Op()
        op.eng, op.fn, op.dma = eng, fn, dma
        op.idx = len(self.ops)
        op.inc = dma
        op.sem = None
        op.val = 0
        deps = {}
        for k in r:
            lw = self.last_w.get(k)
            if lw is not None:
                deps[lw] = True
            self.readers.setdefault(k, []).append(op.idx)
        for k in w:
            lw = self.last_w.get(k)
            if lw is not None and lw not in deps:
                deps[lw] = False
            for rd in self.readers.get(k, ()):
                if rd != op.idx and rd not in deps:
                    deps[rd] = False
            self.readers[k] = []
            self.last_w[k] = op.idx
        op.deps = deps
        self.ops.append(op)
        return op

    def dma(self, eng, out, in_, r=(), w=()):
        return self.add(eng, lambda e: e.dma_start(out=out, in_=in_), r=r, w=w, dma=True)

    def emit(self, stack):
        nc = self.nc
        ops = self.ops
        for op in ops:
            for d, raw in op.deps.items():
                y = ops[d]
                if y.dma:
                    continue
                if y.eng == op.eng and not raw:
                    continue
                y.inc = True
        esem = {e: stack.enter_context(nc.semaphore("es_" + e)) for e in ("pe", "act", "dve", "pool")}
        dsem = {q: [stack.enter_context(nc.semaphore("ds_%s%d" % (q, i))) for i in range(n)]
                for q, n in self.NDMASEM.items()}
        cnt = {e: 0 for e in esem}
        dcnt = {q: 0 for q in dsem}
        for op in ops:
            if op.dma:
                m = dcnt[op.eng]
                dcnt[op.eng] += 1
                n = self.NDMASEM[op.eng]
                op.sem = dsem[op.eng][m % n]
                op.val = 16 * (m // n + 1)
            elif op.inc:
                cnt[op.eng] += 1
                op.sem = esem[op.eng]
                op.val = cnt[op.eng]
        block = stack.enter_context(nc.Block())

        def run(engname):
            def body(e):
                waited = {}
                for op in ops:
                    if op.eng != engname:
                        continue
                    need = {}
                    for d, raw in op.deps.items():
                        y = ops[d]
                        if (not y.dma) and y.eng == engname and not raw:
                            continue
                        key = id(y.sem)
                        if key not in need or need[key][1] < y.val:
                            need[key] = (y.sem, y.val)
                    if op.dma and op.val > 16:
                        key = id(op.sem)
                        v = op.val - 16
                        if key not in need or need[key][1] < v:
                            need[key] = (op.sem, v)
                    for key, (s, v) in need.items():
                        if waited.get(key, 0) >= v:
                            continue
                        e.wait_ge(s, v)
                        waited[key] = v
                    ins = op.fn(e)
                    if op.dma:
                        ins.then_inc(op.sem, 16)
                    elif op.inc:
                        ins.then_inc(op.sem, 1)
                if engname in dsem:
                    m = dcnt[engname]
                    n = self.NDMASEM[engname]
                    for i in range(min(m, n)):
                        uses = (m - 1 - i) // n + 1
                        e.wait_ge(dsem[engname][i], 16 * uses)
            return body

        block.tensor(run("pe"))
        block.scalar(run("act"))
        block.vector(run("dve"))
        block.gpsimd(run("pool"))
        block.sync(run("sp"))


class Ring:
    def __init__(self, name, aps):
        self.name = name
        self.aps = aps
        self.i = 0

    def next(self):
        k = self.i % len(self.aps)
        self.i += 1
        return self.aps[k], (self.name, k)


class Pack:
    def __init__(self):
        self.cols = 0
        self.items = {}
        self.arrs = []

    def put(self, name, arr):
        arr = np.ascontiguousarray(arr, dtype=np.float32)
        assert arr.shape[0] == 128
        a2 = arr.reshape(128, -1)
        self.items[name] = (self.cols, arr.shape[1:])
        self.cols += a2.shape[1]
        self.arrs.append(a2)

    def build(self):
        return np.ascontiguousarray(np.concatenate(self.arrs, axis=1))


def col_layout(v):
    v = np.asarray(v, dtype=np.float32)
    F = v.shape[-1]
    lead = v.shape[:-1]
    a = v.reshape(lead + (F // 128, 128))
    return np.moveaxis(a, -1, 0)


def pack_params(inp, with_values=True):
    pk = Pack()
    g = (lambda k: np.asarray(inp[k], dtype=np.float32))
    pk.put("norm_mix", col_layout(g("norm_mix_w")))
    pk.put("norm_ffn", col_layout(g("norm_ffn_w")))
    pk.put("norm_final", col_layout(g("norm_final_w")))
    pk.put("sconv_cw", np.moveaxis(col_layout(g("sconv_conv_w")), 2, 3))
    pk.put("ffn_cw", np.moveaxis(col_layout(g("ffn_conv_w")), 2, 3))
    pk.put("ffn_cb", col_layout(g("ffn_conv_b")))
    pk.put("ssd_cw", np.moveaxis(col_layout(g("ssd_conv_w")[0]), 1, 2))
    pk.put("ssd_cb", col_layout(g("ssd_conv_b")[0]))
    pk.put("ssd_D", col_layout(np.repeat(g("ssd_d")[0], 64)))
    pk.put("ssd_nw", col_layout(g("ssd_norm_w")[0]))
    pk.put("ssd_dtb", np.broadcast_to(g("ssd_dt_bias")[0][None, :], (128, 32)))
    pk.put("ssd_alog", np.broadcast_to(g("ssd_a_log")[0][None, :], (128, 32)))
    return pk


def build_program(pk_items, pk_cols, nlayers=4, npool=NPOOL):
    nc = bass.Bass("TRN2", target_bir_lowering=False)
    P = Prog(nc)
    stack = contextlib.ExitStack()

    def din(name, shape, dt=F32):
        return nc.dram_tensor(name, list(shape), dt, kind="ExternalInput").ap()

    def dout(name, shape, dt=F32):
        return nc.dram_tensor(name, list(shape), dt, kind="ExternalOutput").ap()

    def sb(name, shape, dt):
        return stack.enter_context(nc.sbuf_tensor(name, list(shape), dt))

    xT_in = din("xT_in", [128, 8, T])
    params_in = din("params", [128, pk_cols])
    sconv_past_in = din("sconv_past", [2, 128, 8, NS, 2])
    ffn_past_in = din("ffn_past", [4, 128, 44, NS, 2])
    W = {
        "sconv_w_in": din("sconv_w_in", [2, D, 3 * D]),
        "sconv_w_out": din("sconv_w_out", [2, D, D]),
        "ffn_w_up": din("ffn_w_up", [4, D, 2 * DFF]),
        "ffn_w_down": din("ffn_w_down", [4, DFF, D]),
    }
    W["ssd_w_in"] = din("ssd_w_in", [1, D, 5152])
    W["ssd_w_out"] = din("ssd_w_out", [1, 2048, D])
    ssmc_past_in = din("ssmc_past", [128, 24, NS, 3])
    ssm_state_in = din("ssm_state", [NS * 2048, 128])
    cmat_in = din("cmat", [128, 4, 128])
    ssmc_out = dout("ssmc_out", [128, 24, 3 + 3 * NS])
    ssm_p_out = dout("ssm_p_out", [4, 128, 512])
    ssm_s_out = dout("ssm_s_out", [NS * 2048, 128])
    W["attn_w_qkv"] = din("attn_w_qkv", [1, D, 3 * D])
    W["attn_w_o"] = din("attn_w_o", [1, D, D])
    rope_in = din("rope", [128, 2, T])
    cm_in = din("cm_mask", [128, 2, 256])
    acm_in = din("acm", [128, 4, 128])
    pt_in = din("page_tab", [1, NS * 16], I32)
    cache_k_in = din("cache_k", [npool * 128, D])
    cache_v_in = din("cache_v", [npool * 128, D])
    kT_out = dout("kT_out", [128, 8, T])
    vT_out = dout("vT_out", [128, 8, T])
    yT_out = dout("yT_out", [128, 8, T])
    sconv_out = dout("sconv_out", [2, 128, 8, 2 + 2 * NS])
    ffn_out = dout("ffn_out", [4, 128, 44, 2 + 2 * NS])

    xT = sb("xT", [128, 8, T], F32)
    hT = sb("hT", [128, 8, TP], BF16)
    WK = sb("WK", [128, 8, T], BF16)
    prm = sb("prm", [128, pk_cols], F32)
    NWS = 3
    wsl = [sb("wsl%d" % i, [128, 4096], BF16) for i in range(NWS)]
    ones_m = sb("ones_m", [128, 128], BF16)
    rstd = sb("rstd", [128, T], F32)
    epst = sb("epst", [128, 1], F32)
    NTMP = 6
    tmpf = [sb("tmpf%d" % i, [128, 512], F32) for i in range(NTMP)]
    stage = sb("stage", [128, 44, 2 + 2 * NS], F32)
    pastb = sb("pastb", [128, 44, NS, 2], F32)
    psall = stack.enter_context(nc.psum_tensor("psall", [128, 8, 512], F32))
    ps = [psall[:, i, :] for i in range(8)]

    cmat = sb("cmat_sb", [128, 4, 128], F32)
    ident_f, tri_f, maskneg_f, ones_f = cmat[:, 0, :], cmat[:, 1, :], cmat[:, 2, :], cmat[:, 3, :]
    ident_b = sb("ident_b", [128, 128], BF16)
    ones_b = sb("ones_b", [128, 128], BF16)
    ones_g = sb("ones_g", [128, 128], BF16)
    onet = sb("onet", [128, 1], F32)
    dt_tok = sb("dt_tok", [128, 17, 32], F32)
    dtA_tok = sb("dtA_tok", [128, 17, 32], F32)
    a_bc = sb("a_bc", [128, 32], F32)
    prevT = sb("prevT", [128, 512], F32)
    prevT_bf = sb("prevT_bf", [128, 512], BF16)
    smt = sb("smt", [128, 64], F32)
    sst = [sb("sst%d" % i, [128, 4, 128], F32) for i in range(2)]
    snew = [sb("snew%d" % i, [128, 4, 128], F32) for i in range(1)]
    ys_t = sb("ys_t", [128, 4, NS], F32)
    sx_t = sb("sx_t", [128, 3, 4, NS], F32)
    dmy = {e: sb("dmy_" + e, [128, 2], F32) for e in ("act", "dve", "pool", "sp")}
    tmp_ring = Ring("tmpf", [t for t in tmpf])
    sst_ring = Ring("sst", sst)
    snew_ring = Ring("snew", snew)
    ws_ring = Ring("wsl", [t for t in wsl])

    def prm_ap(name):
        off, shp = pk_items[name]
        n = int(np.prod(shp))
        a = prm[:, off:off + n]
        return a, off, shp

    def pcol(name, *idx):
        off, shp = pk_items[name]
        flat = 0
        for i, s in zip(idx, shp):
            flat = flat * s + i
        return prm[:, off + flat:off + flat + 1]

    P.dma("sp", prm[:], params_in, w=["prm"])
    for c in range(8):
        P.dma("sp", xT[:, c, :], xT_in[:, c, :], w=[("x", c)])
    P.add("pool", lambda e: e.memset(ones_m[:], 1.0 / 1024.0), w=["ones"])
    P.add("pool", lambda e: e.memset(epst[:], EPS), w=["epst"])
    P.add("pool", lambda e: e.memset(onet[:], 1.0), w=["onet"])
    P.add("pool", lambda e: e.memset(ones_b[:], 1.0), w=["ones_b"])
    P.add("pool", lambda e: e.memset(ones_g[:], 1.0 / 512.0), w=["ones_g"])
    P.dma("sp", cmat[:], cmat_in, w=["cmat"])
    P.add("dve", lambda e: e.tensor_copy(out=ident_b[:], in_=ident_f), r=["cmat"], w=["ident_b"])
    P.add("pool", lambda e: e.memset(hT[:, :, 0:PADL], 0.0), w=["hpad"])


    def mm(out, lhsT, rhs, start, stop, r, w):
        P.add("pe", lambda e: e.matmul(out, lhsT=lhsT, rhs=rhs, start=start, stop=stop), r=r, w=w)

    def trp(out, in_, ident, r, w):
        P.add("pe", lambda e: e.transpose(out, in_, ident), r=r, w=w)

    def actf(out, in_, func, r, w, bias=None, scale=1.0):
        if bias is None:
            P.add("act", lambda e: e.activation(out=out, in_=in_, func=func, scale=scale), r=r, w=w)
        else:
            P.add("act", lambda e: e.activation(out=out, in_=in_, func=func, bias=bias, scale=scale), r=r, w=w)

    def acp(out, in_, r, w):
        P.add("act", lambda e: e.copy(out=out, in_=in_), r=r, w=w)

    def vtt(out, in0, in1, op, r, w, eng="dve"):
        P.add(eng, lambda e: e.tensor_tensor(out=out, in0=in0, in1=in1, op=op), r=r, w=w)

    def vts(out, in0, s1, s2, op0, op1, r, w):
        if s2 is None:
            P.add("dve", lambda e: e.tensor_scalar(out=out, in0=in0, scalar1=s1, scalar2=None, op0=op0), r=r, w=w)
        else:
            P.add("dve", lambda e: e.tensor_scalar(out=out, in0=in0, scalar1=s1, scalar2=s2, op0=op0, op1=op1),
                  r=r, w=w)

    def vstt(out, in0, scalar, in1, op0, op1, r, w, accum_out=None):
        if accum_out is None:
            P.add("dve", lambda e: e.scalar_tensor_tensor(out=out, in0=in0, scalar=scalar, in1=in1, op0=op0, op1=op1),
                  r=r, w=w)
        else:
            P.add("dve", lambda e: e.scalar_tensor_tensor(out=out, in0=in0, scalar=scalar, in1=in1, op0=op0, op1=op1,
                                                          accum_out=accum_out), r=r, w=w)

    def vcp(out, in_, r, w, eng="dve"):
        P.add(eng, lambda e: e.tensor_copy(out=out, in_=in_), r=r, w=w)

    def load_w(wap, kchunks, c0, ncols, slot_ap, slot_off, tokw):
        nk = len(kchunks)
        k0 = kchunks[0]
        assert kchunks == list(range(k0, k0 + nk))
        src = wap[k0 * 128:(k0 + nk) * 128, c0:c0 + ncols].rearrange("(c p) m -> p c m", p=128)
        dst = slot_ap[:, slot_off:slot_off + nk * ncols].rearrange("p (c m) -> p c m", c=nk)
        P.dma("pool", dst, src, w=[tokw])

    def wview(slot_ap, slot_off, nk, ncols):
        return slot_ap[:, slot_off:slot_off + nk * ncols].rearrange("p (c m) -> p c m", c=nk)

    def rmsnorm(wname, widx, final=False):
        for c in range(8):
            P.add("act", lambda e, c=c: e.activation(out=WK[:, c, :], in_=xT[:, c, :], func=AF.Square),
                  r=[("x", c)], w=[("wk", c)])
        for ti, (t0, w) in enumerate(TT):
            for c in range(8):
                P.add("pe", lambda e, c=c, w=w, t0=t0, ti=ti: e.matmul(ps[ti][:, 0:w], lhsT=ones_m[:],
                                                                     rhs=WK[:, c, t0:t0 + w],
                                                                     start=(c == 0), stop=(c == 7)),
                      r=[("wk", c), "ones"], w=[("ps", ti)])
        P.add("act", lambda e: e.activation(out=rstd[:, 0:NP].rearrange("p (a b) -> p a b", a=4),
                                            in_=psall[:, 0:4, :], func=AF.Sqrt, bias=epst[:], scale=1.0),
              r=[("ps", 0), ("ps", 1), ("ps", 2), ("ps", 3), "epst"], w=["rstd"])
        P.add("act", lambda e: e.activation(out=rstd[:, NP:T], in_=ps[4][:, 0:NS], func=AF.Sqrt,
                                            bias=epst[:], scale=1.0),
              r=[("ps", 4), "epst"], w=["rstd"])
        P.add("dve", lambda e: e.reciprocal(out=rstd[:, :], in_=rstd[:, :]), r=["rstd"], w=["rstd"])
        for (t0, w) in TT:
            for c in range(8):
                if final:
                    ap, tok = tmp_ring.next()
                    dst = ap[:, 0:w]
                else:
                    dst = hT[:, c, PADL + t0:PADL + t0 + w]
                    tok = ("h", c)
                P.add("dve", lambda e, c=c, t0=t0, w=w, dst=dst: e.scalar_tensor_tensor(
                    out=dst, in0=xT[:, c, t0:t0 + w], scalar=pcol(wname, *(widx + (c,))),
                    in1=rstd[:, t0:t0 + w], op0=ALU.mult, op1=ALU.mult),
                      r=[("x", c), "rstd", "prm"], w=[tok])
                if final:
                    P.dma("sp", yT_out[:, c, t0:t0 + w], dst, r=[tok])

    def h_dst(c, t0, w):
        return hT[:, c, PADL + t0:PADL + t0 + w]

    def add_resid(o, t0, w, bank):
        P.add("dve", lambda e: e.tensor_tensor(out=xT[:, o, t0:t0 + w], in0=xT[:, o, t0:t0 + w],
                                               in1=ps[bank][:, 0:w], op=ALU.add),
              r=[("x", o), ("ps", bank)], w=[("x", o)])

    def out_proj(wap, kchunks_all, src, src_tokf, banks, tiles=None, src_t0=0):
        tiles = TT if tiles is None else tiles
        nk = len(kchunks_all)
        gcols = 256 if nk > 8 else 512
        bi = 0
        for og in range(D // gcols):
            slot, stok = ws_ring.next()
            load_w(wap, kchunks_all, og * gcols, gcols, slot, 0, stok)
            wv = wview(slot, 0, nk, gcols)
            for oo in range(gcols // 128):
                o = og * (gcols // 128) + oo
                for (t0, w) in tiles:
                    bank = banks[bi % len(banks)]
                    bi += 1
                    for ci in range(nk):
                        P.add("pe", lambda e, ci=ci, oo=oo, t0=t0, w=w, bank=bank, wv=wv: e.matmul(
                            ps[bank][:, 0:w], lhsT=wv[:, ci, oo * 128:(oo + 1) * 128],
                            rhs=src[:, ci, t0 - src_t0:t0 - src_t0 + w],
                            start=(ci == 0), stop=(ci == nk - 1)),
                              r=[stok, src_tokf(ci)], w=[("ps", bank)])
                    add_resid(o, t0, w, bank)

    CT2 = conv_tiles(2)

    def sconv_layer(li, j):
        rmsnorm("norm_mix", (li,))
        w_in = W["sconv_w_in"][j]
        P.dma("sp", pastb[:, 0:8, :, :], sconv_past_in[j], w=["pastb"])
        yv = WK
        grp = 0
        for i in range(8):
            slot, stok = ws_ring.next()
            for q in range(3):
                load_w(w_in, list(range(8)), q * D + i * 128, 128, slot, q * 1024, stok)
            wv = [wview(slot, q * 1024, 8, 128) for q in range(3)]
            cw = [pcol("sconv_cw", j, i, k) for k in range(3)]
            tiles = [(s, n, False) for (s, n) in CT2] + [(NP, NS, True)]
            for (s, n, is_s) in tiles:
                b0 = 3 * (grp % 2)
                grp += 1
                if is_s:
                    c0, wd = PADL + NP, NS
                else:
                    c0, wd = PADL + s - 2, n + 2
                for q in range(3):
                    for c in range(8):
                        P.add("pe", lambda e, q=q, c=c, c0=c0, wd=wd, b0=b0, wv=wv: e.matmul(
                            ps[b0 + q][:, 0:wd], lhsT=wv[q][:, c, :], rhs=hT[:, c, c0:c0 + wd],
                            start=(c == 0), stop=(c == 7)),
                              r=[stok, ("h", c), "hpad"], w=[("ps", b0 + q)])
                go, gi, va = ps[b0], ps[b0 + 1], ps[b0 + 2]
                vs, vtok = tmp_ring.next()
                P.add("act", lambda e, vs=vs, va=va, wd=wd: e.copy(out=vs[:, 0:wd], in_=va[:, 0:wd]),
                      r=[("ps", b0 + 2)], w=[vtok])
                u, utok = tmp_ring.next()
                P.add("dve", lambda e, u=u, gi=gi, vs=vs, wd=wd: e.tensor_tensor(
                    out=u[:, 0:wd], in0=gi[:, 0:wd], in1=vs[:, 0:wd], op=ALU.mult),
                      r=[("ps", b0 + 1), vtok], w=[utok])
                cc, ctok = tmp_ring.next()
                if not is_s:
                    P.add("dve", lambda e, cc=cc, u=u, n=n, cw=cw: e.tensor_scalar(
                        out=cc[:, 0:n], in0=u[:, 2:n + 2], scalar1=cw[2], scalar2=None, op0=ALU.mult),
                          r=[utok, "prm"], w=[ctok])
                    P.add("dve", lambda e, cc=cc, u=u, n=n, cw=cw: e.scalar_tensor_tensor(
                        out=cc[:, 0:n], in0=u[:, 1:n + 1], scalar=cw[1], in1=cc[:, 0:n], op0=ALU.mult, op1=ALU.add),
                          r=[utok, ctok, "prm"], w=[ctok])
                    P.add("dve", lambda e, cc=cc, u=u, n=n, cw=cw: e.scalar_tensor_tensor(
                        out=cc[:, 0:n], in0=u[:, 0:n], scalar=cw[0], in1=cc[:, 0:n], op0=ALU.mult, op1=ALU.add),
                          r=[utok, ctok, "prm"], w=[ctok])
                    P.add("dve", lambda e, cc=cc, go=go, n=n, i=i, s=s: e.tensor_tensor(
                        out=yv[:, i, s:s + n], in0=go[:, 2:n + 2], in1=cc[:, 0:n], op=ALU.mult),
                          r=[("ps", b0), ctok], w=[("wk", i)])
                    if s + n == NP:
                        P.add("act", lambda e, u=u, n=n, i=i: e.copy(out=stage[:, i, 0:2], in_=u[:, n:n + 2]),
                              r=[utok], w=["stage"])
                else:
                    p0 = pastb[:, i, :, 0]
                    p1 = pastb[:, i, :, 1]
                    P.add("dve", lambda e, cc=cc, u=u, cw=cw: e.tensor_scalar(
                        out=cc[:, 0:NS], in0=u[:, 0:NS], scalar1=cw[2], scalar2=None, op0=ALU.mult),
                          r=[utok, "prm"], w=[ctok])
                    P.add("dve", lambda e, cc=cc, p1=p1, cw=cw: e.scalar_tensor_tensor(
                        out=cc[:, 0:NS], in0=p1, scalar=cw[1], in1=cc[:, 0:NS], op0=ALU.mult, op1=ALU.add),
                          r=["pastb", ctok, "prm"], w=[ctok])
                    P.add("dve", lambda e, cc=cc, p0=p0, cw=cw: e.scalar_tensor_tensor(
                        out=cc[:, 0:NS], in0=p0, scalar=cw[0], in1=cc[:, 0:NS], op0=ALU.mult, op1=ALU.add),
                          r=["pastb", ctok, "prm"], w=[ctok])
                    P.add("dve", lambda e, cc=cc, go=go, i=i: e.tensor_tensor(
                        out=yv[:, i, NP:NP + NS], in0=go[:, 0:NS], in1=cc[:, 0:NS], op=ALU.mult),
                          r=[("ps", b0), ctok], w=[("wk", i)])
                    sv = stage[:, i, 2:2 + 2 * NS].rearrange("p (b k) -> p b k", k=2)
                    P.add("act", lambda e, sv=sv, u=u: e.copy(out=sv[:, :, 1], in_=u[:, 0:NS]),
                          r=[utok], w=["stage"])
                    P.add("pool", lambda e, sv=sv, p1=p1: e.tensor_copy(out=sv[:, :, 0], in_=p1),
                          r=["pastb"], w=["stage"])
        P.dma("sp", sconv_out[j], stage[:, 0:8, :], r=["stage"])
        out_proj(W["sconv_w_out"][j], list(range(8)), yv, lambda ci: ("wk", ci), [6, 7])

    def ffn_layer(li):
        rmsnorm("norm_ffn", (li,))
        w_up = W["ffn_w_up"][li]
        P.dma("sp", pastb[:, :, :, :], ffn_past_in[li], w=["pastb"])
        act = WK
        grp = 0
        for chunks in (list(range(0, 8)), list(range(8, 15)), list(range(15, 22))):
            for ii, i in enumerate(chunks):
                slot, stok = ws_ring.next()
                load_w(w_up, list(range(8)), i * 128, 128, slot, 0, stok)
                load_w(w_up, list(range(8)), DFF + i * 128, 128, slot, 1024, stok)
                wv = [wview(slot, 0, 8, 128), wview(slot, 1024, 8, 128)]
                fidx = [i, NFC + i]
                cw = [[pcol("ffn_cw", li, f, k) for k in range(3)] for f in fidx]
                cb = [pcol("ffn_cb", li, f) for f in fidx]
                tiles = [(s, n, False) for (s, n) in CT2] + [(NP, NS, True)]
                for (s, n, is_s) in tiles:
                    b0 = 2 * (grp % 2)
                    grp += 1
                    if is_s:
                        c0, wd = PADL + NP, NS
                    else:
                        c0, wd = PADL + s - 2, n + 2
                    for q in range(2):
                        for c in range(8):
                            P.add("pe", lambda e, q=q, c=c, c0=c0, wd=wd, b0=b0, wv=wv: e.matmul(
                                ps[b0 + q][:, 0:wd], lhsT=wv[q][:, c, :], rhs=hT[:, c, c0:c0 + wd],
                                start=(c == 0), stop=(c == 7)),
                                  r=[stok, ("h", c), "hpad"], w=[("ps", b0 + q)])
                    cts = []
                    for q in range(2):
                        pq = ps[b0 + q]
                        cc, ctok = tmp_ring.next()
                        cts.append((cc, ctok))
                        f = fidx[q]
                        if not is_s:
                            P.add("act", lambda e, cc=cc, pq=pq, n=n, q=q, cw=cw, cb=cb: e.activation(
                                out=cc[:, 0:n], in_=pq[:, 2:n + 2], func=AF.Identity, bias=cb[q], scale=cw[q][2]),
                                  r=[("ps", b0 + q), "prm"], w=[ctok])
                            P.add("dve", lambda e, cc=cc, pq=pq, n=n, q=q, cw=cw: e.scalar_tensor_tensor(
                                out=cc[:, 0:n], in0=pq[:, 1:n + 1], scalar=cw[q][1], in1=cc[:, 0:n],
                                op0=ALU.mult, op1=ALU.add), r=[("ps", b0 + q), ctok, "prm"], w=[ctok])
                            P.add("dve", lambda e, cc=cc, pq=pq, n=n, q=q, cw=cw: e.scalar_tensor_tensor(
                                out=cc[:, 0:n], in0=pq[:, 0:n], scalar=cw[q][0], in1=cc[:, 0:n],
                                op0=ALU.mult, op1=ALU.add), r=[("ps", b0 + q), ctok, "prm"], w=[ctok])
                            if s + n == NP:
                                P.add("act", lambda e, pq=pq, n=n, f=f: e.copy(out=stage[:, f, 0:2], in_=pq[:, n:n + 2]),
                                      r=[("ps", b0 + q)], w=["stage"])
                        else:
                            p0 = pastb[:, f, :, 0]
                            p1 = pastb[:, f, :, 1]
                            P.add("act", lambda e, cc=cc, pq=pq, q=q, cw=cw, cb=cb: e.activation(
                                out=cc[:, 0:NS], in_=pq[:, 0:NS], func=AF.Identity, bias=cb[q], scale=cw[q][2]),
                                  r=[("ps", b0 + q), "prm"], w=[ctok])
                            P.add("dve", lambda e, cc=cc, p1=p1, q=q, cw=cw: e.scalar_tensor_tensor(
                                out=cc[:, 0:NS], in0=p1, scalar=cw[q][1], in1=cc[:, 0:NS],
                                op0=ALU.mult, op1=ALU.add), r=["pastb", ctok, "prm"], w=[ctok])
                            P.add("dve", lambda e, cc=cc, p0=p0, q=q, cw=cw: e.scalar_tensor_tensor(
                                out=cc[:, 0:NS], in0=p0, scalar=cw[q][0], in1=cc[:, 0:NS],
                                op0=ALU.mult, op1=ALU.add), r=["pastb", ctok, "prm"], w=[ctok])
                            sv = stage[:, f, 2:2 + 2 * NS].rearrange("p (b k) -> p b k", k=2)
                            P.add("act", lambda e, sv=sv, pq=pq: e.copy(out=sv[:, :, 1], in_=pq[:, 0:NS]),
                                  r=[("ps", b0 + q)], w=["stage"])
                            P.add("pool", lambda e, sv=sv, p1=p1: e.tensor_copy(out=sv[:, :, 0], in_=p1),
                                  r=["pastb"], w=["stage"])
                    (ca, catok), (cg, cgtok) = cts
                    nn = NS if is_s else n
                    P.add("act", lambda e, cg=cg, nn=nn: e.activation(out=cg[:, 0:nn], in_=cg[:, 0:nn], func=AF.Silu),
                          r=[cgtok], w=[cgtok])
                    P.add("dve", lambda e, ca=ca, cg=cg, nn=nn, ii=ii, s=s: e.tensor_tensor(
                        out=act[:, ii, s:s + nn], in0=ca[:, 0:nn], in1=cg[:, 0:nn], op=ALU.mult),
                          r=[catok, cgtok], w=[("wk", ii)])
            out_proj(W["ffn_w_down"][li], chunks, act, lambda ci: ("wk", ci), [4, 5])
        P.dma("sp", ffn_out[li], stage[:, :, :], r=["stage"])


    CT3 = conv_tiles(3)

    def ssd_layer(li):
        rmsnorm("norm_mix", (li,))
        w_in = W["ssd_w_in"][0]
        prm_dtb, _, _ = prm_ap("ssd_dtb")
        prm_alog, _, _ = prm_ap("ssd_alog")
        pst = pastb[:, :, :, :].rearrange("p c b k -> p (c b k)")[:, 0:24 * NS * 3] \
            .rearrange("p (c b k) -> p c b k", c=24, b=NS)
        P.dma("sp", pst, ssmc_past_in, w=["pastb"])
        stg = stage[:, :, :].rearrange("p c k -> p (c k)")[:, 0:24 * 51].rearrange("p (c k) -> p c k", c=24)
        slot, stok = ws_ring.next()
        load_w(w_in, list(range(8)), 5120, 32, slot, 0, stok)
        wdt = wview(slot, 0, 8, 32)
        for tc in range(17):
            t0 = tc * 128
            w = 128 if tc < 16 else NS
            bank, col = (0, tc * 32) if tc < 16 else (1, 0)
            for c in range(8):
                mm(ps[bank][0:w, col:col + 32], hT[:, c, PADL + t0:PADL + t0 + w], wdt[:, c, :], c == 0, c == 7,
                   r=[stok, ("h", c)], w=[("ps", bank)])
        dtf = dt_tok[:, :, :].rearrange("p a b -> p (a b)")
        dAf = dtA_tok[:, :, :].rearrange("p a b -> p (a b)")
        vtt(dt_tok[:, 0:16, :], ps[0].rearrange("p (a b) -> p a b", b=32),
            prm_dtb.unsqueeze(1).broadcast_to([128, 16, 32]), ALU.add, r=[("ps", 0), "prm"], w=["dt_tok"])
        vtt(dt_tok[0:NS, 16, :], ps[1][0:NS, 0:32], prm_dtb[0:NS, :], ALU.add, r=[("ps", 1), "prm"], w=["dt_tok"])
        actf(dAf, dtf, AF.Abs, r=["dt_tok"], w=["dtA_tok"])
        actf(dAf, dAf, AF.Exp, r=["dtA_tok"], w=["dtA_tok"], scale=-1.0)
        actf(dAf, dAf, AF.Ln, r=["dtA_tok", "onet"], w=["dtA_tok"], bias=onet[:], scale=1.0)
        vstt(dtf, dtf, 0.0, dAf, ALU.max, ALU.add, r=["dt_tok", "dtA_tok"], w=["dt_tok"])
        actf(a_bc[:], prm_alog, AF.Exp, r=["prm"], w=["a_bc"])
        vtt(dtA_tok[:, :, :], dt_tok[:, :, :], a_bc[:].unsqueeze(1).broadcast_to([128, 17, 32]), ALU.mult,
            r=["dt_tok", "a_bc"], w=["dtA_tok"])
        vts(dAf, dAf, -1.0, None, ALU.mult, None, r=["dtA_tok"], w=["dtA_tok"])

        Dg = rstd[:, 0:1024].rearrange("p (h l) -> p h l", h=8)
        rhs2 = rstd[:, 1024:2048].rearrange("p (h l) -> p h l", h=8)
        xg_tok = [("wk", i) for i in range(4)]
        for g in range(4):
            cols = [2048 + 512 * g + 128 * i for i in range(4)] + [4096 + 128 * g, 4608 + 128 * g]
            bi = 0
            for q, col in enumerate(cols):
                cc = (col - 2048) // 128
                slot, stok = ws_ring.next()
                load_w(w_in, list(range(8)), col, 128, slot, 0, stok)
                wv = wview(slot, 0, 8, 128)
                cw = [pcol("ssd_cw", cc, k) for k in range(4)]
                cb = pcol("ssd_cb", cc)
                tiles = [(s_, n_, False) for (s_, n_) in CT3] + [(NP, NS, True)]
                for (s_, n_, is_s) in tiles:
                    bank = bi % 4
                    bi += 1
                    if is_s:
                        c0, wd = PADL + NP, NS
                    else:
                        c0, wd = PADL + s_ - 3, n_ + 3
                    for c in range(8):
                        mm(ps[bank][:, 0:wd], wv[:, c, :], hT[:, c, c0:c0 + wd], c == 0, c == 7,
                           r=[stok, ("h", c), "hpad"], w=[("ps", bank)])
                    pq = ps[bank]
                    ct, ctok = tmp_ring.next()
                    if not is_s:
                        actf(ct[:, 0:n_], pq[:, 3:n_ + 3], AF.Identity, r=[("ps", bank), "prm"], w=[ctok],
                             bias=cb, scale=cw[3])
                        for k in (2, 1, 0):
                            vstt(ct[:, 0:n_], pq[:, k:k + n_], cw[k], ct[:, 0:n_], ALU.mult, ALU.add,
                                 r=[("ps", bank), ctok, "prm"], w=[ctok])
                        actf(WK[:, q, s_:s_ + n_], ct[:, 0:n_], AF.Silu, r=[ctok], w=[("wk", q)])
                        if s_ + n_ == NP:
                            acp(stg[:, cc, 0:3], pq[:, n_:n_ + 3], r=[("ps", bank)], w=["stage"])
                    else:
                        actf(ct[:, 0:NS], pq[:, 0:NS], AF.Identity, r=[("ps", bank), "prm"], w=[ctok],
                             bias=cb, scale=cw[3])
                        for k in (2, 1, 0):
                            vstt(ct[:, 0:NS], pst[:, cc, :, k], cw[k], ct[:, 0:NS], ALU.mult, ALU.add,
                                 r=["pastb", ctok, "prm"], w=[ctok])
                        actf(WK[:, q, NP:NP + NS], ct[:, 0:NS], AF.Silu, r=[ctok], w=[("wk", q)])
                        sv = stg[:, cc, 3:3 + 3 * NS].rearrange("p (b k) -> p b k", k=3)
                        acp(sv[:, :, 2], pq[:, 0:NS], r=[("ps", bank)], w=["stage"])
                        vcp(sv[:, :, 0], pst[:, cc, :, 1], r=["pastb"], w=["stage"], eng="pool")
                        vcp(sv[:, :, 1], pst[:, cc, :, 2], r=["pastb"], w=["stage"], eng="pool")
            BT = WK[:, 4, :]
            CTm = WK[:, 5, :]
            P.add("pool", lambda e: e.memset(prevT[:], 0.0), w=["prevT"])
            P.add("pool", lambda e: e.memset(prevT_bf[:], 0.0), w=["prevT_bf"])
            for tc in range(16):
                t0 = tc * 128
                psb = ps[0].bitcast(BF16)
                for i in range(4):
                    mm(ps[0][:, i * 128:(i + 1) * 128], WK[:, i, t0:t0 + 128], ident_b[:], True, True,
                       r=[("wk", i), "ident_b"], w=[("ps", 0)])
                mm(ps[1][:, 256:384], BT[:, t0:t0 + 128], ident_b[:], True, True, r=[("wk", 4), "ident_b"],
                   w=[("ps", 1)])
                xdt, xdtok = tmp_ring.next()
                xdtv = xdt[:, :].bitcast(BF16)[:, 0:512].rearrange("p (h d) -> p h d", h=8)
                xdtdv = xdt[:, :].bitcast(BF16)[:, 512:1024].rearrange("p (h d) -> p h d", h=8)
                vtt(xdtv, ps[0].rearrange("p (h d) -> p h d", h=8),
                    dt_tok[:, tc, 8 * g:8 * g + 8].unsqueeze(2).broadcast_to([128, 8, 64]), ALU.mult,
                    r=[("ps", 0), "dt_tok"], w=[xdtok])
                bt, bttok = tmp_ring.next()
                btv = bt[:, :].bitcast(BF16)[:, 0:128]
                cbv = bt[:, :].bitcast(BF16)[:, 128:256]
                acp(btv, ps[1][:, 256:384], r=[("ps", 1)], w=[bttok])
                dA = dtA_tok[:, tc, 8 * g:8 * g + 8]
                mm(ps[1][:, 0:8], tri_f, dA, True, True, r=["cmat", "dtA_tok"], w=[("ps", 1)])
                mm(ps[1][:, 8:16], ones_f, dA, True, True, r=["cmat", "dtA_tok"], w=[("ps", 1)])
                acp(smt[:, 0:16], ps[1][:, 0:16], r=[("ps", 1)], w=["smt"])
                actf(smt[:, 16:32], smt[:, 0:16], AF.Exp, r=["smt"], w=["smt"])
                vtt(smt[:, 32:40], smt[:, 8:16], smt[:, 0:8], ALU.subtract, r=["smt"], w=["smt"])
                actf(smt[:, 40:48], smt[:, 32:40], AF.Exp, r=["smt"], w=["smt"])
                acum = smt[:, 0:8]
                ea = smt[:, 16:24]
                cd = smt[:, 24:32]
                dte = smt[:, 40:48]
                vtt(Dg, ident_f.unsqueeze(1).broadcast_to([128, 8, 128]),
                    acum.unsqueeze(2).broadcast_to([128, 8, 128]), ALU.mult, r=["cmat", "smt"], w=["Dg"])
                vtt(rhs2, maskneg_f.unsqueeze(1).broadcast_to([128, 8, 128]),
                    acum.unsqueeze(2).broadcast_to([128, 8, 128]), ALU.subtract, r=["cmat", "smt"], w=["rhs2"])
                for hb in range(2):
                    bk = 2 + hb
                    mm(ps[bk][:, :], ones_f, Dg[:, 4 * hb:4 * hb + 4, :].rearrange("p h l -> p (h l)"), True, False,
                       r=["cmat", "Dg"], w=[("ps", bk)])
                    mm(ps[bk][:, :], ident_f, rhs2[:, 4 * hb:4 * hb + 4, :].rearrange("p h l -> p (h l)"), False, True,
                       r=["cmat", "rhs2"], w=[("ps", bk)])
                lt, lttok = tmp_ring.next()
                ltv = lt[:, :].bitcast(BF16)
                for hb in range(2):
                    actf(ltv[:, 512 * hb:512 * hb + 512], ps[2 + hb][:, :], AF.Exp, r=[("ps", 2 + hb)], w=[lttok])
                mm(ps[1][:, 128:256], BT[:, t0:t0 + 128], CTm[:, t0:t0 + 128], True, True,
                   r=[("wk", 4), ("wk", 5)], w=[("ps", 1)])
                acp(cbv, ps[1][:, 128:256], r=[("ps", 1)], w=[bttok])
                mt, mttok = tmp_ring.next()
                mtv = mt[:, :].bitcast(BF16).rearrange("p (h l) -> p h l", h=8)
                vtt(mtv, ltv.rearrange("p (h l) -> p h l", h=8), cbv.unsqueeze(1).broadcast_to([128, 8, 128]),
                    ALU.mult, r=[lttok, bttok], w=[mttok])
                for h in range(8):
                    mm(ps[4][:, h * 64:(h + 1) * 64], mtv[:, h, :], xdtv[:, h, :], True, True,
                       r=[mttok, xdtok], w=[("ps", 4)])
                mm(ps[5][:, :], CTm[:, t0:t0 + 128], prevT_bf[:], True, True, r=[("wk", 5), "prevT_bf"],
                   w=[("ps", 5)])
                yd, ydtok = tmp_ring.next()
                acp(yd[:, :], ps[4][:, :], r=[("ps", 4)], w=[ydtok])
                yt_, yttok = tmp_ring.next()
                vtt(yt_.rearrange("p (h d) -> p h d", h=8), ps[5].rearrange("p (h d) -> p h d", h=8),
                    ea.unsqueeze(2).broadcast_to([128, 8, 64]), ALU.mult, r=[("ps", 5), "smt"], w=[yttok])
                vtt(yt_[:, :], yt_[:, :], yd[:, :], ALU.add, r=[yttok, ydtok], w=[yttok])
                vtt(xdtdv, xdtv, dte.unsqueeze(2).broadcast_to([128, 8, 64]), ALU.mult, r=[xdtok, "smt"], w=[xdtok])
                mm(ps[6][:, :], btv, xdtdv.rearrange("p h d -> p (h d)"), True, True, r=[bttok, xdtok], w=[("ps", 6)])
                vtt(prevT[:].rearrange("p (h d) -> p h d", h=8), prevT[:].rearrange("p (h d) -> p h d", h=8),
                    cd.unsqueeze(2).broadcast_to([128, 8, 64]), ALU.mult, r=["prevT", "smt"], w=["prevT"])
                vtt(prevT[:], prevT[:], ps[6][:, :], ALU.add, r=["prevT", ("ps", 6)], w=["prevT"])
                acp(prevT_bf[:], prevT[:], r=["prevT"], w=["prevT_bf"])
                for i in range(4):
                    mm(ps[7][:, i * 128:(i + 1) * 128], yt_[:, i * 128:(i + 1) * 128], ident_f, True, True,
                       r=[yttok, "cmat"], w=[("ps", 7)])
                for i in range(4):
                    vstt(WK[:, i, t0:t0 + 128], WK[:, i, t0:t0 + 128], pcol("ssd_D", 4 * g + i),
                         ps[7][:, i * 128:(i + 1) * 128], ALU.mult, ALU.add,
                         r=[("wk", i), ("ps", 7), "prm"], w=[("wk", i)])
            P.dma("sp", ssm_p_out[g], prevT[:], r=["prevT"])
            dexps = []
            for which, src in ((0, dt_tok), (1, dtA_tok)):
                dx, dxtok = tmp_ring.next()
                dexps.append((dx, dxtok))
                for i in range(4):
                    vcp(dx[0:NS, i * 128:(i + 1) * 128].rearrange("p (a b) -> p a b", a=2),
                        src[0:NS, 16, 8 * g + 2 * i:8 * g + 2 * i + 2].unsqueeze(2).broadcast_to([NS, 2, 64]),
                        r=["dt_tok", "dtA_tok"], w=[dxtok])
            for i in range(4):
                for which in range(2):
                    dx, dxtok = dexps[which]
                    mm(ps[0][:, which * 64 + i * 16:which * 64 + i * 16 + 16], dx[0:NS, i * 128:(i + 1) * 128],
                       ident_f[0:NS, 0:NS], True, True, r=[dxtok, "cmat"], w=[("ps", 0)])
            dtx = sx_t[:, 0, :, :]
            dec = sx_t[:, 1, :, :]
            xdx = sx_t[:, 2, :, :]
            acp(dtx, ps[0][:, 0:64].rearrange("p (i b) -> p i b", i=4), r=[("ps", 0)], w=["sx"])
            actf(dec, ps[0][:, 64:128].rearrange("p (i b) -> p i b", i=4), AF.Exp, r=[("ps", 0)], w=["sx"])
            vtt(xdx, WK[:, 0:4, NP:NP + NS], dtx, ALU.mult, r=xg_tok + ["sx"], w=["sx"])
            P.add("pool", lambda e: e.memset(ys_t[:], 0.0), w=["ys"])
            for hb in range(2):
                bd, bdtok = tmp_ring.next()
                bdv = bd[:, :].bitcast(BF16).rearrange("p (b n) -> p b n", b=8)
                cdg, cdtok = tmp_ring.next()
                cdv = cdg[:, :].bitcast(BF16).rearrange("p (b n) -> p b n", b=8)
                vtt(bdv, ident_b[:].unsqueeze(1).broadcast_to([128, 8, 128]),
                    BT[:, NP + 8 * hb:NP + 8 * hb + 8].unsqueeze(2).broadcast_to([128, 8, 128]), ALU.mult,
                    r=["ident_b", ("wk", 4)], w=[bdtok])
                vtt(cdv, ident_b[:].unsqueeze(1).broadcast_to([128, 8, 128]),
                    CTm[:, NP + 8 * hb:NP + 8 * hb + 8].unsqueeze(2).broadcast_to([128, 8, 128]), ALU.mult,
                    r=["ident_b", ("wk", 5)], w=[cdtok])
                for k2 in range(2):
                    mm(ps[2 + k2][:, :], ones_b[:], bd[:, :].bitcast(BF16)[:, 512 * k2:512 * k2 + 512], True, True,
                       r=["ones_b", bdtok], w=[("ps", 2 + k2)])
                    mm(ps[4 + k2][:, :], ones_b[:], cdg[:, :].bitcast(BF16)[:, 512 * k2:512 * k2 + 512], True, True,
                       r=["ones_b", cdtok], w=[("ps", 4 + k2)])
                for bl in range(8):
                    b = 8 * hb + bl
                    BBv = ps[2 + bl // 4][:, (bl % 4) * 128:(bl % 4) * 128 + 128]
                    CCv = ps[4 + bl // 4][:, (bl % 4) * 128:(bl % 4) * 128 + 128]
                    sin_, sintok = sst_ring.next()
                    sout, souttok = snew_ring.next()
                    r0 = b * 2048 + g * 512
                    P.dma("sp", sin_[:], ssm_state_in[r0:r0 + 512, :].rearrange("(i p) n -> p i n", p=128), w=[sintok])
                    for i in range(4):
                        tq, tqtok = tmp_ring.next()
                        vts(tq[:, 0:128], BBv, xdx[:, i, b:b + 1], None, ALU.mult, None,
                            r=[("ps", 2 + bl // 4), "sx"], w=[tqtok])
                        vstt(sout[:, i, :], sin_[:, i, :], dec[:, i, b:b + 1], tq[:, 0:128], ALU.mult, ALU.add,
                             r=[sintok, "sx", tqtok], w=[souttok])
                        vstt(tq[:, 128:256], sout[:, i, :], 1.0, CCv, ALU.mult, ALU.mult,
                             r=[souttok, ("ps", 4 + bl // 4)], w=[tqtok, "ys"], accum_out=ys_t[:, i, b:b + 1])
                    P.dma("sp", ssm_s_out[r0:r0 + 512, :].rearrange("(i p) n -> p i n", p=128), sout[:], r=[souttok])
            for i in range(4):
                vstt(WK[:, i, NP:NP + NS], WK[:, i, NP:NP + NS], pcol("ssd_D", 4 * g + i), ys_t[:, i, :],
                     ALU.mult, ALU.add, r=[("wk", i), "ys", "prm"], w=[("wk", i)])
            slot, stok = ws_ring.next()
            load_w(w_in, list(range(8)), 512 * g, 512, slot, 0, stok)
            wz = wview(slot, 0, 8, 512)
            for (t0, w) in TT:
                for i in range(4):
                    bank = i % 2
                    for c in range(8):
                        mm(ps[bank][:, 0:w], wz[:, c, i * 128:(i + 1) * 128], hT[:, c, PADL + t0:PADL + t0 + w],
                           c == 0, c == 7, r=[stok, ("h", c)], w=[("ps", bank)])
                    sz, sztok = tmp_ring.next()
                    actf(sz[:, 0:w], ps[bank][:, 0:w], AF.Silu, r=[("ps", bank)], w=[sztok])
                    vtt(WK[:, i, t0:t0 + w], WK[:, i, t0:t0 + w], sz[:, 0:w], ALU.mult, r=[("wk", i), sztok],
                        w=[("wk", i)])
                    sq, sqtok = tmp_ring.next()
                    sqv = sq[:, :].bitcast(BF16)
                    actf(sqv[:, 0:w], WK[:, i, t0:t0 + w], AF.Square, r=[("wk", i)], w=[sqtok])
                    mm(ps[2][:, 0:w], ones_g[:], sqv[:, 0:w], i == 0, i == 3, r=["ones_g", sqtok], w=[("ps", 2)])
                rs, rstok = tmp_ring.next()
                actf(rs[:, 0:w], ps[2][:, 0:w], AF.Sqrt, r=[("ps", 2), "epst"], w=[rstok], bias=epst[:], scale=1.0)
                P.add("dve", lambda e, rs=rs, w=w: e.reciprocal(out=rs[:, 0:w], in_=rs[:, 0:w]), r=[rstok], w=[rstok])
                for i in range(4):
                    vstt(WK[:, i, t0:t0 + w], WK[:, i, t0:t0 + w], pcol("ssd_nw", 4 * g + i), rs[:, 0:w],
                         ALU.mult, ALU.mult, r=[("wk", i), rstok, "prm"], w=[("wk", i)])
            if True:
                out_proj(W["ssd_w_out"][0], list(range(4 * g, 4 * g + 4)), WK, lambda ci: ("wk", ci), [6, 7])
        P.dma("sp", ssmc_out, stg, r=["stage"])


    def barrier():
        P.add("pe", lambda e: e.matmul(ps[7][0:1, 0:1], lhsT=ident_b[:, 0:1], rhs=ident_b[:, 0:1], start=True, stop=True),
              r=["ident_b"], w=[("ps", 7), ("bar", "pe")])
        P.add("act", lambda e: e.copy(out=dmy["act"][:, 0:1], in_=dmy["act"][:, 1:2]), w=[("bar", "act")])
        P.add("dve", lambda e: e.tensor_copy(out=dmy["dve"][:, 0:1], in_=dmy["dve"][:, 1:2]), w=[("bar", "dve")])
        P.add("pool", lambda e: e.tensor_copy(out=dmy["pool"][:, 0:1], in_=dmy["pool"][:, 1:2]), w=[("bar", "pool")])
        P.dma("sp", dmy["sp"][:, 0:1], dmy["sp"][:, 1:2], w=[("bar", "sp")])

    def attn_layer(li):
        rmsnorm("norm_mix", (li,))
        barrier()
        w_qkv = W["attn_w_qkv"][0]
        WKf = WK[:, :, :].rearrange("p c t -> p (c t)")

        def reg(o, n):
            return WKf[:, o:o + n]
        QT = reg(0, T)
        KT = reg(T, T)
        VT = reg(2 * T, T)
        OT = reg(3 * T, T)
        VA = reg(4 * T, 16 * 2 * 65).rearrange("p (t h e) -> p t h e", t=16, h=2)
        NMT = reg(4 * T + 2080, T)
        CM = reg(5 * T + 2080, 512).rearrange("p (k q) -> p k q", k=2)
        o0 = 5 * T + 2080 + 512
        acm = reg(o0, 1024).bitcast(F32).rearrange("p (a b) -> p a b", a=4)
        Pm_b = reg(o0 + 1024, 128)
        kmf = reg(o0 + 1152, 16).bitcast(F32)
        kmb2 = reg(o0 + 1168, 16)
        ptf = reg(o0 + 1184, 512).bitcast(F32)
        idx_i = reg(o0 + 1696, 512).bitcast(I32)
        pti = reg(o0 + 2208, 512).bitcast(I32)
        assert o0 + 2984 <= 8 * T
        ropes = rstd[:, :].bitcast(BF16).rearrange("p (a t) -> p a t", a=2)
        cosT, sinT = ropes[:, 0, :], ropes[:, 1, :]
        SS = sst[0][:, :, :].rearrange("p a b -> p (a b)")
        QS = SS[:, 0:128].rearrange("p (c b) -> p c b", c=8)
        KS = SS[:, 128:256].rearrange("p (c b) -> p c b", c=8)
        VS = SS[:, 256:384].rearrange("p (c b) -> p c b", c=8)
        OS = SS[:, 384:512].rearrange("p (c b) -> p c b", c=8)
        OSb = sst[1][:, 0, :].bitcast(BF16)[:, 0:128].rearrange("p (c b) -> p c b", c=8)

        for a_ in range(2):
            for h_ in range(2):
                P.dma("pool", ropes[:, a_, h_ * 1032:(h_ + 1) * 1032], rope_in[:, a_, h_ * 1032:(h_ + 1) * 1032],
                      w=["rope"])
        P.dma("pool", CM, cm_in, w=["cm"])
        P.dma("sp", acm, acm_in, w=["acm"])
        P.dma("sp", pti, pt_in.partition_broadcast(128), w=["pti"])
        vcp(Pm_b, acm[:, 0, :], r=["acm"], w=["pmb"])
        vcp(ptf, pti, r=["pti"], w=["ptf"])
        vts(ptf, ptf, 128.0, acm[:, 3, 0:1], ALU.mult, ALU.add, r=["ptf", "acm"], w=["ptf"])
        vcp(idx_i, ptf, r=["ptf"], w=["idx"])
        P.add("pool", lambda e: e.memset(VA[:, :, :, 64:65], 1.0), w=["va"])
        SEL = []
        for hl_, tns in enumerate((dt_tok, dtA_tok)):
            sv_ = tns[:, :, :].rearrange("p a b -> p (a b)")[:, 0:512].bitcast(BF16).rearrange("p (r m) -> p r m", r=8)
            SEL.append(sv_)
            vcp(sv_[0:16, :, :], ident_b[0:16, 8 * hl_:8 * hl_ + 8].unsqueeze(2).broadcast_to([16, 8, 128]),
                r=["ident_b"], w=["sel"])

        import os
        for g in range(8 if os.environ.get('SK_GRP') is None else 0):
            for which, (dst, dtok) in enumerate(((QT, "qt"), (KT, "kt"), (VT, "vt"))):
                if os.environ.get('SK_W%d' % which) is not None:
                    continue
                slot, stok = ws_ring.next()
                load_w(w_qkv, list(range(8)), which * D + g * 128, 128, slot, 0, stok)
                wv = wview(slot, 0, 8, 128)
                for ti, (t0, w) in enumerate(TT):
                    bank = ti % 2
                    for c in range(8):
                        mm(ps[bank][:, 0:w], wv[:, c, :], hT[:, c, PADL + t0:PADL + t0 + w], c == 0, c == 7,
                           r=[stok, ("h", c)], w=[("ps", bank)])
                    pq = ps[bank]
                    if which == 2:
                        vf, vftok = tmp_ring.next()
                        acp(vf[:, 0:w], pq[:, 0:w], r=[("ps", bank)], w=[vftok])
                        P.dma("sp", vT_out[:, g, t0:t0 + w], vf[:, 0:w], r=[vftok])
                        vcp(VT[:, t0:t0 + w], vf[:, 0:w], r=[vftok], w=["vt"], eng="pool")
                        if t0 == NP:
                            vcp(VS[:, g, :], vf[:, 0:NS], r=[vftok], w=["ss"], eng="pool")
                        continue
                    qs, qstok = tmp_ring.next()
                    qsb = qs[:, :].bitcast(BF16)
                    acp(qsb[:, 0:w], pq[:, 0:w], r=[("ps", bank)], w=[qstok])
                    ROT = os.environ.get('SK_ROT') is None
                    if ROT:
                        mm(ps[2 + bank][:, 0:w], Pm_b, qsb[:, 0:w], True, True, r=["pmb", qstok], w=[("ps", 2 + bank)])
                    t1, t1tok = tmp_ring.next()
                    acp(t1[:, 0:w], pq[:, 0:w], r=[("ps", bank)], w=[t1tok])
                    vtt(t1[:, 0:w], t1[:, 0:w], cosT[:, t0:t0 + w], ALU.mult, r=[t1tok, "rope"], w=[t1tok])
                    t2, t2tok = tmp_ring.next()
                    if ROT:
                        acp(t2[:, 0:w], ps[2 + bank][:, 0:w], r=[("ps", 2 + bank)], w=[t2tok])
                        vtt(t2[:, 0:w], t2[:, 0:w], sinT[:, t0:t0 + w], ALU.mult, r=[t2tok, "rope"], w=[t2tok])
                        vtt(t1[:, 0:w], t1[:, 0:w], t2[:, 0:w], ALU.add, r=[t1tok, t2tok], w=[t1tok])
                    acp(dst[:, t0:t0 + w], t1[:, 0:w], r=[t1tok], w=[dtok])
                    if which == 1:
                        P.dma("sp", kT_out[:, g, t0:t0 + w], t1[:, 0:w], r=[t1tok])
                        if t0 < NP and os.environ.get('SK_RED') is None:
                            P.add("dve", lambda e, ti=ti, t1=t1: e.tensor_reduce(
                                out=kmf[:, 2 * ti:2 * ti + 2], in_=t1[:, 0:512].rearrange("p (j k) -> p j k", j=2),
                                axis=AX.X, op=ALU.add), r=[t1tok], w=["kmf"])
                    if t0 == NP:
                        vcp((QS if which == 0 else KS)[:, g, :], t1[:, 0:NS], r=[t1tok], w=["ss"], eng="pool")
            vts(kmf, kmf, 1.0 / 256.0, None, ALU.mult, None, r=["kmf"], w=["kmf"])
            vtt(kmb2.rearrange("p (h j) -> p h j", h=2), kmf.unsqueeze(1).broadcast_to([128, 2, 8]),
                acm[:, 3, 1:3].unsqueeze(2).broadcast_to([128, 2, 8]), ALU.mult, r=["kmf", "acm"], w=["kmb"])
            for tc in range(16 if os.environ.get('SK_VTOK') is None else 0):
                bank = 4 + tc % 2
                mm(ps[bank][:, 0:128], VT[:, tc * 128:(tc + 1) * 128], ident_b[:], True, True, r=["vt", "ident_b"],
                   w=[("ps", bank)])
                acp(VA[:, tc, :, 0:64], ps[bank][:, 0:128].rearrange("p (h d) -> p h d", h=2), r=[("ps", bank)],
                    w=["va"])
            vts(VT, KT, acm[:, 3, 2:3], None, ALU.mult, None, r=["kt", "vt", "acm"], w=["vt"])
            vts(KT, KT, acm[:, 3, 1:2], None, ALU.mult, None, r=["kt", "acm"], w=["kt"])
            KTh = [KT, VT]
            import os
            for qc in range(2, 16 if os.environ.get('SK_GATE') is None else 2):
                ob = qc // 2
                mm(ps[6][:, 0:16], QT[:, qc * 128:(qc + 1) * 128], kmb2, True, True, r=["qt", "kmb"], w=[("ps", 6)])
                gm, gmtok = tmp_ring.next()
                gmv = gm[:, 0:16].rearrange("p (h j) -> p h j", h=2)
                m8 = gm[:, 16:32]
                sel = gm[:, 32:48].rearrange("p (h j) -> p h j", h=2)
                nmb = gm[:, 64:72].bitcast(BF16)
                vtt(gmv, ps[6][:, 0:16].rearrange("p (h j) -> p h j", h=2),
                    acm[:, 1, ob * 8:ob * 8 + 8].unsqueeze(1).broadcast_to([128, 2, 8]), ALU.add,
                    r=[("ps", 6), "acm"], w=[gmtok])
                GL = int(os.environ.get('SK_GL', '9'))
                if GL < 2:
                    continue
                for hl in range(2):
                    P.add("dve", lambda e, hl=hl, gm=gm: e.max(out=gm[:, 16 + hl * 8:24 + hl * 8],
                                                                 in_=gm[:, hl * 8:hl * 8 + 8]),
                          r=[gmtok], w=[gmtok])
                if GL < 3:
                    continue
                vtt(sel, gmv, m8.rearrange("p (h k) -> p h k", h=2)[:, :, 2:3].broadcast_to([128, 2, 8]), ALU.is_ge,
                    r=[gmtok], w=[gmtok])
                vtt(sel, sel, acm[:, 2, ob * 8:ob * 8 + 8].unsqueeze(1).broadcast_to([128, 2, 8]), ALU.mult,
                    r=[gmtok, "acm"], w=[gmtok])
                vts(nmb, gm[:, 32:48], 30000.0, -30000.0, ALU.mult, ALU.add, r=[gmtok], w=[gmtok])
                if GL < 4:
                    continue
                mm(ps[7][0:16, 0:128], nmb, ident_b[:], True, True, r=[gmtok, "ident_b"], w=[("ps", 7)])
                acp(NMT[0:16, qc * 128:(qc + 1) * 128], ps[7][0:16, 0:128], r=[("ps", 7)], w=["nmt"])
            sbi = 0
            import os
            for qb in range(8 if os.environ.get('SK_ATT') is None else 0):
                q0 = qb * 256
                nkt = 2 * (qb + 1)
                ottok = "otb"
                otv = reg(o0 + 2720, 256).rearrange("p (a f) -> p a f", a=2)
                rz = reg(o0 + 2976, 8).bitcast(F32)
                for hl in range(2):
                    rows = slice(64 * hl, 64 * hl + 64)
                    pso = ps[4 + hl]
                    for kt in range(nkt):
                        j = kt // 2
                        sbank = sbi % 4
                        sbi += 1
                        mm(ps[sbank][:, 0:256], KTh[hl][:, kt * 128:(kt + 1) * 128], QT[:, q0:q0 + 256], True, False,
                           r=["kt", "vt", "qt"], w=[("ps", sbank)])
                        if j < qb:
                            r_ = hl * 8 + j
                            mm(ps[sbank][:, 0:256], SEL[hl][0:16, j, :],
                               NMT[0:16, q0:q0 + 256], False, True, r=["sel", "nmt"], w=[("ps", sbank)])
                        else:
                            mm(ps[sbank][:, 0:256], ident_b[:], CM[:, kt % 2, :], False, True, r=["ident_b", "cm"],
                               w=[("ps", sbank)])
                        pt_, pttok = tmp_ring.next()
                        ptb = pt_[:, :].bitcast(BF16)
                        actf(ptb[:, 0:256], ps[sbank][:, 0:256], AF.Exp, r=[("ps", sbank)], w=[pttok], scale=0.125)
                        for qh in range(2):
                            mm(ps[4 + qh][:, 0:65], ptb[:, qh * 128:(qh + 1) * 128], VA[:, kt, hl, :],
                               kt == 0, kt == nkt - 1, r=[pttok, "va"], w=[("ps", 4 + qh)])
                    for qh in range(2):
                        P.add("dve", lambda e, qh=qh, hl=hl, rz=rz: e.reciprocal(
                            out=rz[:, 2 * hl + qh:2 * hl + qh + 1], in_=ps[4 + qh][:, 64:65]),
                              r=[("ps", 4 + qh)], w=[ottok])
                        vts(otv[:, qh, 64 * hl:64 * hl + 64], ps[4 + qh][:, 0:64],
                            rz[:, 2 * hl + qh:2 * hl + qh + 1], None, ALU.mult, None, r=[("ps", 4 + qh), ottok],
                            w=[ottok])
                for qh in range(2):
                    mm(ps[6][:, qh * 128:(qh + 1) * 128], otv[:, qh, :], ident_b[:], True, True,
                       r=[ottok, "ident_b"], w=[("ps", 6)])
                acp(OT[:, q0:q0 + 256], ps[6][:, 0:256], r=[("ps", 6)], w=["ot"])
            if os.environ.get('SK_OPJ') is None:
              out_proj(W["attn_w_o"][0], [g], OT.rearrange("p (c t) -> p c t", c=1), lambda ci: "ot", [6, 7],
                     tiles=TT[0:4])

        barrier()
        kt_b = [reg(0, 1024), reg(1024, 1024), reg(2048, 1024)]
        vt_b = [reg(3072, 1024), reg(4096, 1024), reg(5120, 1024)]
        qbc = reg(6144, 1024)
        vnb = reg(7168, 1024)
        prod = reg(8192, 1024)
        prodv = reg(9216, 1024)
        Qd = prodv
        Lall = reg(10240, 512).bitcast(F32)
        Pm_ = reg(10752, 512).bitcast(F32)
        gate = reg(11264, 256).bitcast(F32)
        bias = reg(11520, 256).bitcast(F32)
        m8s = reg(11776, 256).bitcast(F32)
        sm2 = reg(12032, 256).bitcast(F32)
        own, pown, Zt, rZ = sm2[:, 0:16], sm2[:, 16:32], sm2[:, 32:48], sm2[:, 48:64]
        qk, qk2 = sm2[:, 64:72], sm2[:, 72:88]
        kt_ring = Ring("ktile", kt_b)
        vt_ring = Ring("vtile", vt_b)
        import os
        def gather(cache_ap, ring, col):
            tile_, tok_ = ring.next()
            P.add("pool", lambda e: e.indirect_dma_start(
                out=tile_, out_offset=None, in_=cache_ap,
                in_offset=bass.IndirectOffsetOnAxis(ap=idx_i[:, col:col + 1], axis=0)),
                  r=["idx"], w=[tok_], dma=True)
            return tile_, tok_
        kq, vq = [], []
        if os.environ.get('SK_DEC') is None:
            for c3 in range(3):
                kq.append(gather(cache_k_in, kt_ring, c3))
        for b in range(NS if os.environ.get('SK_DEC') is None else 0):
            for (src, dstb, dtok) in ((QS, qbc, "qbc"), (VS, vnb, "vnb")):
                vtt(Qd.rearrange("p (c n) -> p c n", c=8), ident_b[:].unsqueeze(1).broadcast_to([128, 8, 128]),
                    src[:, :, b:b + 1].broadcast_to([128, 8, 128]), ALU.mult, r=["ident_b", "ss"], w=["prodv"])
                for k2 in range(2):
                    mm(ps[k2][:, :], ones_b[:], Qd[:, 512 * k2:512 * k2 + 512], True, True, r=["ones_b", "prodv"],
                       w=[("ps", k2)])
                    acp(dstb[:, 512 * k2:512 * k2 + 512], ps[k2][:, :], r=[("ps", k2)], w=[dtok])
            vtt(qk, QS[:, :, b], KS[:, :, b], ALU.mult, r=["ss"], w=["sm2"])
            vtt(qk2.rearrange("p (c h) -> p c h", c=8), qk.unsqueeze(2).broadcast_to([128, 8, 2]),
                acm[:, 3, 1:3].unsqueeze(1).broadcast_to([128, 8, 2]), ALU.mult, r=["sm2", "acm"], w=["sm2"])
            mm(ps[2][:, 0:16], ones_f, qk2, True, True, r=["cmat", "sm2"], w=[("ps", 2)])
            acp(own, ps[2][:, 0:16], r=[("ps", 2)], w=["sm2"])
            for pg in range(16):
                ktile, kttok = kq.pop(0)
                vtt(prod, ktile, qbc, ALU.mult, r=[kttok, "qbc"], w=["prod"])
                P.add("dve", lambda e, pg=pg: e.tensor_reduce(
                    out=Lall[:, pg * 16:(pg + 1) * 16], in_=prod.rearrange("p (h d) -> p h d", h=16),
                    axis=AX.X, op=ALU.add), r=["prod"], w=["lall"])
                nxt = b * 16 + pg + 3
                if nxt < NS * 16:
                    kq.append(gather(cache_k_in, kt_ring, nxt))
                if pg == 12:
                    for c3 in range(3):
                        vq.append(gather(cache_v_in, vt_ring, b * 16 + c3))
            mm(ps[3][:, 0:256], ones_f, Lall, True, True, r=["cmat", "lall"], w=[("ps", 3)])
            acp(Pm_, ps[3][:, 0:256], r=[("ps", 3)], w=["pm"])
            g4 = Pm_.rearrange("p (j two h) -> p j two h", two=2, h=16)
            vtt(gate.rearrange("p (j h) -> p j h", j=8), g4[:, :, 0, :], g4[:, :, 1, :], ALU.add,
                r=["pm"], w=["gate"])
            for h in range(16):
                P.add("dve", lambda e, h=h: e.max(out=m8s[:, h * 8:(h + 1) * 8],
                                                   in_=gate.rearrange("p (j h) -> p h j", j=8)[:, h, :]),
                      r=["gate"], w=["m8s"])
            vtt(bias.rearrange("p (j h) -> p j h", j=8), gate.rearrange("p (j h) -> p j h", j=8),
                m8s.rearrange("p (h k) -> p h k", k=8)[:, :, 2].unsqueeze(1).broadcast_to([128, 8, 16]), ALU.is_ge,
                r=["gate", "m8s"], w=["bias"])
            vts(bias, bias, 30000.0, -30000.0, ALU.mult, ALU.add, r=["bias"], w=["bias"])
            vtt(Lall.rearrange("p (j two h) -> p j two h", two=2, h=16),
                Lall.rearrange("p (j two h) -> p j two h", two=2, h=16),
                bias.rearrange("p (j h) -> p j h", j=8).unsqueeze(2).broadcast_to([128, 8, 2, 16]), ALU.add,
                r=["lall", "bias"], w=["lall"])
            actf(Pm_, Lall, AF.Exp, r=["lall"], w=["pm"], scale=0.125)
            actf(pown, own, AF.Exp, r=["sm2"], w=["sm2"], scale=0.125)
            mm(ps[3][:, 256:512], ones_f, Pm_, True, True, r=["cmat", "pm"], w=[("ps", 3)])
            P.add("dve", lambda e: e.tensor_reduce(out=Zt, in_=ps[3][:, 256:512].rearrange("p (g h) -> p h g", h=16),
                                                   axis=AX.X, op=ALU.add), r=[("ps", 3)], w=["sm2"])
            vtt(Zt, Zt, pown, ALU.add, r=["sm2"], w=["sm2"])
            P.add("dve", lambda e: e.reciprocal(out=rZ, in_=Zt), r=["sm2"], w=["sm2"])
            for pg in range(16):
                vtile, vttok = vq.pop(0)
                vtt(prodv.rearrange("p (h d) -> p h d", h=16), vtile.rearrange("p (h d) -> p h d", h=16),
                    Pm_[:, pg * 16:(pg + 1) * 16].unsqueeze(2).broadcast_to([128, 16, 64]), ALU.mult,
                    r=[vttok, "pm"], w=["prodv"])
                for k2 in range(2):
                    mm(ps[4 + k2][:, :], ones_b[:], prodv[:, 512 * k2:512 * k2 + 512], pg == 0, pg == 15,
                       r=["ones_b", "prodv"], w=[("ps", 4 + k2)])
                if pg + 3 < 16:
                    vq.append(gather(cache_v_in, vt_ring, b * 16 + pg + 3))
            for k2 in range(2):
                tq, tqtok = tmp_ring.next()
                tq3 = tq[:, :].rearrange("p (h d) -> p h d", h=8)
                vtt(tq3, vnb[:, 512 * k2:512 * k2 + 512].rearrange("p (h d) -> p h d", h=8),
                    pown[:, 8 * k2:8 * k2 + 8].unsqueeze(2).broadcast_to([128, 8, 64]), ALU.mult,
                    r=["vnb", "sm2"], w=[tqtok])
                vtt(tq[:, :], tq[:, :], ps[4 + k2][:, :], ALU.add, r=[tqtok, ("ps", 4 + k2)], w=[tqtok])
                vtt(tq3, tq3, rZ[:, 8 * k2:8 * k2 + 8].unsqueeze(2).broadcast_to([128, 8, 64]), ALU.mult,
                    r=[tqtok, "sm2"], w=[tqtok])
                tq4 = tq[:, :].rearrange("p (c n) -> p c n", c=4)
                vtt(tq4, tq4, ident_f.unsqueeze(1).broadcast_to([128, 4, 128]), ALU.mult, r=[tqtok, "cmat"], w=[tqtok])
                P.add("dve", lambda e, tq4=tq4, k2=k2, b=b: e.tensor_reduce(
                    out=OS[:, 4 * k2:4 * k2 + 4, b], in_=tq4, axis=AX.X, op=ALU.add), r=[tqtok], w=["os"])
        vcp(OSb, OS, r=["os"], w=["osb"])
        out_proj(W["attn_w_o"][0], list(range(8)), OSb, lambda ci: "osb", [6, 7], tiles=[TT[4]], src_t0=NP)
        barrier()

    for li in range(nlayers):
        kind, j = li % 3, li // 3
        if kind == 0:
            sconv_layer(li, j)
        elif kind == 1:
            ssd_layer(li)
        else:
            attn_layer(li)
        ffn_layer(li)

    rmsnorm("norm_final", (), final=True)

    P.emit(stack)
    stack.close()
    return nc


def make_cmat():
    i = np.arange(128)
    ident = (i[:, None] == i[None, :]).astype(np.float32)
    tri = (i[:, None] <= i[None, :]).astype(np.float32)
    maskneg = np.where(i[None, :] >= i[:, None], 0.0, -30000.0).astype(np.float32)
    ones = np.ones((128, 128), np.float32)
    return np.ascontiguousarray(np.stack([ident, tri, maskneg, ones], axis=1))


def make_rope():
    theta, rot = 500000.0, 16
    half = rot // 2
    inv_freq = (np.float32(theta) ** (-(np.arange(half, dtype=np.float32) * np.float32(2.0)) / np.float32(rot))).astype(np.float32)
    pos = np.concatenate([np.arange(NP), np.full(NS, NP)]).astype(np.float32)
    out = np.zeros((128, 2, T), np.float32)
    out[:, 0, :] = 1.0
    for p in range(128):
        d = p % 64
        if d < rot:
            ang = (pos * inv_freq[d % half]).astype(np.float32)
            out[p, 0] = np.cos(ang)
            out[p, 1] = np.sin(ang)
    return out


def make_cm():
    k = np.arange(128)[:, None, None]
    kt = np.arange(2)[None, :, None]
    q = np.arange(256)[None, None, :]
    return np.ascontiguousarray(np.where(kt * 128 + k <= q, 0.0, -30000.0).astype(np.float32))


def make_acm():
    a = np.zeros((128, 4, 128), np.float32)
    for m in range(128):
        d = m % 64
        if d < 8:
            a[m + 8, 0, m] = -1.0
        elif d < 16:
            a[m - 8, 0, m] = 1.0
    for ob in range(8):
        for j in range(8):
            a[:, 1, ob * 8 + j] = 0.0 if j < ob else -1e9
            a[:, 2, ob * 8 + j] = 1.0 if j < ob else 0.0
    a[:, 3, 0] = np.arange(128)
    a[:64, 3, 1] = 1.0
    a[64:, 3, 2] = 1.0
    return a


def fm_tokens(a):
    r, F = a.shape
    return np.ascontiguousarray(a.reshape(r, F // 128, 128).transpose(2, 1, 0))


_CACHE = {}


def kernel(**inp):
    nlayers = int(inp.pop("_nlayers", 4))
    small_cache = bool(inp.pop("_small_cache", False))
    f32 = lambda k: np.asarray(inp[k], dtype=np.float32)
    pk = pack_params(inp)
    params = pk.build()
    nc = build_program(pk.items, pk.cols, nlayers=nlayers, npool=(2 if small_cache else NPOOL))

    x_prompt = f32("x_prompt")
    x_sample = f32("x_sample")
    st_sconv = f32("state_sconv")
    st_ffn = f32("state_ffn_conv")
    shared = {
        "params": params,
        "sconv_w_in": f32("sconv_w_in"), "sconv_w_out": f32("sconv_w_out"),
        "ffn_w_up": f32("ffn_w_up"), "ffn_w_down": f32("ffn_w_down"),
        "ssd_w_in": f32("ssd_w_in"), "ssd_w_out": f32("ssd_w_out"),
        "cmat": make_cmat(),
        "attn_w_qkv": f32("attn_w_qkv"), "attn_w_o": f32("attn_w_o"),
        "rope": make_rope(), "cm_mask": make_cm(), "acm": make_acm(),
    }
    if small_cache:
        shared["cache_k"] = np.zeros((256, D), np.float32)
        shared["cache_v"] = np.zeros((256, D), np.float32)
    else:
        shared["cache_k"] = f32("cache_k")[0].reshape(NPOOL * 128, D)
        shared["cache_v"] = f32("cache_v")[0].reshape(NPOOL * 128, D)
    page_table = np.asarray(inp["page_table"]).astype(np.int32)
    if small_cache:
        page_table = np.zeros_like(page_table)
    st_ssm = f32("state_ssm")
    st_ssmc = f32("state_ssm_conv")
    in_maps = []
    for core in range(NCORES):
        sl = slice(core * NS, (core + 1) * NS)
        xcat = np.concatenate([x_prompt[core], x_sample[sl, 0]], axis=0)
        m = dict(shared)
        m["xT_in"] = fm_tokens(xcat)
        sp = st_sconv[:, sl].reshape(2, NS, 2, 8, 128).transpose(0, 4, 3, 1, 2)
        m["sconv_past"] = np.ascontiguousarray(sp)
        fp = st_ffn[:, sl].reshape(4, NS, 2, 44, 128).transpose(0, 4, 3, 1, 2)
        m["ffn_past"] = np.ascontiguousarray(fp)
        m["ssmc_past"] = np.ascontiguousarray(st_ssmc[0, sl].reshape(NS, 3, 24, 128).transpose(3, 2, 0, 1))
        m["ssm_state"] = np.ascontiguousarray(st_ssm[0, sl].reshape(NS * 2048, 128))
        m["page_tab"] = np.ascontiguousarray(page_table[sl].reshape(1, NS * 16))
        in_maps.append(m)
    res = run_bass_kernel_spmd(nc, in_maps, core_ids=list(range(NCORES)))
    R = res.results
    global _DBG
    _DBG = R

    y_prompt = np.zeros((8, NP, D), np.float32)
    y_sample = np.zeros((128, 1, D), np.float32)
    sconv_p = np.zeros((2, 8, 2, D), np.float32)
    sconv_s = np.zeros((2, 128, 2, D), np.float32)
    ffn_p = np.zeros((4, 8, 2, 2 * DFF), np.float32)
    ffn_s = np.zeros((4, 128, 2, 2 * DFF), np.float32)
    ssm_p = np.zeros((1, 8, 32, 64, 128), np.float32)
    ssm_s = np.zeros((1, 128, 32, 64, 128), np.float32)
    ssmc_p = np.zeros((1, 8, 3, 3072), np.float32)
    ssmc_s = np.zeros((1, 128, 3, 3072), np.float32)
    k_p = np.zeros((1, 8, NP, 16, 64), np.float32)
    v_p = np.zeros((1, 8, NP, 16, 64), np.float32)
    k_s = np.zeros((1, 128, 1, 16, 64), np.float32)
    v_s = np.zeros((1, 128, 1, 16, 64), np.float32)
    for core in range(NCORES):
        sl = slice(core * NS, (core + 1) * NS)
        r = R[core]
        yT = r["yT_out"]
        yt = yT.transpose(2, 1, 0).reshape(T, D)
        y_prompt[core] = yt[:NP]
        y_sample[sl, 0] = yt[NP:]
        so = r["sconv_out"]
        sconv_p[:, core] = so[:, :, :, 0:2].transpose(0, 3, 2, 1).reshape(2, 2, D)
        ss = so[:, :, :, 2:].reshape(2, 128, 8, NS, 2).transpose(0, 3, 4, 2, 1).reshape(2, NS, 2, D)
        sconv_s[:, sl] = ss
        fo = r["ffn_out"]
        ffn_p[:, core] = fo[:, :, :, 0:2].transpose(0, 3, 2, 1).reshape(4, 2, 2 * DFF)
        fs = fo[:, :, :, 2:].reshape(4, 128, 44, NS, 2).transpose(0, 3, 4, 2, 1).reshape(4, NS, 2, 2 * DFF)
        ffn_s[:, sl] = fs
        if nlayers >= 2:
            ssm_p[0, core] = r["ssm_p_out"].reshape(4, 128, 8, 64).transpose(0, 2, 3, 1).reshape(32, 64, 128)
            ssm_s[0, sl] = r["ssm_s_out"].reshape(NS, 32, 64, 128)
            co = r["ssmc_out"]
            ssmc_p[0, core] = co[:, :, 0:3].transpose(2, 1, 0).reshape(3, 3072)
            ssmc_s[0, sl] = co[:, :, 3:].reshape(128, 24, NS, 3).transpose(2, 3, 1, 0).reshape(NS, 3, 3072)
        if nlayers >= 3:
            for (dstp, dsts, nm) in ((k_p, k_s, "kT_out"), (v_p, v_s, "vT_out")):
                kt_ = r[nm].transpose(2, 1, 0).reshape(T, D)
                dstp[0, core] = kt_[:NP].reshape(NP, 16, 64)
                dsts[0, sl, 0] = kt_[NP:].reshape(NS, 16, 64)
    H, Pd, N = 32, 64, 128
    return (y_prompt, y_sample, sconv_p, sconv_s,
            ssm_p, ssm_s, ssmc_p, ssmc_s,
            k_p, v_p, k_s, v_s,
            ffn_p, ffn_s)
```

```python
import contextlib
import numpy as np
import concourse.bass as bass
import concourse.mybir as mybir
from concourse.bass_utils import run_bass_kernel_spmd
from concourse.ap import AP

F32 = mybir.dt.float32
BF16 = mybir.dt.bfloat16
I32 = mybir.dt.int32
AF = mybir.ActivationFunctionType
ALU = mybir.AluOpType
AX = mybir.AxisListType

NCORES = 8
D = 1024
NP = 2048
NS = 16
T = NP + NS
PADL = 3
TP = T + PADL + 1
DFF = 2816
NFC = 22
NPOOL = 2560
EPS = 1e-6
TT = [(0, 512), (512, 512), (1024, 512), (1536, 512), (2048, 16)]


def conv_tiles(halo):
    n = 512 - halo
    out = []
    s = 0
    while s < NP:
        out.append((s, min(n, NP - s)))
        s += n
    return out


class Op:
    __slots__ = ("eng", "fn", "dma", "deps", "inc", "sem", "val", "idx")


class Prog:
    ENGS = ("pe", "act", "dve", "pool", "sp")
    NDMASEM = {"sp": 8, "pool": 8, "act": 4}

    def __init__(self, nc):
        self.nc = nc
        self.ops = []
        self.last_w = {}
        self.readers = {}

    BAR = tuple(("bar", e) for e in ("pe", "act", "dve", "pool", "sp"))

    def add(self, eng, fn, r=(), w=(), dma=False):
        r = list(r) + list(self.BAR)
        op = Op()
        op.eng, op.fn, op.dma = eng, fn, dma
        op.idx = len(self.ops)
        op.inc = dma
        op.sem = None
        op.val = 0
        deps = {}
        for k in r:
            lw = self.last_w.get(k)
            if lw is not None:
                deps[lw] = True
            self.readers.setdefault(k, []).append(op.idx)
        for k in w:
            lw = self.last_w.get(k)
            if lw is not None and lw not in deps:
                deps[lw] = False
            for rd in self.readers.get(k, ()):
                if rd != op.idx and rd not in deps:
                    deps[rd] = False
            self.readers[k] = []
            self.last_w[k] = op.idx
        op.deps = deps
        self.ops.append(op)
        return op

    def dma(self, eng, out, in_, r=(), w=()):
        return self.add(eng, lambda e: e.dma_start(out=out, in_=in_), r=r, w=w, dma=True)

    def emit(self, stack):
        nc = self.nc
        ops = self.ops
        for op in ops:
            for d, raw in op.deps.items():
                y = ops[d]
                if y.dma:
                    continue
                if y.eng == op.eng and not raw:
                    continue
                y.inc = True
        esem = {e: stack.enter_context(nc.semaphore("es_" + e)) for e in ("pe", "act", "dve", "pool")}
        dsem = {q: [stack.enter_context(nc.semaphore("ds_%s%d" % (q, i))) for i in range(n)]
                for q, n in self.NDMASEM.items()}
        cnt = {e: 0 for e in esem}
        dcnt = {q: 0 for q in dsem}
        for op in ops:
            if op.dma:
                m = dcnt[op.eng]
                dcnt[op.eng] += 1
                n = self.NDMASEM[op.eng]
                op.sem = dsem[op.eng][m % n]
                op.val = 16 * (m // n + 1)
            elif op.inc:
                cnt[op.eng] += 1
                op.sem = esem[op.eng]
                op.val = cnt[op.eng]
        block = stack.enter_context(nc.Block())

        def run(engname):
            def body(e):
                waited = {}
                for op in ops:
                    if op.eng != engname:
                        continue
                    need = {}
                    for d, raw in op.deps.items():
                        y = ops[d]
                        if (not y.dma) and y.eng == engname and not raw:
                            continue
                        key = id(y.sem)
                        if key not in need or need[key][1] < y.val:
                            need[key] = (y.sem, y.val)
                    if op.dma and op.val > 16:
                        key = id(op.sem)
                        v = op.val - 16
                        if key not in need or need[key][1] < v:
                            need[key] = (op.sem, v)
                    for key, (s, v) in need.items():
                        if waited.get(key, 0) >= v:
                            continue
                        e.wait_ge(s, v)
                        waited[key] = v
                    ins = op.fn(e)
                    if op.dma:
                        ins.then_inc(op.sem, 16)
                    elif op.inc:
                        ins.then_inc(op.sem, 1)
                if engname in dsem:
                    m = dcnt[engname]
                    n = self.NDMASEM[engname]
                    for i in range(min(m, n)):
                        uses = (m - 1 - i) // n + 1
                        e.wait_ge(dsem[engname][i], 16 * uses)
            return body

        block.tensor(run("pe"))
        block.scalar(run("act"))
        block.vector(run("dve"))
        block.gpsimd(run("pool"))
        block.sync(run("sp"))


class Ring:
    def __init__(self, name, aps):
        self.name = name
        self.aps = aps
        self.i = 0

    def next(self):
        k = self.i % len(self.aps)
        self.i += 1
        return self.aps[k], (self.name, k)


class Pack:
    def __init__(self):
        self.cols = 0
        self.items = {}
        self.arrs = []

    def put(self, name, arr):
        arr = np.ascontiguousarray(arr, dtype=np.float32)
        assert arr.shape[0] == 128
        a2 = arr.reshape(128, -1)
        self.items[name] = (self.cols, arr.shape[1:])
        self.cols += a2.shape[1]
        self.arrs.append(a2)

    def build(self):
        return np.ascontiguousarray(np.concatenate(self.arrs, axis=1))


def col_layout(v):
    v = np.asarray(v, dtype=np.float32)
    F = v.shape[-1]
    lead = v.shape[:-1]
    a = v.reshape(lead + (F // 128, 128))
    return np.moveaxis(a, -1, 0)


def pack_params(inp, with_values=True):
    pk = Pack()
    g = (lambda k: np.asarray(inp[k], dtype=np.float32))
    pk.put("norm_mix", col_layout(g("norm_mix_w")))
    pk.put("norm_ffn", col_layout(g("norm_ffn_w")))
    pk.put("norm_final", col_layout(g("norm_final_w")))
    pk.put("sconv_cw", np.moveaxis(col_layout(g("sconv_conv_w")), 2, 3))
    pk.put("ffn_cw", np.moveaxis(col_layout(g("ffn_conv_w")), 2, 3))
    pk.put("ffn_cb", col_layout(g("ffn_conv_b")))
    pk.put("ssd_cw", np.moveaxis(col_layout(g("ssd_conv_w")[0]), 1, 2))
    pk.put("ssd_cb", col_layout(g("ssd_conv_b")[0]))
    pk.put("ssd_D", col_layout(np.repeat(g("ssd_d")[0], 64)))
    pk.put("ssd_nw", col_layout(g("ssd_norm_w")[0]))
    pk.put("ssd_dtb", np.broadcast_to(g("ssd_dt_bias")[0][None, :], (128, 32)))
    pk.put("ssd_alog", np.broadcast_to(g("ssd_a_log")[0][None, :], (128, 32)))
    return pk


def build_program(pk_items, pk_cols, nlayers=4, npool=NPOOL):
    nc = bass.Bass("TRN2", target_bir_lowering=False)
    P = Prog(nc)
    stack = contextlib.ExitStack()

    def din(name, shape, dt=F32):
        return nc.dram_tensor(name, list(shape), dt, kind="ExternalInput").ap()

    def dout(name, shape, dt=F32):
        return nc.dram_tensor(name, list(shape), dt, kind="ExternalOutput").ap()

    def sb(name, shape, dt):
        return stack.enter_context(nc.sbuf_tensor(name, list(shape), dt))

    xT_in = din("xT_in", [128, 8, T])
    params_in = din("params", [128, pk_cols])
    sconv_past_in = din("sconv_past", [2, 128, 8, NS, 2])
    ffn_past_in = din("ffn_past", [4, 128, 44, NS, 2])
    W = {
        "sconv_w_in": din("sconv_w_in", [2, D, 3 * D]),
        "sconv_w_out": din("sconv_w_out", [2, D, D]),
        "ffn_w_up": din("ffn_w_up", [4, D, 2 * DFF]),
        "ffn_w_down": din("ffn_w_down", [4, DFF, D]),
    }
    W["ssd_w_in"] = din("ssd_w_in", [1, D, 5152])
    W["ssd_w_out"] = din("ssd_w_out", [1, 2048, D])
    ssmc_past_in = din("ssmc_past", [128, 24, NS, 3])
    ssm_state_in = din("ssm_state", [NS * 2048, 128])
    cmat_in = din("cmat", [128, 4, 128])
    ssmc_out = dout("ssmc_out", [128, 24, 3 + 3 * NS])
    ssm_p_out = dout("ssm_p_out", [4, 128, 512])
    ssm_s_out = dout("ssm_s_out", [NS * 2048, 128])
    W["attn_w_qkv"] = din("attn_w_qkv", [1, D, 3 * D])
    W["attn_w_o"] = din("attn_w_o", [1, D, D])
    rope_in = din("rope", [128, 2, T])
    cm_in = din("cm_mask", [128, 2, 256])
    acm_in = din("acm", [128, 4, 128])
    pt_in = din("page_tab", [1, NS * 16], I32)
    cache_k_in = din("cache_k", [npool * 128, D])
    cache_v_in = din("cache_v", [npool * 128, D])
    kT_out = dout("kT_out", [128, 8, T])
    vT_out = dout("vT_out", [128, 8, T])
    yT_out = dout("yT_out", [128, 8, T])
    sconv_out = dout("sconv_out", [2, 128, 8, 2 + 2 * NS])
    ffn_out = dout("ffn_out", [4, 128, 44, 2 + 2 * NS])

    xT = sb("xT", [128, 8, T], F32)
    hT = sb("hT", [128, 8, TP], BF16)
    WK = sb("WK", [128, 8, T], BF16)
    prm = sb("prm", [128, pk_cols], F32)
    NWS = 3
    wsl = [sb("wsl%d" % i, [128, 4096], BF16) for i in range(NWS)]
    ones_m = sb("ones_m", [128, 128], BF16)
    rstd = sb("rstd", [128, T], F32)
    epst = sb("epst", [128, 1], F32)
    NTMP = 6
    tmpf = [sb("tmpf%d" % i, [128, 512], F32) for i in range(NTMP)]
    stage = sb("stage", [128, 44, 2 + 2 * NS], F32)
    pastb = sb("pastb", [128, 44, NS, 2], F32)
    psall = stack.enter_context(nc.psum_tensor("psall", [128, 8, 512], F32))
    ps = [psall[:, i, :] for i in range(8)]

    cmat = sb("cmat_sb", [128, 4, 128], F32)
    ident_f, tri_f, maskneg_f, ones_f = cmat[:, 0, :], cmat[:, 1, :], cmat[:, 2, :], cmat[:, 3, :]
    ident_b = sb("ident_b", [128, 128], BF16)
    ones_b = sb("ones_b", [128, 128], BF16)
    ones_g = sb("ones_g", [128, 128], BF16)
    onet = sb("onet", [128, 1], F32)
    dt_tok = sb("dt_tok", [128, 17, 32], F32)
    dtA_tok = sb("dtA_tok", [128, 17, 32], F32)
    a_bc = sb("a_bc", [128, 32], F32)
    prevT = sb("prevT", [128, 512], F32)
    prevT_bf = sb("prevT_bf", [128, 512], BF16)
    smt = sb("smt", [128, 64], F32)
    sst = [sb("sst%d" % i, [128, 4, 128], F32) for i in range(2)]
    snew = [sb("snew%d" % i, [128, 4, 128], F32) for i in range(1)]
    ys_t = sb("ys_t", [128, 4, NS], F32)
    sx_t = sb("sx_t", [128, 3, 4, NS], F32)
    dmy = {e: sb("dmy_" + e, [128, 2], F32) for e in ("act", "dve", "pool", "sp")}
    tmp_ring = Ring("tmpf", [t for t in tmpf])
    sst_ring = Ring("sst", sst)
    snew_ring = Ring("snew", snew)
    ws_ring = Ring("wsl", [t for t in wsl])

    def prm_ap(name):
        off, shp = pk_items[name]
        n = int(np.prod(shp))
        a = prm[:, off:off + n]
        return a, off, shp

    def pcol(name, *idx):
        off, shp = pk_items[name]
        flat = 0
        for i, s in zip(idx, shp):
            flat = flat * s + i
        return prm[:, off + flat:off + flat + 1]

    P.dma("sp", prm[:], params_in, w=["prm"])
    for c in range(8):
        P.dma("sp", xT[:, c, :], xT_in[:, c, :], w=[("x", c)])
    P.add("pool", lambda e: e.memset(ones_m[:], 1.0 / 1024.0), w=["ones"])
    P.add("pool", lambda e: e.memset(epst[:], EPS), w=["epst"])
    P.add("pool", lambda e: e.memset(onet[:], 1.0), w=["onet"])
    P.add("pool", lambda e: e.memset(ones_b[:], 1.0), w=["ones_b"])
    P.add("pool", lambda e: e.memset(ones_g[:], 1.0 / 512.0), w=["ones_g"])
    P.dma("sp", cmat[:], cmat_in, w=["cmat"])
    P.add("dve", lambda e: e.tensor_copy(out=ident_b[:], in_=ident_f), r=["cmat"], w=["ident_b"])
    P.add("pool", lambda e: e.memset(hT[:, :, 0:PADL], 0.0), w=["hpad"])


    def mm(out, lhsT, rhs, start, stop, r, w):
        P.add("pe", lambda e: e.matmul(out, lhsT=lhsT, rhs=rhs, start=start, stop=stop), r=r, w=w)

    def trp(out, in_, ident, r, w):
        P.add("pe", lambda e: e.transpose(out, in_, ident), r=r, w=w)

    def actf(out, in_, func, r, w, bias=None, scale=1.0):
        if bias is None:
            P.add("act", lambda e: e.activation(out=out, in_=in_, func=func, scale=scale), r=r, w=w)
        else:
            P.add("act", lambda e: e.activation(out=out, in_=in_, func=func, bias=bias, scale=scale), r=r, w=w)

    def acp(out, in_, r, w):
        P.add("act", lambda e: e.copy(out=out, in_=in_), r=r, w=w)

    def vtt(out, in0, in1, op, r, w, eng="dve"):
        P.add(eng, lambda e: e.tensor_tensor(out=out, in0=in0, in1=in1, op=op), r=r, w=w)

    def vts(out, in0, s1, s2, op0, op1, r, w):
        if s2 is None:
            P.add("dve", lambda e: e.tensor_scalar(out=out, in0=in0, scalar1=s1, scalar2=None, op0=op0), r=r, w=w)
        else:
            P.add("dve", lambda e: e.tensor_scalar(out=out, in0=in0, scalar1=s1, scalar2=s2, op0=op0, op1=op1),
                  r=r, w=w)

    def vstt(out, in0, scalar, in1, op0, op1, r, w, accum_out=None):
        if accum_out is None:
            P.add("dve", lambda e: e.scalar_tensor_tensor(out=out, in0=in0, scalar=scalar, in1=in1, op0=op0, op1=op1),
                  r=r, w=w)
        else:
            P.add("dve", lambda e: e.scalar_tensor_tensor(out=out, in0=in0, scalar=scalar, in1=in1, op0=op0, op1=op1,
                                                          accum_out=accum_out), r=r, w=w)

    def vcp(out, in_, r, w, eng="dve"):
        P.add(eng, lambda e: e.tensor_copy(out=out, in_=in_), r=r, w=w)

    def load_w(wap, kchunks, c0, ncols, slot_ap, slot_off, tokw):
        nk = len(kchunks)
        k0 = kchunks[0]
        assert kchunks == list(range(k0, k0 + nk))
        src = wap[k0 * 128:(k0 + nk) * 128, c0:c0 + ncols].rearrange("(c p) m -> p c m", p=128)
        dst = slot_ap[:, slot_off:slot_off + nk * ncols].rearrange("p (c m) -> p c m", c=nk)
        P.dma("pool", dst, src, w=[tokw])

    def wview(slot_ap, slot_off, nk, ncols):
        return slot_ap[:, slot_off:slot_off + nk * ncols].rearrange("p (c m) -> p c m", c=nk)

    def rmsnorm(wname, widx, final=False):
        for c in range(8):
            P.add("act", lambda e, c=c: e.activation(out=WK[:, c, :], in_=xT[:, c, :], func=AF.Square),
                  r=[("x", c)], w=[("wk", c)])
        for ti, (t0, w) in enumerate(TT):
            for c in range(8):
                P.add("pe", lambda e, c=c, w=w, t0=t0, ti=ti: e.matmul(ps[ti][:, 0:w], lhsT=ones_m[:],
                                                                     rhs=WK[:, c, t0:t0 + w],
                                                                     start=(c == 0), stop=(c == 7)),
                      r=[("wk", c), "ones"], w=[("ps", ti)])
        P.add("act", lambda e: e.activation(out=rstd[:, 0:NP].rearrange("p (a b) -> p a b", a=4),
                                            in_=psall[:, 0:4, :], func=AF.Sqrt, bias=epst[:], scale=1.0),
              r=[("ps", 0), ("ps", 1), ("ps", 2), ("ps", 3), "epst"], w=["rstd"])
        P.add("act", lambda e: e.activation(out=rstd[:, NP:T], in_=ps[4][:, 0:NS], func=AF.Sqrt,
                                            bias=epst[:], scale=1.0),
              r=[("ps", 4), "epst"], w=["rstd"])
        P.add("dve", lambda e: e.reciprocal(out=rstd[:, :], in_=rstd[:, :]), r=["rstd"], w=["rstd"])
        for (t0, w) in TT:
            for c in range(8):
                if final:
                    ap, tok = tmp_ring.next()
                    dst = ap[:, 0:w]
                else:
                    dst = hT[:, c, PADL + t0:PADL + t0 + w]
                    tok = ("h", c)
                P.add("dve", lambda e, c=c, t0=t0, w=w, dst=dst: e.scalar_tensor_tensor(
                    out=dst, in0=xT[:, c, t0:t0 + w], scalar=pcol(wname, *(widx + (c,))),
                    in1=rstd[:, t0:t0 + w], op0=ALU.mult, op1=ALU.mult),
                      r=[("x", c), "rstd", "prm"], w=[tok])
                if final:
                    P.dma("sp", yT_out[:, c, t0:t0 + w], dst, r=[tok])

    def h_dst(c, t0, w):
        return hT[:, c, PADL + t0:PADL + t0 + w]

    def add_resid(o, t0, w, bank):
        P.add("dve", lambda e: e.tensor_tensor(out=xT[:, o, t0:t0 + w], in0=xT[:, o, t0:t0 + w],
                                               in1=ps[bank][:, 0:w], op=ALU.add),
              r=[("x", o), ("ps", bank)], w=[("x", o)])

    def out_proj(wap, kchunks_all, src, src_tokf, banks, tiles=None, src_t0=0):
        tiles = TT if tiles is None else tiles
        nk = len(kchunks_all)
        gcols = 256 if nk > 8 else 512
        bi = 0
        for og in range(D // gcols):
            slot, stok = ws_ring.next()
            load_w(wap, kchunks_all, og * gcols, gcols, slot, 0, stok)
            wv = wview(slot, 0, nk, gcols)
            for oo in range(gcols // 128):
                o = og * (gcols // 128) + oo
                for (t0, w) in tiles:
                    bank = banks[bi % len(banks)]
                    bi += 1
                    for ci in range(nk):
                        P.add("pe", lambda e, ci=ci, oo=oo, t0=t0, w=w, bank=bank, wv=wv: e.matmul(
                            ps[bank][:, 0:w], lhsT=wv[:, ci, oo * 128:(oo + 1) * 128],
                            rhs=src[:, ci, t0 - src_t0:t0 - src_t0 + w],
                            start=(ci == 0), stop=(ci == nk - 1)),
                              r=[stok, src_tokf(ci)], w=[("ps", bank)])
                    add_resid(o, t0, w, bank)

    CT2 = conv_tiles(2)

    def sconv_layer(li, j):
        rmsnorm("norm_mix", (li,))
        w_in = W["sconv_w_in"][j]
        P.dma("sp", pastb[:, 0:8, :, :], sconv_past_in[j], w=["pastb"])
        yv = WK
        grp = 0
        for i in range(8):
            slot, stok = ws_ring.next()
            for q in range(3):
                load_w(w_in, list(range(8)), q * D + i * 128, 128, slot, q * 1024, stok)
            wv = [wview(slot, q * 1024, 8, 128) for q in range(3)]
            cw = [pcol("sconv_cw", j, i, k) for k in range(3)]
            tiles = [(s, n, False) for (s, n) in CT2] + [(NP, NS, True)]
            for (s, n, is_s) in tiles:
                b0 = 3 * (grp % 2)
                grp += 1
                if is_s:
                    c0, wd = PADL + NP, NS
                else:
                    c0, wd = PADL + s - 2, n + 2
                for q in range(3):
                    for c in range(8):
                        P.add("pe", lambda e, q=q, c=c, c0=c0, wd=wd, b0=b0, wv=wv: e.matmul(
                            ps[b0 + q][:, 0:wd], lhsT=wv[q][:, c, :], rhs=hT[:, c, c0:c0 + wd],
                            start=(c == 0), stop=(c == 7)),
                              r=[stok, ("h", c), "hpad"], w=[("ps", b0 + q)])
                go, gi, va = ps[b0], ps[b0 + 1], ps[b0 + 2]
                vs, vtok = tmp_ring.next()
                P.add("act", lambda e, vs=vs, va=va, wd=wd: e.copy(out=vs[:, 0:wd], in_=va[:, 0:wd]),
                      r=[("ps", b0 + 2)], w=[vtok])
                u, utok = tmp_ring.next()
                P.add("dve", lambda e, u=u, gi=gi, vs=vs, wd=wd: e.tensor_tensor(
                    out=u[:, 0:wd], in0=gi[:, 0:wd], in1=vs[:, 0:wd], op=ALU.mult),
                      r=[("ps", b0 + 1), vtok], w=[utok])
                cc, ctok = tmp_ring.next()
                if not is_s:
                    P.add("dve", lambda e, cc=cc, u=u, n=n, cw=cw: e.tensor_scalar(
                        out=cc[:, 0:n], in0=u[:, 2:n + 2], scalar1=cw[2], scalar2=None, op0=ALU.mult),
                          r=[utok, "prm"], w=[ctok])
                    P.add("dve", lambda e, cc=cc, u=u, n=n, cw=cw: e.scalar_tensor_tensor(
                        out=cc[:, 0:n], in0=u[:, 1:n + 1], scalar=cw[1], in1=cc[:, 0:n], op0=ALU.mult, op1=ALU.add),
                          r=[utok, ctok, "prm"], w=[ctok])
                    P.add("dve", lambda e, cc=cc, u=u, n=n, cw=cw: e.scalar_tensor_tensor(
                        out=cc[:, 0:n], in0=u[:, 0:n], scalar=cw[0], in1=cc[:, 0:n], op0=ALU.mult, op1=ALU.add),
                          r=[utok, ctok, "prm"], w=[ctok])
                    P.add("dve", lambda e, cc=cc, go=go, n=n, i=i, s=s: e.tensor_tensor(
                        out=yv[:, i, s:s + n], in0=go[:, 2:n + 2], in1=cc[:, 0:n], op=ALU.mult),
                          r=[("ps", b0), ctok], w=[("wk", i)])
                    if s + n == NP:
                        P.add("act", lambda e, u=u, n=n, i=i: e.copy(out=stage[:, i, 0:2], in_=u[:, n:n + 2]),
                              r=[utok], w=["stage"])
                else:
                    p0 = pastb[:, i, :, 0]
                    p1 = pastb[:, i, :, 1]
                    P.add("dve", lambda e, cc=cc, u=u, cw=cw: e.tensor_scalar(
                        out=cc[:, 0:NS], in0=u[:, 0:NS], scalar1=cw[2], scalar2=None, op0=ALU.mult),
                          r=[utok, "prm"], w=[ctok])
                    P.add("dve", lambda e, cc=cc, p1=p1, cw=cw: e.scalar_tensor_tensor(
                        out=cc[:, 0:NS], in0=p1, scalar=cw[1], in1=cc[:, 0:NS], op0=ALU.mult, op1=ALU.add),
                          r=["pastb", ctok, "prm"], w=[ctok])
                    P.add("dve", lambda e, cc=cc, p0=p0, cw=cw: e.scalar_tensor_tensor(
                        out=cc[:, 0:NS], in0=p0, scalar=cw[0], in1=cc[:, 0:NS], op0=ALU.mult, op1=ALU.add),
                          r=["pastb", ctok, "prm"], w=[ctok])
                    P.add("dve", lambda e, cc=cc, go=go, i=i: e.tensor_tensor(
                        out=yv[:, i, NP:NP + NS], in0=go[:, 0:NS], in1=cc[:, 0:NS], op=ALU.mult),
                          r=[("ps", b0), ctok], w=[("wk", i)])
                    sv = stage[:, i, 2:2 + 2 * NS].rearrange("p (b k) -> p b k", k=2)
                    P.add("act", lambda e, sv=sv, u=u: e.copy(out=sv[:, :, 1], in_=u[:, 0:NS]),
                          r=[utok], w=["stage"])
                    P.add("pool", lambda e, sv=sv, p1=p1: e.tensor_copy(out=sv[:, :, 0], in_=p1),
                          r=["pastb"], w=["stage"])
        P.dma("sp", sconv_out[j], stage[:, 0:8, :], r=["stage"])
        out_proj(W["sconv_w_out"][j], list(range(8)), yv, lambda ci: ("wk", ci), [6, 7])

    def ffn_layer(li):
        rmsnorm("norm_ffn", (li,))
        w_up = W["ffn_w_up"][li]
        P.dma("sp", pastb[:, :, :, :], ffn_past_in[li], w=["pastb"])
        act = WK
        grp = 0
        for chunks in (list(range(0, 8)), list(range(8, 15)), list(range(15, 22))):
            for ii, i in enumerate(chunks):
                slot, stok = ws_ring.next()
                load_w(w_up, list(range(8)), i * 128, 128, slot, 0, stok)
                load_w(w_up, list(range(8)), DFF + i * 128, 128, slot, 1024, stok)
                wv = [wview(slot, 0, 8, 128), wview(slot, 1024, 8, 128)]
                fidx = [i, NFC + i]
                cw = [[pcol("ffn_cw", li, f, k) for k in range(3)] for f in fidx]
                cb = [pcol("ffn_cb", li, f) for f in fidx]
                tiles = [(s, n, False) for (s, n) in CT2] + [(NP, NS, True)]
                for (s, n, is_s) in tiles:
                    b0 = 2 * (grp % 2)
                    grp += 1
                    if is_s:
                        c0, wd = PADL + NP, NS
                    else:
                        c0, wd = PADL + s - 2, n + 2
                    for q in range(2):
                        for c in range(8):
                            P.add("pe", lambda e, q=q, c=c, c0=c0, wd=wd, b0=b0, wv=wv: e.matmul(
                                ps[b0 + q][:, 0:wd], lhsT=wv[q][:, c, :], rhs=hT[:, c, c0:c0 + wd],
                                start=(c == 0), stop=(c == 7)),
                                  r=[stok, ("h", c), "hpad"], w=[("ps", b0 + q)])
                    cts = []
                    for q in range(2):
                        pq = ps[b0 + q]
                        cc, ctok = tmp_ring.next()
                        cts.append((cc, ctok))
                        f = fidx[q]
                        if not is_s:
                            P.add("act", lambda e, cc=cc, pq=pq, n=n, q=q, cw=cw, cb=cb: e.activation(
                                out=cc[:, 0:n], in_=pq[:, 2:n + 2], func=AF.Identity, bias=cb[q], scale=cw[q][2]),
                                  r=[("ps", b0 + q), "prm"], w=[ctok])
                            P.add("dve", lambda e, cc=cc, pq=pq, n=n, q=q, cw=cw: e.scalar_tensor_tensor(
                                out=cc[:, 0:n], in0=pq[:, 1:n + 1], scalar=cw[q][1], in1=cc[:, 0:n],
                                op0=ALU.mult, op1=ALU.add), r=[("ps", b0 + q), ctok, "prm"], w=[ctok])
                            P.add("dve", lambda e, cc=cc, pq=pq, n=n, q=q, cw=cw: e.scalar_tensor_tensor(
                                out=cc[:, 0:n], in0=pq[:, 0:n], scalar=cw[q][0], in1=cc[:, 0:n],
                                op0=ALU.mult, op1=ALU.add), r=[("ps", b0 + q), ctok, "prm"], w=[ctok])
                            if s + n == NP:
                                P.add("act", lambda e, pq=pq, n=n, f=f: e.copy(out=stage[:, f, 0:2], in_=pq[:, n:n + 2]),
                                      r=[("ps", b0 + q)], w=["stage"])
                        else:
                            p0 = pastb[:, f, :, 0]
                            p1 = pastb[:, f, :, 1]
                            P.add("act", lambda e, cc=cc, pq=pq, q=q, cw=cw, cb=cb: e.activation(
                                out=cc[:, 0:NS], in_=pq[:, 0:NS], func=AF.Identity, bias=cb[q], scale=cw[q][2]),
                                  r=[("ps", b0 + q), "prm"], w=[ctok])
                            P.add("dve", lambda e, cc=cc, p1=p1, q=q, cw=cw: e.scalar_tensor_tensor(
                                out=cc[:, 0:NS], in0=p1, scalar=cw[q][1], in1=cc[:, 0:NS],
                                op0=ALU.mult, op1=ALU.add), r=["pastb", ctok, "prm"], w=[ctok])
                            P.add("dve", lambda e, cc=cc, p0=p0, q=q, cw=cw: e.scalar_tensor_tensor(
                                out=cc[:, 0:NS], in0=p0, scalar=cw[q][0], in1=cc[:, 0:NS],
                                op0=ALU.mult, op1=ALU.add), r=["pastb", ctok, "prm"], w=[ctok])
                            sv = stage[:, f, 2:2 + 2 * NS].rearrange("p (b k) -> p b k", k=2)
                            P.add("act", lambda e, sv=sv, pq=pq: e.copy(out=sv[:, :, 1], in_=pq[:, 0:NS]),
                                  r=[("ps", b0 + q)], w=["stage"])
                            P.add("pool", lambda e, sv=sv, p1=p1: e.tensor_copy(out=sv[:, :, 0], in_=p1),
                                  r=["pastb"], w=["stage"])
                    (ca, catok), (cg, cgtok) = cts
                    nn = NS if is_s else n
                    P.add("act", lambda e, cg=cg, nn=nn: e.activation(out=cg[:, 0:nn], in_=cg[:, 0:nn], func=AF.Silu),
                          r=[cgtok], w=[cgtok])
                    P.add("dve", lambda e, ca=ca, cg=cg, nn=nn, ii=ii, s=s: e.tensor_tensor(
                        out=act[:, ii, s:s + nn], in0=ca[:, 0:nn], in1=cg[:, 0:nn], op=ALU.mult),
                          r=[catok, cgtok], w=[("wk", ii)])
            out_proj(W["ffn_w_down"][li], chunks, act, lambda ci: ("wk", ci), [4, 5])
        P.dma("sp", ffn_out[li], stage[:, :, :], r=["stage"])


    CT3 = conv_tiles(3)

    def ssd_layer(li):
        rmsnorm("norm_mix", (li,))
        w_in = W["ssd_w_in"][0]
        prm_dtb, _, _ = prm_ap("ssd_dtb")
        prm_alog, _, _ = prm_ap("ssd_alog")
        pst = pastb[:, :, :, :].rearrange("p c b k -> p (c b k)")[:, 0:24 * NS * 3] \
            .rearrange("p (c b k) -> p c b k", c=24, b=NS)
        P.dma("sp", pst, ssmc_past_in, w=["pastb"])
        stg = stage[:, :, :].rearrange("p c k -> p (c k)")[:, 0:24 * 51].rearrange("p (c k) -> p c k", c=24)
        slot, stok = ws_ring.next()
        load_w(w_in, list(range(8)), 5120, 32, slot, 0, stok)
        wdt = wview(slot, 0, 8, 32)
        for tc in range(17):
            t0 = tc * 128
            w = 128 if tc < 16 else NS
            bank, col = (0, tc * 32) if tc < 16 else (1, 0)
            for c in range(8):
                mm(ps[bank][0:w, col:col + 32], hT[:, c, PADL + t0:PADL + t0 + w], wdt[:, c, :], c == 0, c == 7,
                   r=[stok, ("h", c)], w=[("ps", bank)])
        dtf = dt_tok[:, :, :].rearrange("p a b -> p (a b)")
        dAf = dtA_tok[:, :, :].rearrange("p a b -> p (a b)")
        vtt(dt_tok[:, 0:16, :], ps[0].rearrange("p (a b) -> p a b", b=32),
            prm_dtb.unsqueeze(1).broadcast_to([128, 16, 32]), ALU.add, r=[("ps", 0), "prm"], w=["dt_tok"])
        vtt(dt_tok[0:NS, 16, :], ps[1][0:NS, 0:32], prm_dtb[0:NS, :], ALU.add, r=[("ps", 1), "prm"], w=["dt_tok"])
        actf(dAf, dtf, AF.Abs, r=["dt_tok"], w=["dtA_tok"])
        actf(dAf, dAf, AF.Exp, r=["dtA_tok"], w=["dtA_tok"], scale=-1.0)
        actf(dAf, dAf, AF.Ln, r=["dtA_tok", "onet"], w=["dtA_tok"], bias=onet[:], scale=1.0)
        vstt(dtf, dtf, 0.0, dAf, ALU.max, ALU.add, r=["dt_tok", "dtA_tok"], w=["dt_tok"])
        actf(a_bc[:], prm_alog, AF.Exp, r=["prm"], w=["a_bc"])
        vtt(dtA_tok[:, :, :], dt_tok[:, :, :], a_bc[:].unsqueeze(1).broadcast_to([128, 17, 32]), ALU.mult,
            r=["dt_tok", "a_bc"], w=["dtA_tok"])
        vts(dAf, dAf, -1.0, None, ALU.mult, None, r=["dtA_tok"], w=["dtA_tok"])

        Dg = rstd[:, 0:1024].rearrange("p (h l) -> p h l", h=8)
        rhs2 = rstd[:, 1024:2048].rearrange("p (h l) -> p h l", h=8)
        xg_tok = [("wk", i) for i in range(4)]
        for g in range(4):
            cols = [2048 + 512 * g + 128 * i for i in range(4)] + [4096 + 128 * g, 4608 + 128 * g]
            bi = 0
            for q, col in enumerate(cols):
                cc = (col - 2048) // 128
                slot, stok = ws_ring.next()
                load_w(w_in, list(range(8)), col, 128, slot, 0, stok)
                wv = wview(slot, 0, 8, 128)
                cw = [pcol("ssd_cw", cc, k) for k in range(4)]
                cb = pcol("ssd_cb", cc)
                tiles = [(s_, n_, False) for (s_, n_) in CT3] + [(NP, NS, True)]
                for (s_, n_, is_s) in tiles:
                    bank = bi % 4
                    bi += 1
                    if is_s:
                        c0, wd = PADL + NP, NS
                    else:
                        c0, wd = PADL + s_ - 3, n_ + 3
                    for c in range(8):
                        mm(ps[bank][:, 0:wd], wv[:, c, :], hT[:, c, c0:c0 + wd], c == 0, c == 7,
                           r=[stok, ("h", c), "hpad"], w=[("ps", bank)])
                    pq = ps[bank]
                    ct, ctok = tmp_ring.next()
                    if not is_s:
                        actf(ct[:, 0:n_], pq[:, 3:n_ + 3], AF.Identity, r=[("ps", bank), "prm"], w=[ctok],
                             bias=cb, scale=cw[3])
                        for k in (2, 1, 0):
                            vstt(ct[:, 0:n_], pq[:, k:k + n_], cw[k], ct[:, 0:n_], ALU.mult, ALU.add,
                                 r=[("ps", bank), ctok, "prm"], w=[ctok])
                        actf(WK[:, q, s_:s_ + n_], ct[:, 0:n_], AF.Silu, r=[ctok], w=[("wk", q)])
                        if s_ + n_ == NP:
                            acp(stg[:, cc, 0:3], pq[:, n_:n_ + 3], r=[("ps", bank)], w=["stage"])
                    else:
                        actf(ct[:, 0:NS], pq[:, 0:NS], AF.Identity, r=[("ps", bank), "prm"], w=[ctok],
                             bias=cb, scale=cw[3])
                        for k in (2, 1, 0):
                            vstt(ct[:, 0:NS], pst[:, cc, :, k], cw[k], ct[:, 0:NS], ALU.mult, ALU.add,
                                 r=["pastb", ctok, "prm"], w=[ctok])
                        actf(WK[:, q, NP:NP + NS], ct[:, 0:NS], AF.Silu, r=[ctok], w=[("wk", q)])
                        sv = stg[:, cc, 3:3 + 3 * NS].rearrange("p (b k) -> p b k", k=3)
                        acp(sv[:, :, 2], pq[:, 0:NS], r=[("ps", bank)], w=["stage"])
                        vcp(sv[:, :, 0], pst[:, cc, :, 1], r=["pastb"], w=["stage"], eng="pool")
                        vcp(sv[:, :, 1], pst[:, cc, :, 2], r=["pastb"], w=["stage"], eng="pool")
            BT = WK[:, 4, :]
            CTm = WK[:, 5, :]
            P.add("pool", lambda e: e.memset(prevT[:], 0.0), w=["prevT"])
            P.add("pool", lambda e: e.memset(prevT_bf[:], 0.0), w=["prevT_bf"])
            for tc in range(16):
                t0 = tc * 128
                psb = ps[0].bitcast(BF16)
                for i in range(4):
                    mm(ps[0][:, i * 128:(i + 1) * 128], WK[:, i, t0:t0 + 128], ident_b[:], True, True,
                       r=[("wk", i), "ident_b"], w=[("ps", 0)])
                mm(ps[1][:, 256:384], BT[:, t0:t0 + 128], ident_b[:], True, True, r=[("wk", 4), "ident_b"],
                   w=[("ps", 1)])
                xdt, xdtok = tmp_ring.next()
                xdtv = xdt[:, :].bitcast(BF16)[:, 0:512].rearrange("p (h d) -> p h d", h=8)
                xdtdv = xdt[:, :].bitcast(BF16)[:, 512:1024].rearrange("p (h d) -> p h d", h=8)
                vtt(xdtv, ps[0].rearrange("p (h d) -> p h d", h=8),
                    dt_tok[:, tc, 8 * g:8 * g + 8].unsqueeze(2).broadcast_to([128, 8, 64]), ALU.mult,
                    r=[("ps", 0), "dt_tok"], w=[xdtok])
                bt, bttok = tmp_ring.next()
                btv = bt[:, :].bitcast(BF16)[:, 0:128]
                cbv = bt[:, :].bitcast(BF16)[:, 128:256]
                acp(btv, ps[1][:, 256:384], r=[("ps", 1)], w=[bttok])
                dA = dtA_tok[:, tc, 8 * g:8 * g + 8]
                mm(ps[1][:, 0:8], tri_f, dA, True, True, r=["cmat", "dtA_tok"], w=[("ps", 1)])
                mm(ps[1][:, 8:16], ones_f, dA, True, True, r=["cmat", "dtA_tok"], w=[("ps", 1)])
                acp(smt[:, 0:16], ps[1][:, 0:16], r=[("ps", 1)], w=["smt"])
                actf(smt[:, 16:32], smt[:, 0:16], AF.Exp, r=["smt"], w=["smt"])
                vtt(smt[:, 32:40], smt[:, 8:16], smt[:, 0:8], ALU.subtract, r=["smt"], w=["smt"])
                actf(smt[:, 40:48], smt[:, 32:40], AF.Exp, r=["smt"], w=["smt"])
                acum = smt[:, 0:8]
                ea = smt[:, 16:24]
                cd = smt[:, 24:32]
                dte = smt[:, 40:48]
                vtt(Dg, ident_f.unsqueeze(1).broadcast_to([128, 8, 128]),
                    acum.unsqueeze(2).broadcast_to([128, 8, 128]), ALU.mult, r=["cmat", "smt"], w=["Dg"])
                vtt(rhs2, maskneg_f.unsqueeze(1).broadcast_to([128, 8, 128]),
                    acum.unsqueeze(2).broadcast_to([128, 8, 128]), ALU.subtract, r=["cmat", "smt"], w=["rhs2"])
                for hb in range(2):
                    bk = 2 + hb
                    mm(ps[bk][:, :], ones_f, Dg[:, 4 * hb:4 * hb + 4, :].rearrange("p h l -> p (h l)"), True, False,
                       r=["cmat", "Dg"], w=[("ps", bk)])
                    mm(ps[bk][:, :], ident_f, rhs2[:, 4 * hb:4 * hb + 4, :].rearrange("p h l -> p (h l)"), False, True,
                       r=["cmat", "rhs2"], w=[("ps", bk)])
                lt, lttok = tmp_ring.next()
                ltv = lt[:, :].bitcast(BF16)
                for hb in range(2):
                    actf(ltv[:, 512 * hb:512 * hb + 512], ps[2 + hb][:, :], AF.Exp, r=[("ps", 2 + hb)], w=[lttok])
                mm(ps[1][:, 128:256], BT[:, t0:t0 + 128], CTm[:, t0:t0 + 128], True, True,
                   r=[("wk", 4), ("wk", 5)], w=[("ps", 1)])
                acp(cbv, ps[1][:, 128:256], r=[("ps", 1)], w=[bttok])
                mt, mttok = tmp_ring.next()
                mtv = mt[:, :].bitcast(BF16).rearrange("p (h l) -> p h l", h=8)
                vtt(mtv, ltv.rearrange("p (h l) -> p h l", h=8), cbv.unsqueeze(1).broadcast_to([128, 8, 128]),
                    ALU.mult, r=[lttok, bttok], w=[mttok])
                for h in range(8):
                    mm(ps[4][:, h * 64:(h + 1) * 64], mtv[:, h, :], xdtv[:, h, :], True, True,
                       r=[mttok, xdtok], w=[("ps", 4)])
                mm(ps[5][:, :], CTm[:, t0:t0 + 128], prevT_bf[:], True, True, r=[("wk", 5), "prevT_bf"],
                   w=[("ps", 5)])
                yd, ydtok = tmp_ring.next()
                acp(yd[:, :], ps[4][:, :], r=[("ps", 4)], w=[ydtok])
                yt_, yttok = tmp_ring.next()
                vtt(yt_.rearrange("p (h d) -> p h d", h=8), ps[5].rearrange("p (h d) -> p h d", h=8),
                    ea.unsqueeze(2).broadcast_to([128, 8, 64]), ALU.mult, r=[("ps", 5), "smt"], w=[yttok])
                vtt(yt_[:, :], yt_[:, :], yd[:, :], ALU.add, r=[yttok, ydtok], w=[yttok])
                vtt(xdtdv, xdtv, dte.unsqueeze(2).broadcast_to([128, 8, 64]), ALU.mult, r=[xdtok, "smt"], w=[xdtok])
                mm(ps[6][:, :], btv, xdtdv.rearrange("p h d -> p (h d)"), True, True, r=[bttok, xdtok], w=[("ps", 6)])
                vtt(prevT[:].rearrange("p (h d) -> p h d", h=8), prevT[:].rearrange("p (h d) -> p h d", h=8),
                    cd.unsqueeze(2).broadcast_to([128, 8, 64]), ALU.mult, r=["prevT", "smt"], w=["prevT"])
                vtt(prevT[:], prevT[:], ps[6][:, :], ALU.add, r=["prevT", ("ps", 6)], w=["prevT"])
                acp(prevT_bf[:], prevT[:], r=["prevT"], w=["prevT_bf"])
                for i in range(4):
                    mm(ps[7][:, i * 128:(i + 1) * 128], yt_[:, i * 128:(i + 1) * 128], ident_f, True, True,
                       r=[yttok, "cmat"], w=[("ps", 7)])
                for i in range(4):
                    vstt(WK[:, i, t0:t0 + 128], WK[:, i, t0:t0 + 128], pcol("ssd_D", 4 * g + i),
                         ps[7][:, i * 128:(i + 1) * 128], ALU.mult, ALU.add,
                         r=[("wk", i), ("ps", 7), "prm"], w=[("wk", i)])
            P.dma("sp", ssm_p_out[g], prevT[:], r=["prevT"])
            dexps = []
            for which, src in ((0, dt_tok), (1, dtA_tok)):
                dx, dxtok = tmp_ring.next()
                dexps.append((dx, dxtok))
                for i in range(4):
                    vcp(dx[0:NS, i * 128:(i + 1) * 128].rearrange("p (a b) -> p a b", a=2),
                        src[0:NS, 16, 8 * g + 2 * i:8 * g + 2 * i + 2].unsqueeze(2).broadcast_to([NS, 2, 64]),
                        r=["dt_tok", "dtA_tok"], w=[dxtok])
            for i in range(4):
                for which in range(2):
                    dx, dxtok = dexps[which]
                    mm(ps[0][:, which * 64 + i * 16:which * 64 + i * 16 + 16], dx[0:NS, i * 128:(i + 1) * 128],
                       ident_f[0:NS, 0:NS], True, True, r=[dxtok, "cmat"], w=[("ps", 0)])
            dtx = sx_t[:, 0, :, :]
            dec = sx_t[:, 1, :, :]
            xdx = sx_t[:, 2, :, :]
            acp(dtx, ps[0][:, 0:64].rearrange("p (i b) -> p i b", i=4), r=[("ps", 0)], w=["sx"])
            actf(dec, ps[0][:, 64:128].rearrange("p (i b) -> p i b", i=4), AF.Exp, r=[("ps", 0)], w=["sx"])
            vtt(xdx, WK[:, 0:4, NP:NP + NS], dtx, ALU.mult, r=xg_tok + ["sx"], w=["sx"])
            P.add("pool", lambda e: e.memset(ys_t[:], 0.0), w=["ys"])
            for hb in range(2):
                bd, bdtok = tmp_ring.next()
                bdv = bd[:, :].bitcast(BF16).rearrange("p (b n) -> p b n", b=8)
                cdg, cdtok = tmp_ring.next()
                cdv = cdg[:, :].bitcast(BF16).rearrange("p (b n) -> p b n", b=8)
                vtt(bdv, ident_b[:].unsqueeze(1).broadcast_to([128, 8, 128]),
                    BT[:, NP + 8 * hb:NP + 8 * hb + 8].unsqueeze(2).broadcast_to([128, 8, 128]), ALU.mult,
                    r=["ident_b", ("wk", 4)], w=[bdtok])
                vtt(cdv, ident_b[:].unsqueeze(1).broadcast_to([128, 8, 128]),
                    CTm[:, NP + 8 * hb:NP + 8 * hb + 8].unsqueeze(2).broadcast_to([128, 8, 128]), ALU.mult,
                    r=["ident_b", ("wk", 5)], w=[cdtok])
                for k2 in range(2):
                    mm(ps[2 + k2][:, :], ones_b[:], bd[:, :].bitcast(BF16)[:, 512 * k2:512 * k2 + 512], True, True,
                       r=["ones_b", bdtok], w=[("ps", 2 + k2)])
                    mm(ps[4 + k2][:, :], ones_b[:], cdg[:, :].bitcast(BF16)[:, 512 * k2:512 * k2 + 512], True, True,
                       r=["ones_b", cdtok], w=[("ps", 4 + k2)])
                for bl in range(8):
                    b = 8 * hb + bl
                    BBv = ps[2 + bl // 4][:, (bl % 4) * 128:(bl % 4) * 128 + 128]
                    CCv = ps[4 + bl // 4][:, (bl % 4) * 128:(bl % 4) * 128 + 128]
                    sin_, sintok = sst_ring.next()
                    sout, souttok = snew_ring.next()
                    r0 = b * 2048 + g * 512
                    P.dma("sp", sin_[:], ssm_state_in[r0:r0 + 512, :].rearrange("(i p) n -> p i n", p=128), w=[sintok])
                    for i in range(4):
                        tq, tqtok = tmp_ring.next()
                        vts(tq[:, 0:128], BBv, xdx[:, i, b:b + 1], None, ALU.mult, None,
                            r=[("ps", 2 + bl // 4), "sx"], w=[tqtok])
                        vstt(sout[:, i, :], sin_[:, i, :], dec[:, i, b:b + 1], tq[:, 0:128], ALU.mult, ALU.add,
                             r=[sintok, "sx", tqtok], w=[souttok])
                        vstt(tq[:, 128:256], sout[:, i, :], 1.0, CCv, ALU.mult, ALU.mult,
                             r=[souttok, ("ps", 4 + bl // 4)], w=[tqtok, "ys"], accum_out=ys_t[:, i, b:b + 1])
                    P.dma("sp", ssm_s_out[r0:r0 + 512, :].rearrange("(i p) n -> p i n", p=128), sout[:], r=[souttok])
            for i in range(4):
                vstt(WK[:, i, NP:NP + NS], WK[:, i, NP:NP + NS], pcol("ssd_D", 4 * g + i), ys_t[:, i, :],
                     ALU.mult, ALU.add, r=[("wk", i), "ys", "prm"], w=[("wk", i)])
            slot, stok = ws_ring.next()
            load_w(w_in, list(range(8)), 512 * g, 512, slot, 0, stok)
            wz = wview(slot, 0, 8, 512)
            for (t0, w) in TT:
                for i in range(4):
                    bank = i % 2
                    for c in range(8):
                        mm(ps[bank][:, 0:w], wz[:, c, i * 128:(i + 1) * 128], hT[:, c, PADL + t0:PADL + t0 + w],
                           c == 0, c == 7, r=[stok, ("h", c)], w=[("ps", bank)])
                    sz, sztok = tmp_ring.next()
                    actf(sz[:, 0:w], ps[bank][:, 0:w], AF.Silu, r=[("ps", bank)], w=[sztok])
                    vtt(WK[:, i, t0:t0 + w], WK[:, i, t0:t0 + w], sz[:, 0:w], ALU.mult, r=[("wk", i), sztok],
                        w=[("wk", i)])
                    sq, sqtok = tmp_ring.next()
                    sqv = sq[:, :].bitcast(BF16)
                    actf(sqv[:, 0:w], WK[:, i, t0:t0 + w], AF.Square, r=[("wk", i)], w=[sqtok])
                    mm(ps[2][:, 0:w], ones_g[:], sqv[:, 0:w], i == 0, i == 3, r=["ones_g", sqtok], w=[("ps", 2)])
                rs, rstok = tmp_ring.next()
                actf(rs[:, 0:w], ps[2][:, 0:w], AF.Sqrt, r=[("ps", 2), "epst"], w=[rstok], bias=epst[:], scale=1.0)
                P.add("dve", lambda e, rs=rs, w=w: e.reciprocal(out=rs[:, 0:w], in_=rs[:, 0:w]), r=[rstok], w=[rstok])
                for i in range(4):
                    vstt(WK[:, i, t0:t0 + w], WK[:, i, t0:t0 + w], pcol("ssd_nw", 4 * g + i), rs[:, 0:w],
                         ALU.mult, ALU.mult, r=[("wk", i), rstok, "prm"], w=[("wk", i)])
            if True:
                out_proj(W["ssd_w_out"][0], list(range(4 * g, 4 * g + 4)), WK, lambda ci: ("wk", ci), [6, 7])
        P.dma("sp", ssmc_out, stg, r=["stage"])


    def barrier():
        P.add("pe", lambda e: e.matmul(ps[7][0:1, 0:1], lhsT=ident_b[:, 0:1], rhs=ident_b[:, 0:1], start=True, stop=True),
              r=["ident_b"], w=[("ps", 7), ("bar", "pe")])
        P.add("act", lambda e: e.copy(out=dmy["act"][:, 0:1], in_=dmy["act"][:, 1:2]), w=[("bar", "act")])
        P.add("dve", lambda e: e.tensor_copy(out=dmy["dve"][:, 0:1], in_=dmy["dve"][:, 1:2]), w=[("bar", "dve")])
        P.add("pool", lambda e: e.tensor_copy(out=dmy["pool"][:, 0:1], in_=dmy["pool"][:, 1:2]), w=[("bar", "pool")])
        P.dma("sp", dmy["sp"][:, 0:1], dmy["sp"][:, 1:2], w=[("bar", "sp")])

    def attn_layer(li):
        rmsnorm("norm_mix", (li,))
        barrier()
        w_qkv = W["attn_w_qkv"][0]
        WKf = WK[:, :, :].rearrange("p c t -> p (c t)")

        def reg(o, n):
            return WKf[:, o:o + n]
        QT = reg(0, T)
        KT = reg(T, T)
        VT = reg(2 * T, T)
        OT = reg(3 * T, T)
        VA = reg(4 * T, 16 * 2 * 65).rearrange("p (t h e) -> p t h e", t=16, h=2)
        NMT = reg(4 * T + 2080, T)
        CM = reg(5 * T + 2080, 512).rearrange("p (k q) -> p k q", k=2)
        o0 = 5 * T + 2080 + 512
        acm = reg(o0, 1024).bitcast(F32).rearrange("p (a b) -> p a b", a=4)
        Pm_b = reg(o0 + 1024, 128)
        kmf = reg(o0 + 1152, 16).bitcast(F32)
        kmb2 = reg(o0 + 1168, 16)
        ptf = reg(o0 + 1184, 512).bitcast(F32)
        idx_i = reg(o0 + 1696, 512).bitcast(I32)
        pti = reg(o0 + 2208, 512).bitcast(I32)
        assert o0 + 2984 <= 8 * T
        ropes = rstd[:, :].bitcast(BF16).rearrange("p (a t) -> p a t", a=2)
        cosT, sinT = ropes[:, 0, :], ropes[:, 1, :]
        SS = sst[0][:, :, :].rearrange("p a b -> p (a b)")
        QS = SS[:, 0:128].rearrange("p (c b) -> p c b", c=8)
        KS = SS[:, 128:256].rearrange("p (c b) -> p c b", c=8)
        VS = SS[:, 256:384].rearrange("p (c b) -> p c b", c=8)
        OS = SS[:, 384:512].rearrange("p (c b) -> p c b", c=8)
        OSb = sst[1][:, 0, :].bitcast(BF16)[:, 0:128].rearrange("p (c b) -> p c b", c=8)

        for a_ in range(2):
            for h_ in range(2):
                P.dma("pool", ropes[:, a_, h_ * 1032:(h_ + 1) * 1032], rope_in[:, a_, h_ * 1032:(h_ + 1) * 1032],
                      w=["rope"])
        P.dma("pool", CM, cm_in, w=["cm"])
        P.dma("sp", acm, acm_in, w=["acm"])
        P.dma("sp", pti, pt_in.partition_broadcast(128), w=["pti"])
        vcp(Pm_b, acm[:, 0, :], r=["acm"], w=["pmb"])
        vcp(ptf, pti, r=["pti"], w=["ptf"])
        vts(ptf, ptf, 128.0, acm[:, 3, 0:1], ALU.mult, ALU.add, r=["ptf", "acm"], w=["ptf"])
        vcp(idx_i, ptf, r=["ptf"], w=["idx"])
        P.add("pool", lambda e: e.memset(VA[:, :, :, 64:65], 1.0), w=["va"])
        SEL = []
        for hl_, tns in enumerate((dt_tok, dtA_tok)):
            sv_ = tns[:, :, :].rearrange("p a b -> p (a b)")[:, 0:512].bitcast(BF16).rearrange("p (r m) -> p r m", r=8)
            SEL.append(sv_)
            vcp(sv_[0:16, :, :], ident_b[0:16, 8 * hl_:8 * hl_ + 8].unsqueeze(2).broadcast_to([16, 8, 128]),
                r=["ident_b"], w=["sel"])

        import os
        for g in range(8 if os.environ.get('SK_GRP') is None else 0):
            for which, (dst, dtok) in enumerate(((QT, "qt"), (KT, "kt"), (VT, "vt"))):
                if os.environ.get('SK_W%d' % which) is not None:
                    continue
                slot, stok = ws_ring.next()
                load_w(w_qkv, list(range(8)), which * D + g * 128, 128, slot, 0, stok)
                wv = wview(slot, 0, 8, 128)
                for ti, (t0, w) in enumerate(TT):
                    bank = ti % 2
                    for c in range(8):
                        mm(ps[bank][:, 0:w], wv[:, c, :], hT[:, c, PADL + t0:PADL + t0 + w], c == 0, c == 7,
                           r=[stok, ("h", c)], w=[("ps", bank)])
                    pq = ps[bank]
                    if which == 2:
                        vf, vftok = tmp_ring.next()
                        acp(vf[:, 0:w], pq[:, 0:w], r=[("ps", bank)], w=[vftok])
                        P.dma("sp", vT_out[:, g, t0:t0 + w], vf[:, 0:w], r=[vftok])
                        vcp(VT[:, t0:t0 + w], vf[:, 0:w], r=[vftok], w=["vt"], eng="pool")
                        if t0 == NP:
                            vcp(VS[:, g, :], vf[:, 0:NS], r=[vftok], w=["ss"], eng="pool")
                        continue
                    qs, qstok = tmp_ring.next()
                    qsb = qs[:, :].bitcast(BF16)
                    acp(qsb[:, 0:w], pq[:, 0:w], r=[("ps", bank)], w=[qstok])
                    ROT = os.environ.get('SK_ROT') is None
                    if ROT:
                        mm(ps[2 + bank][:, 0:w], Pm_b, qsb[:, 0:w], True, True, r=["pmb", qstok], w=[("ps", 2 + bank)])
                    t1, t1tok = tmp_ring.next()
                    acp(t1[:, 0:w], pq[:, 0:w], r=[("ps", bank)], w=[t1tok])
                    vtt(t1[:, 0:w], t1[:, 0:w], cosT[:, t0:t0 + w], ALU.mult, r=[t1tok, "rope"], w=[t1tok])
                    t2, t2tok = tmp_ring.next()
                    if ROT:
                        acp(t2[:, 0:w], ps[2 + bank][:, 0:w], r=[("ps", 2 + bank)], w=[t2tok])
                        vtt(t2[:, 0:w], t2[:, 0:w], sinT[:, t0:t0 + w], ALU.mult, r=[t2tok, "rope"], w=[t2tok])
                        vtt(t1[:, 0:w], t1[:, 0:w], t2[:, 0:w], ALU.add, r=[t1tok, t2tok], w=[t1tok])
                    acp(dst[:, t0:t0 + w], t1[:, 0:w], r=[t1tok], w=[dtok])
                    if which == 1:
                        P.dma("sp", kT_out[:, g, t0:t0 + w], t1[:, 0:w], r=[t1tok])
                        if t0 < NP and os.environ.get('SK_RED') is None:
                            P.add("dve", lambda e, ti=ti, t1=t1: e.tensor_reduce(
                                out=kmf[:, 2 * ti:2 * ti + 2], in_=t1[:, 0:512].rearrange("p (j k) -> p j k", j=2),
                                axis=AX.X, op=ALU.add), r=[t1tok], w=["kmf"])
                    if t0 == NP:
                        vcp((QS if which == 0 else KS)[:, g, :], t1[:, 0:NS], r=[t1tok], w=["ss"], eng="pool")
            vts(kmf, kmf, 1.0 / 256.0, None, ALU.mult, None, r=["kmf"], w=["kmf"])
            vtt(kmb2.rearrange("p (h j) -> p h j", h=2), kmf.unsqueeze(1).broadcast_to([128, 2, 8]),
                acm[:, 3, 1:3].unsqueeze(2).broadcast_to([128, 2, 8]), ALU.mult, r=["kmf", "acm"], w=["kmb"])
            for tc in range(16 if os.environ.get('SK_VTOK') is None else 0):
                bank = 4 + tc % 2
                mm(ps[bank][:, 0:128], VT[:, tc * 128:(tc + 1) * 128], ident_b[:], True, True, r=["vt", "ident_b"],
                   w=[("ps", bank)])
                acp(VA[:, tc, :, 0:64], ps[bank][:, 0:128].rearrange("p (h d) -> p h d", h=2), r=[("ps", bank)],
                    w=["va"])
            vts(VT, KT, acm[:, 3, 2:3], None, ALU.mult, None, r=["kt", "vt", "acm"], w=["vt"])
            vts(KT, KT, acm[:, 3, 1:2], None, ALU.mult, None, r=["kt", "acm"], w=["kt"])
            KTh = [KT, VT]
            for qi in range(14):
                qc = qi + 2
                mm(ps[6][:, qi * 16:(qi + 1) * 16], QT[:, qc * 128:(qc + 1) * 128], kmb2, True, True,
                   r=["qt", "kmb"], w=[("ps", 6)])
            ga, gatok = tmp_ring.next()
            gb, gbtok = tmp_ring.next()
            gm = ga[:, 0:224]
            m8 = ga[:, 224:448]
            sel = gb[:, 0:224]
            nmb = gb[:, 256:368].bitcast(BF16)

            def v4(ap):
                return ap.rearrange("p (o f j) -> p o f j", o=7, f=4)

            def c4(mat):
                return acm[:, mat, 8:64].rearrange("p (o j) -> p o j", o=7).unsqueeze(2).broadcast_to([128, 7, 4, 8])
            for o_ in range(7):
                vtt(gm[:, o_ * 32:(o_ + 1) * 32].rearrange("p (f j) -> p f j", f=4),
                    ps[6][:, o_ * 32:(o_ + 1) * 32].rearrange("p (f j) -> p f j", f=4),
                    acm[:, 1, (o_ + 1) * 8:(o_ + 2) * 8].unsqueeze(1).broadcast_to([128, 4, 8]), ALU.add,
                    r=[("ps", 6), "acm"], w=[gatok])
            g3 = gm.rearrange("p (r j) -> p r j", j=8)
            t3 = m8.rearrange("p (r j) -> p r j", j=8)
            c3 = sel.rearrange("p (r j) -> p r j", j=8)
            for i_ in range(8):
                if i_ == 0:
                    vtt(c3, g3[:, :, 0:1].broadcast_to([128, 28, 8]), g3, ALU.is_gt, r=[gatok], w=[gbtok])
                else:
                    vtt(t3, g3[:, :, i_:i_ + 1].broadcast_to([128, 28, 8]), g3, ALU.is_gt, r=[gatok], w=[gatok])
                    vtt(c3, c3, t3, ALU.add, r=[gbtok, gatok], w=[gbtok])
            vts(sel, sel, -1.0, 2.5, ALU.mult, ALU.add, r=[gbtok], w=[gbtok])
            vts(sel, sel, 0.5, 0.0, ALU.min, ALU.max, r=[gbtok], w=[gbtok])
            for o_ in range(7):
                vtt(sel[:, o_ * 32:(o_ + 1) * 32].rearrange("p (f j) -> p f j", f=4),
                    sel[:, o_ * 32:(o_ + 1) * 32].rearrange("p (f j) -> p f j", f=4),
                    acm[:, 2, (o_ + 1) * 8:(o_ + 2) * 8].unsqueeze(1).broadcast_to([128, 4, 8]), ALU.mult,
                    r=[gbtok, "acm"], w=[gbtok])
            vts(nmb, sel, 60000.0, -30000.0, ALU.mult, ALU.add, r=[gbtok], w=[gbtok])
            for qi in range(14):
                pb = 6 + qi % 2
                mm(ps[pb][0:16, 0:128], nmb[:, qi * 16:(qi + 1) * 16], ident_b[:],
                   True, True, r=[gbtok, "ident_b"], w=[("ps", pb)])
                acp(NMT[0:16, 256 + qi * 128:256 + qi * 128 + 128], ps[pb][0:16, 0:128], r=[("ps", pb)], w=["nmt"])
            sbi = 0
            import os
            for qb in range(8 if os.environ.get('SK_ATT') is None else 0):
                q0 = qb * 256
                nkt = 2 * (qb + 1)
                ottok = "otb"
                otv = reg(o0 + 2720, 256).rearrange("p (a f) -> p a f", a=2)
                rz = reg(o0 + 2976, 8).bitcast(F32)
                for hl in range(2):
                    rows = slice(64 * hl, 64 * hl + 64)
                    pso = ps[4 + hl]
                    for kt in range(nkt):
                        j = kt // 2
                        sbank = sbi % 4
                        sbi += 1
                        mm(ps[sbank][:, 0:256], KTh[hl][:, kt * 128:(kt + 1) * 128], QT[:, q0:q0 + 256], True, False,
                           r=["kt", "vt", "qt"], w=[("ps", sbank)])
                        if j < qb:
                            r_ = hl * 8 + j
                            mm(ps[sbank][:, 0:256], SEL[hl][0:16, j, :],
                               NMT[0:16, q0:q0 + 256], False, True, r=["sel", "nmt"], w=[("ps", sbank)])
                        else:
                            mm(ps[sbank][:, 0:256], ident_b[:], CM[:, kt % 2, :], False, True, r=["ident_b", "cm"],
                               w=[("ps", sbank)])
                        pt_, pttok = tmp_ring.next()
                        ptb = pt_[:, :].bitcast(BF16)
                        actf(ptb[:, 0:256], ps[sbank][:, 0:256], AF.Exp, r=[("ps", sbank)], w=[pttok], scale=0.125)
                        for qh in range(2):
                            mm(ps[4 + qh][:, 0:65], ptb[:, qh * 128:(qh + 1) * 128], VA[:, kt, hl, :],
                               kt == 0, kt == nkt - 1, r=[pttok, "va"], w=[("ps", 4 + qh)])
                    for qh in range(2):
                        P.add("dve", lambda e, qh=qh, hl=hl, rz=rz: e.reciprocal(
                            out=rz[:, 2 * hl + qh:2 * hl + qh + 1], in_=ps[4 + qh][:, 64:65]),
                              r=[("ps", 4 + qh)], w=[ottok])
                        vts(otv[:, qh, 64 * hl:64 * hl + 64], ps[4 + qh][:, 0:64],
                            rz[:, 2 * hl + qh:2 * hl + qh + 1], None, ALU.mult, None, r=[("ps", 4 + qh), ottok],
                            w=[ottok])
                for qh in range(2):
                    mm(ps[6][:, qh * 128:(qh + 1) * 128], otv[:, qh, :], ident_b[:], True, True,
                       r=[ottok, "ident_b"], w=[("ps", 6)])
                acp(OT[:, q0:q0 + 256], ps[6][:, 0:256], r=[("ps", 6)], w=["ot"])
            if os.environ.get('SK_OPJ') is None:
              out_proj(W["attn_w_o"][0], [g], OT.rearrange("p (c t) -> p c t", c=1), lambda ci: "ot", [6, 7],
                     tiles=TT[0:4])

        barrier()
        kt_b = [reg(0, 1024), reg(1024, 1024), reg(2048, 1024)]
        vt_b = [reg(3072, 1024), reg(4096, 1024), reg(5120, 1024)]
        qbc = reg(6144, 1024)
        vnb = reg(7168, 1024)
        prod = reg(8192, 1024)
        prodv = reg(9216, 1024)
        Qd = prodv
        Lall = reg(10240, 512).bitcast(F32)
        Pm_ = reg(10752, 512).bitcast(F32)
        gate = reg(11264, 256).bitcast(F32)
        bias = reg(11520, 256).bitcast(F32)
        m8s = reg(11776, 256).bitcast(F32)
        sm2 = reg(12032, 256).bitcast(F32)
        own, pown, Zt, rZ = sm2[:, 0:16], sm2[:, 16:32], sm2[:, 32:48], sm2[:, 48:64]
        qk, qk2 = sm2[:, 64:72], sm2[:, 72:88]
        kt_ring = Ring("ktile", kt_b)
        vt_ring = Ring("vtile", vt_b)
        import os
        def gather(cache_ap, ring, col):
            tile_, tok_ = ring.next()
            P.add("pool", lambda e: e.indirect_dma_start(
                out=tile_, out_offset=None, in_=cache_ap,
                in_offset=bass.IndirectOffsetOnAxis(ap=idx_i[:, col:col + 1], axis=0)),
                  r=["idx"], w=[tok_], dma=True)
            return tile_, tok_
        kq, vq = [], []
        if os.environ.get('SK_DEC') is None:
            for c3 in range(3):
                kq.append(gather(cache_k_in, kt_ring, c3))
        for b in range(NS if os.environ.get('SK_DEC') is None else 0):
            for (src, dstb, dtok) in ((QS, qbc, "qbc"), (VS, vnb, "vnb")):
                vtt(Qd.rearrange("p (c n) -> p c n", c=8), ident_b[:].unsqueeze(1).broadcast_to([128, 8, 128]),
                    src[:, :, b:b + 1].broadcast_to([128, 8, 128]), ALU.mult, r=["ident_b", "ss"], w=["prodv"])
                for k2 in range(2):
                    mm(ps[k2][:, :], ones_b[:], Qd[:, 512 * k2:512 * k2 + 512], True, True, r=["ones_b", "prodv"],
                       w=[("ps", k2)])
                    acp(dstb[:, 512 * k2:512 * k2 + 512], ps[k2][:, :], r=[("ps", k2)], w=[dtok])
            vtt(qk, QS[:, :, b], KS[:, :, b], ALU.mult, r=["ss"], w=["sm2"])
            vtt(qk2.rearrange("p (c h) -> p c h", c=8), qk.unsqueeze(2).broadcast_to([128, 8, 2]),
                acm[:, 3, 1:3].unsqueeze(1).broadcast_to([128, 8, 2]), ALU.mult, r=["sm2", "acm"], w=["sm2"])
            mm(ps[2][:, 0:16], ones_f, qk2, True, True, r=["cmat", "sm2"], w=[("ps", 2)])
            acp(own, ps[2][:, 0:16], r=[("ps", 2)], w=["sm2"])
            for pg in range(16):
                ktile, kttok = kq.pop(0)
                vtt(prod, ktile, qbc, ALU.mult, r=[kttok, "qbc"], w=["prod"])
                P.add("dve", lambda e, pg=pg: e.tensor_reduce(
                    out=Lall[:, pg * 16:(pg + 1) * 16], in_=prod.rearrange("p (h d) -> p h d", h=16),
                    axis=AX.X, op=ALU.add), r=["prod"], w=["lall"])
                nxt = b * 16 + pg + 3
                if nxt < NS * 16:
                    kq.append(gather(cache_k_in, kt_ring, nxt))
                if pg == 12:
                    for c3 in range(3):
                        vq.append(gather(cache_v_in, vt_ring, b * 16 + c3))
            mm(ps[3][:, 0:256], ones_f, Lall, True, True, r=["cmat", "lall"], w=[("ps", 3)])
            acp(Pm_, ps[3][:, 0:256], r=[("ps", 3)], w=["pm"])
            g4 = Pm_.rearrange("p (j two h) -> p j two h", two=2, h=16)
            vtt(gate.rearrange("p (j h) -> p j h", j=8), g4[:, :, 0, :], g4[:, :, 1, :], ALU.add,
                r=["pm"], w=["gate"])
            for h in range(16):
                P.add("dve", lambda e, h=h: e.max(out=m8s[:, h * 8:(h + 1) * 8],
                                                   in_=gate.rearrange("p (j h) -> p h j", j=8)[:, h, :]),
                      r=["gate"], w=["m8s"])
            vtt(bias.rearrange("p (j h) -> p j h", j=8), gate.rearrange("p (j h) -> p j h", j=8),
                m8s.rearrange("p (h k) -> p h k", k=8)[:, :, 2].unsqueeze(1).broadcast_to([128, 8, 16]), ALU.is_ge,
                r=["gate", "m8s"], w=["bias"])
            vts(bias, bias, 30000.0, -30000.0, ALU.mult, ALU.add, r=["bias"], w=["bias"])
            vtt(Lall.rearrange("p (j two h) -> p j two h", two=2, h=16),
                Lall.rearrange("p (j two h) -> p j two h", two=2, h=16),
                bias.rearrange("p (j h) -> p j h", j=8).unsqueeze(2).broadcast_to([128, 8, 2, 16]), ALU.add,
                r=["lall", "bias"], w=["lall"])
            actf(Pm_, Lall, AF.Exp, r=["lall"], w=["pm"], scale=0.125)
            actf(pown, own, AF.Exp, r=["sm2"], w=["sm2"], scale=0.125)
            mm(ps[3][:, 256:512], ones_f, Pm_, True, True, r=["cmat", "pm"], w=[("ps", 3)])
            P.add("dve", lambda e: e.tensor_reduce(out=Zt, in_=ps[3][:, 256:512].rearrange("p (g h) -> p h g", h=16),
                                                   axis=AX.X, op=ALU.add), r=[("ps", 3)], w=["sm2"])
            vtt(Zt, Zt, pown, ALU.add, r=["sm2"], w=["sm2"])
            P.add("dve", lambda e: e.reciprocal(out=rZ, in_=Zt), r=["sm2"], w=["sm2"])
            for pg in range(16):
                vtile, vttok = vq.pop(0)
                vtt(prodv.rearrange("p (h d) -> p h d", h=16), vtile.rearrange("p (h d) -> p h d", h=16),
                    Pm_[:, pg * 16:(pg + 1) * 16].unsqueeze(2).broadcast_to([128, 16, 64]), ALU.mult,
                    r=[vttok, "pm"], w=["prodv"])
                for k2 in range(2):
                    mm(ps[4 + k2][:, :], ones_b[:], prodv[:, 512 * k2:512 * k2 + 512], pg == 0, pg == 15,
                       r=["ones_b", "prodv"], w=[("ps", 4 + k2)])
                if pg + 3 < 16:
                    vq.append(gather(cache_v_in, vt_ring, b * 16 + pg + 3))
            for k2 in range(2):
                tq, tqtok = tmp_ring.next()
                tq3 = tq[:, :].rearrange("p (h d) -> p h d", h=8)
                vtt(tq3, vnb[:, 512 * k2:512 * k2 + 512].rearrange("p (h d) -> p h d", h=8),
                    pown[:, 8 * k2:8 * k2 + 8].unsqueeze(2).broadcast_to([128, 8, 64]), ALU.mult,
                    r=["vnb", "sm2"], w=[tqtok])
                vtt(tq[:, :], tq[:, :], ps[4 + k2][:, :], ALU.add, r=[tqtok, ("ps", 4 + k2)], w=[tqtok])
                vtt(tq3, tq3, rZ[:, 8 * k2:8 * k2 + 8].unsqueeze(2).broadcast_to([128, 8, 64]), ALU.mult,
                    r=[tqtok, "sm2"], w=[tqtok])
                tq4 = tq[:, :].rearrange("p (c n) -> p c n", c=4)
                vtt(tq4, tq4, ident_f.unsqueeze(1).broadcast_to([128, 4, 128]), ALU.mult, r=[tqtok, "cmat"], w=[tqtok])
                P.add("dve", lambda e, tq4=tq4, k2=k2, b=b: e.tensor_reduce(
                    out=OS[:, 4 * k2:4 * k2 + 4, b], in_=tq4, axis=AX.X, op=ALU.add), r=[tqtok], w=["os"])
        vcp(OSb, OS, r=["os"], w=["osb"])
        out_proj(W["attn_w_o"][0], list(range(8)), OSb, lambda ci: "osb", [6, 7], tiles=[TT[4]], src_t0=NP)
        barrier()

    for li in range(nlayers):
        kind, j = li % 3, li // 3
        if kind == 0:
            sconv_layer(li, j)
        elif kind == 1:
            ssd_layer(li)
        else:
            attn_layer(li)
        ffn_layer(li)

    rmsnorm("norm_final", (), final=True)

    P.emit(stack)
    stack.close()
    return nc


def make_cmat():
    i = np.arange(128)
    ident = (i[:, None] == i[None, :]).astype(np.float32)
    tri = (i[:, None] <= i[None, :]).astype(np.float32)
    maskneg = np.where(i[None, :] >= i[:, None], 0.0, -30000.0).astype(np.float32)
    ones = np.ones((128, 128), np.float32)
    return np.ascontiguousarray(np.stack([ident, tri, maskneg, ones], axis=1))


def make_rope():
    theta, rot = 500000.0, 16
    half = rot // 2
    inv_freq = (np.float32(theta) ** (-(np.arange(half, dtype=np.float32) * np.float32(2.0)) / np.float32(rot))).astype(np.float32)
    pos = np.concatenate([np.arange(NP), np.full(NS, NP)]).astype(np.float32)
    out = np.zeros((128, 2, T), np.float32)
    out[:, 0, :] = 1.0
    for p in range(128):
        d = p % 64
        if d < rot:
            ang = (pos * inv_freq[d % half]).astype(np.float32)
            out[p, 0] = np.cos(ang)
            out[p, 1] = np.sin(ang)
    return out


def make_cm():
    k = np.arange(128)[:, None, None]
    kt = np.arange(2)[None, :, None]
    q = np.arange(256)[None, None, :]
    return np.ascontiguousarray(np.where(kt * 128 + k <= q, 0.0, -30000.0).astype(np.float32))


def make_acm():
    a = np.zeros((128, 4, 128), np.float32)
    for m in range(128):
        d = m % 64
        if d < 8:
            a[m + 8, 0, m] = -1.0
        elif d < 16:
            a[m - 8, 0, m] = 1.0
    for ob in range(8):
        for j in range(8):
            a[:, 1, ob * 8 + j] = 0.0 if j < ob else -1e9
            a[:, 2, ob * 8 + j] = 1.0 if j < ob else 0.0
    a[:, 3, 0] = np.arange(128)
    a[:64, 3, 1] = 1.0
    a[64:, 3, 2] = 1.0
    return a


def fm_tokens(a):
    r, F = a.shape
    return np.ascontiguousarray(a.reshape(r, F // 128, 128).transpose(2, 1, 0))


_CACHE = {}


def kernel(**inp):
    nlayers = int(inp.pop("_nlayers", 4))
    small_cache = bool(inp.pop("_small_cache", False))
    f32 = lambda k: np.asarray(inp[k], dtype=np.float32)
    pk = pack_params(inp)
    params = pk.build()
    nc = build_program(pk.items, pk.cols, nlayers=nlayers, npool=(2 if small_cache else NPOOL))

    x_prompt = f32("x_prompt")
    x_sample = f32("x_sample")
    st_sconv = f32("state_sconv")
    st_ffn = f32("state_ffn_conv")
    shared = {
        "params": params,
        "sconv_w_in": f32("sconv_w_in"), "sconv_w_out": f32("sconv_w_out"),
        "ffn_w_up": f32("ffn_w_up"), "ffn_w_down": f32("ffn_w_down"),
        "ssd_w_in": f32("ssd_w_in"), "ssd_w_out": f32("ssd_w_out"),
        "cmat": make_cmat(),
        "attn_w_qkv": f32("attn_w_qkv"), "attn_w_o": f32("attn_w_o"),
        "rope": make_rope(), "cm_mask": make_cm(), "acm": make_acm(),
    }
    if small_cache:
        shared["cache_k"] = np.zeros((256, D), np.float32)
        shared["cache_v"] = np.zeros((256, D), np.float32)
    else:
        shared["cache_k"] = f32("cache_k")[0].reshape(NPOOL * 128, D)
        shared["cache_v"] = f32("cache_v")[0].reshape(NPOOL * 128, D)
    page_table = np.asarray(inp["page_table"]).astype(np.int32)
    if small_cache:
        page_table = np.zeros_like(page_table)
    st_ssm = f32("state_ssm")
    st_ssmc = f32("state_ssm_conv")
    in_maps = []
    for core in range(NCORES):
        sl = slice(core * NS, (core + 1) * NS)
        xcat = np.concatenate([x_prompt[core], x_sample[sl, 0]], axis=0)
        m = dict(shared)
        m["xT_in"] = fm_tokens(xcat)
        sp = st_sconv[:, sl].reshape(2, NS, 2, 8, 128).transpose(0, 4, 3, 1, 2)
        m["sconv_past"] = np.ascontiguousarray(sp)
        fp = st_ffn[:, sl].reshape(4, NS, 2, 44, 128).transpose(0, 4, 3, 1, 2)
        m["ffn_past"] = np.ascontiguousarray(fp)
        m["ssmc_past"] = np.ascontiguousarray(st_ssmc[0, sl].reshape(NS, 3, 24, 128).transpose(3, 2, 0, 1))
        m["ssm_state"] = np.ascontiguousarray(st_ssm[0, sl].reshape(NS * 2048, 128))
        m["page_tab"] = np.ascontiguousarray(page_table[sl].reshape(1, NS * 16))
        in_maps.append(m)
    res = run_bass_kernel_spmd(nc, in_maps, core_ids=list(range(NCORES)))
    R = res.results
    global _DBG
    _DBG = R

    y_prompt = np.zeros((8, NP, D), np.float32)
    y_sample = np.zeros((128, 1, D), np.float32)
    sconv_p = np.zeros((2, 8, 2, D), np.float32)
    sconv_s = np.zeros((2, 128, 2, D), np.float32)
    ffn_p = np.zeros((4, 8, 2, 2 * DFF), np.float32)
    ffn_s = np.zeros((4, 128, 2, 2 * DFF), np.float32)
    ssm_p = np.zeros((1, 8, 32, 64, 128), np.float32)
    ssm_s = np.zeros((1, 128, 32, 64, 128), np.float32)
    ssmc_p = np.zeros((1, 8, 3, 3072), np.float32)
    ssmc_s = np.zeros((1, 128, 3, 3072), np.float32)
    k_p = np.zeros((1, 8, NP, 16, 64), np.float32)
    v_p = np.zeros((1, 8, NP, 16, 64), np.float32)
    k_s = np.zeros((1, 128, 1, 16, 64), np.float32)
    v_s = np.zeros((1, 128, 1, 16, 64), np.float32)
    for core in range(NCORES):
        sl = slice(core * NS, (core + 1) * NS)
        r = R[core]
        yT = r["yT_out"]
        yt = yT.transpose(2, 1, 0).reshape(T, D)
        y_prompt[core] = yt[:NP]
        y_sample[sl, 0] = yt[NP:]
        so = r["sconv_out"]
        sconv_p[:, core] = so[:, :, :, 0:2].transpose(0, 3, 2, 1).reshape(2, 2, D)
        ss = so[:, :, :, 2:].reshape(2, 128, 8, NS, 2).transpose(0, 3, 4, 2, 1).reshape(2, NS, 2, D)
        sconv_s[:, sl] = ss
        fo = r["ffn_out"]
        ffn_p[:, core] = fo[:, :, :, 0:2].transpose(0, 3, 2, 1).reshape(4, 2, 2 * DFF)
        fs = fo[:, :, :, 2:].reshape(4, 128, 44, NS, 2).transpose(0, 3, 4, 2, 1).reshape(4, NS, 2, 2 * DFF)
        ffn_s[:, sl] = fs
        if nlayers >= 2:
            ssm_p[0, core] = r["ssm_p_out"].reshape(4, 128, 8, 64).transpose(0, 2, 3, 1).reshape(32, 64, 128)
            ssm_s[0, sl] = r["ssm_s_out"].reshape(NS, 32, 64, 128)
            co = r["ssmc_out"]
            ssmc_p[0, core] = co[:, :, 0:3].transpose(2, 1, 0).reshape(3, 3072)
            ssmc_s[0, sl] = co[:, :, 3:].reshape(128, 24, NS, 3).transpose(2, 3, 1, 0).reshape(NS, 3, 3072)
        if nlayers >= 3:
            for (dstp, dsts, nm) in ((k_p, k_s, "kT_out"), (v_p, v_s, "vT_out")):
                kt_ = r[nm].transpose(2, 1, 0).reshape(T, D)
                dstp[0, core] = kt_[:NP].reshape(NP, 16, 64)
                dsts[0, sl, 0] = kt_[NP:].reshape(NS, 16, 64)
    H, Pd, N = 32, 64, 128
    return (y_prompt, y_sample, sconv_p, sconv_s,
            ssm_p, ssm_s, ssmc_p, ssmc_s,
            k_p, v_p, k_s, v_s,
            ffn_p, ffn_s)
```

```python
import contextlib
import numpy as np
import concourse.bass as bass
import concourse.mybir as mybir
from concourse.bass_utils import run_bass_kernel_spmd
from concourse.ap import AP

F32 = mybir.dt.float32
BF16 = mybir.dt.bfloat16
I32 = mybir.dt.int32
AF = mybir.ActivationFunctionType
ALU = mybir.AluOpType
AX = mybir.AxisListType

NCORES = 8
D = 1024
NP = 2048
NS = 16
T = NP + NS
PADL = 3
TP = T + PADL + 1
DFF = 2816
NFC = 22
NPOOL = 2560
EPS = 1e-6
TT = [(0, 512), (512, 512), (1024, 512), (1536, 512), (2048, 16)]


def conv_tiles(halo):
    n = 512 - halo
    out = []
    s = 0
    while s < NP:
        out.append((s, min(n, NP - s)))
        s += n
    return out


class Op:
    __slots__ = ("eng", "fn", "dma", "deps", "inc", "sem", "val", "idx")


class Prog:
    ENGS = ("pe", "act", "dve", "pool", "sp")
    NDMASEM = {"sp": 8, "pool": 8, "act": 4}

    def __init__(self, nc):
        self.nc = nc
        self.ops = []
        self.last_w = {}
        self.readers = {}

    BAR = tuple(("bar", e) for e in ("pe", "act", "dve", "pool", "sp"))

    def add(self, eng, fn, r=(), w=(), dma=False):
        r = list(r) + list(self.BAR)
        op = Op()
        op.eng, op.fn, op.dma = eng, fn, dma
        op.idx = len(self.ops)
        op.inc = dma
        op.sem = None
        op.val = 0
        deps = {}
        for k in r:
            lw = self.last_w.get(k)
            if lw is not None:
                deps[lw] = True
            self.readers.setdefault(k, []).append(op.idx)
        for k in w:
            lw = self.last_w.get(k)
            if lw is not None and lw not in deps:
                deps[lw] = False
            for rd in self.readers.get(k, ()):
                if rd != op.idx and rd not in deps:
                    deps[rd] = False
            self.readers[k] = []
            self.last_w[k] = op.idx
        op.deps = deps
        self.ops.append(op)
        return op

    def dma(self, eng, out, in_, r=(), w=()):
        return self.add(eng, lambda e: e.dma_start(out=out, in_=in_), r=r, w=w, dma=True)

    def emit(self, stack):
        nc = self.nc
        ops = self.ops
        for op in ops:
            for d, raw in op.deps.items():
                y = ops[d]
                if y.dma:
                    continue
                if y.eng == op.eng and not raw:
                    continue
                y.inc = True
        esem = {e: stack.enter_context(nc.semaphore("es_" + e)) for e in ("pe", "act", "dve", "pool")}
        dsem = {q: [stack.enter_context(nc.semaphore("ds_%s%d" % (q, i))) for i in range(n)]
                for q, n in self.NDMASEM.items()}
        cnt = {e: 0 for e in esem}
        dcnt = {q: 0 for q in dsem}
        for op in ops:
            if op.dma:
                m = dcnt[op.eng]
                dcnt[op.eng] += 1
                n = self.NDMASEM[op.eng]
                op.sem = dsem[op.eng][m % n]
                op.val = 16 * (m // n + 1)
            elif op.inc:
                cnt[op.eng] += 1
                op.sem = esem[op.eng]
                op.val = cnt[op.eng]
        block = stack.enter_context(nc.Block())

        def run(engname):
            def body(e):
                waited = {}
                for op in ops:
                    if op.eng != engname:
                        continue
                    need = {}
                    for d, raw in op.deps.items():
                        y = ops[d]
                        if (not y.dma) and y.eng == engname and not raw:
                            continue
                        key = id(y.sem)
                        if key not in need or need[key][1] < y.val:
                            need[key] = (y.sem, y.val)
                    if op.dma and op.val > 16:
                        key = id(op.sem)
                        v = op.val - 16
                        if key not in need or need[key][1] < v:
                            need[key] = (op.sem, v)
                    for key, (s, v) in need.items():
                        if waited.get(key, 0) >= v:
                            continue
                        e.wait_ge(s, v)
                        waited[key] = v
                    ins = op.fn(e)
                    if op.dma:
                        ins.then_inc(op.sem, 16)
                    elif op.inc:
                        ins.then_inc(op.sem, 1)
                if engname in dsem:
                    m = dcnt[engname]
                    n = self.NDMASEM[engname]
                    for i in range(min(m, n)):
                        uses = (m - 1 - i) // n + 1
                        e.wait_ge(dsem[engname][i], 16 * uses)
            return body

        block.tensor(run("pe"))
        block.scalar(run("act"))
        block.vector(run("dve"))
        block.gpsimd(run("pool"))
        block.sync(run("sp"))


class Ring:
    def __init__(self, name, aps):
        self.name = name
        self.aps = aps
        self.i = 0

    def next(self):
        k = self.i % len(self.aps)
        self.i += 1
        return self.aps[k], (self.name, k)


class Pack:
    def __init__(self):
        self.cols = 0
        self.items = {}
        self.arrs = []

    def put(self, name, arr):
        arr = np.ascontiguousarray(arr, dtype=np.float32)
        assert arr.shape[0] == 128
        a2 = arr.reshape(128, -1)
        self.items[name] = (self.cols, arr.shape[1:])
        self.cols += a2.shape[1]
        self.arrs.append(a2)

    def build(self):
        return np.ascontiguousarray(np.concatenate(self.arrs, axis=1))


def col_layout(v):
    v = np.asarray(v, dtype=np.float32)
    F = v.shape[-1]
    lead = v.shape[:-1]
    a = v.reshape(lead + (F // 128, 128))
    return np.moveaxis(a, -1, 0)


def pack_params(inp, with_values=True):
    pk = Pack()
    g = (lambda k: np.asarray(inp[k], dtype=np.float32))
    pk.put("norm_mix", col_layout(g("norm_mix_w")))
    pk.put("norm_ffn", col_layout(g("norm_ffn_w")))
    pk.put("norm_final", col_layout(g("norm_final_w")))
    pk.put("sconv_cw", np.moveaxis(col_layout(g("sconv_conv_w")), 2, 3))
    pk.put("ffn_cw", np.moveaxis(col_layout(g("ffn_conv_w")), 2, 3))
    pk.put("ffn_cb", col_layout(g("ffn_conv_b")))
    pk.put("ssd_cw", np.moveaxis(col_layout(g("ssd_conv_w")[0]), 1, 2))
    pk.put("ssd_cb", col_layout(g("ssd_conv_b")[0]))
    pk.put("ssd_D", col_layout(np.repeat(g("ssd_d")[0], 64)))
    pk.put("ssd_nw", col_layout(g("ssd_norm_w")[0]))
    pk.put("ssd_dtb", np.broadcast_to(g("ssd_dt_bias")[0][None, :], (128, 32)))
    pk.put("ssd_alog", np.broadcast_to(g("ssd_a_log")[0][None, :], (128, 32)))
    return pk


def build_program(pk_items, pk_cols, nlayers=4, npool=NPOOL):
    nc = bass.Bass("TRN2", target_bir_lowering=False)
    P = Prog(nc)
    stack = contextlib.ExitStack()

    def din(name, shape, dt=F32):
        return nc.dram_tensor(name, list(shape), dt, kind="ExternalInput").ap()

    def dout(name, shape, dt=F32):
        return nc.dram_tensor(name, list(shape), dt, kind="ExternalOutput").ap()

    def sb(name, shape, dt):
        return stack.enter_context(nc.sbuf_tensor(name, list(shape), dt))

    xT_in = din("xT_in", [128, 8, T])
    params_in = din("params", [128, pk_cols])
    sconv_past_in = din("sconv_past", [2, 128, 8, NS, 2])
    ffn_past_in = din("ffn_past", [4, 128, 44, NS, 2])
    W = {
        "sconv_w_in": din("sconv_w_in", [2, D, 3 * D]),
        "sconv_w_out": din("sconv_w_out", [2, D, D]),
        "ffn_w_up": din("ffn_w_up", [4, D, 2 * DFF]),
        "ffn_w_down": din("ffn_w_down", [4, DFF, D]),
    }
    W["ssd_w_in"] = din("ssd_w_in", [1, D, 5152])
    W["ssd_w_out"] = din("ssd_w_out", [1, 2048, D])
    ssmc_past_in = din("ssmc_past", [128, 24, NS, 3])
    ssm_state_in = din("ssm_state", [NS * 2048, 128])
    cmat_in = din("cmat", [128, 4, 128])
    ssmc_out = dout("ssmc_out", [128, 24, 3 + 3 * NS])
    ssm_p_out = dout("ssm_p_out", [4, 128, 512])
    ssm_s_out = dout("ssm_s_out", [NS * 2048, 128])
    W["attn_w_qkv"] = din("attn_w_qkv", [1, D, 3 * D])
    W["attn_w_o"] = din("attn_w_o", [1, D, D])
    rope_in = din("rope", [128, 2, T])
    cm_in = din("cm_mask", [128, 2, 256])
    acm_in = din("acm", [128, 4, 128])
    pt_in = din("page_tab", [1, NS * 16], I32)
    cache_k_in = din("cache_k", [npool * 128, D])
    cache_v_in = din("cache_v", [npool * 128, D])
    kT_out = dout("kT_out", [128, 8, T])
    vT_out = dout("vT_out", [128, 8, T])
    yT_out = dout("yT_out", [128, 8, T])
    sconv_out = dout("sconv_out", [2, 128, 8, 2 + 2 * NS])
    ffn_out = dout("ffn_out", [4, 128, 44, 2 + 2 * NS])

    xT = sb("xT", [128, 8, T], F32)
    hT = sb("hT", [128, 8, TP], BF16)
    WK = sb("WK", [128, 8, T], BF16)
    prm = sb("prm", [128, pk_cols], F32)
    NWS = 3
    wsl = [sb("wsl%d" % i, [128, 4096], BF16) for i in range(NWS)]
    ones_m = sb("ones_m", [128, 128], BF16)
    rstd = sb("rstd", [128, T], F32)
    epst = sb("epst", [128, 1], F32)
    NTMP = 6
    tmpf = [sb("tmpf%d" % i, [128, 512], F32) for i in range(NTMP)]
    stage = sb("stage", [128, 44, 2 + 2 * NS], F32)
    pastb = sb("pastb", [128, 44, NS, 2], F32)
    psall = stack.enter_context(nc.psum_tensor("psall", [128, 8, 512], F32))
    ps = [psall[:, i, :] for i in range(8)]

    cmat = sb("cmat_sb", [128, 4, 128], F32)
    ident_f, tri_f, maskneg_f, ones_f = cmat[:, 0, :], cmat[:, 1, :], cmat[:, 2, :], cmat[:, 3, :]
    ident_b = sb("ident_b", [128, 128], BF16)
    ones_b = sb("ones_b", [128, 128], BF16)
    ones_g = sb("ones_g", [128, 128], BF16)
    onet = sb("onet", [128, 1], F32)
    dt_tok = sb("dt_tok", [128, 17, 32], F32)
    dtA_tok = sb("dtA_tok", [128, 17, 32], F32)
    a_bc = sb("a_bc", [128, 32], F32)
    prevT = sb("prevT", [128, 512], F32)
    prevT_bf = sb("prevT_bf", [128, 512], BF16)
    smt = sb("smt", [128, 64], F32)
    sst = [sb("sst%d" % i, [128, 4, 128], F32) for i in range(2)]
    snew = [sb("snew%d" % i, [128, 4, 128], F32) for i in range(1)]
    ys_t = sb("ys_t", [128, 4, NS], F32)
    sx_t = sb("sx_t", [128, 3, 4, NS], F32)
    dmy = {e: sb("dmy_" + e, [128, 2], F32) for e in ("act", "dve", "pool", "sp")}
    tmp_ring = Ring("tmpf", [t for t in tmpf])
    sst_ring = Ring("sst", sst)
    snew_ring = Ring("snew", snew)
    ws_ring = Ring("wsl", [t for t in wsl])

    def prm_ap(name):
        off, shp = pk_items[name]
        n = int(np.prod(shp))
        a = prm[:, off:off + n]
        return a, off, shp

    def pcol(name, *idx):
        off, shp = pk_items[name]
        flat = 0
        for i, s in zip(idx, shp):
            flat = flat * s + i
        return prm[:, off + flat:off + flat + 1]

    P.dma("sp", prm[:], params_in, w=["prm"])
    for c in range(8):
        P.dma("sp", xT[:, c, :], xT_in[:, c, :], w=[("x", c)])
    P.add("pool", lambda e: e.memset(ones_m[:], 1.0 / 1024.0), w=["ones"])
    P.add("pool", lambda e: e.memset(epst[:], EPS), w=["epst"])
    P.add("pool", lambda e: e.memset(onet[:], 1.0), w=["onet"])
    P.add("pool", lambda e: e.memset(ones_b[:], 1.0), w=["ones_b"])
    P.add("pool", lambda e: e.memset(ones_g[:], 1.0 / 512.0), w=["ones_g"])
    P.dma("sp", cmat[:], cmat_in, w=["cmat"])
    P.add("dve", lambda e: e.tensor_copy(out=ident_b[:], in_=ident_f), r=["cmat"], w=["ident_b"])
    P.add("pool", lambda e: e.memset(hT[:, :, 0:PADL], 0.0), w=["hpad"])


    def mm(out, lhsT, rhs, start, stop, r, w):
        P.add("pe", lambda e: e.matmul(out, lhsT=lhsT, rhs=rhs, start=start, stop=stop), r=r, w=w)

    def trp(out, in_, ident, r, w):
        P.add("pe", lambda e: e.transpose(out, in_, ident), r=r, w=w)

    def actf(out, in_, func, r, w, bias=None, scale=1.0):
        if bias is None:
            P.add("act", lambda e: e.activation(out=out, in_=in_, func=func, scale=scale), r=r, w=w)
        else:
            P.add("act", lambda e: e.activation(out=out, in_=in_, func=func, bias=bias, scale=scale), r=r, w=w)

    def acp(out, in_, r, w):
        P.add("act", lambda e: e.copy(out=out, in_=in_), r=r, w=w)

    def vtt(out, in0, in1, op, r, w, eng="dve"):
        P.add(eng, lambda e: e.tensor_tensor(out=out, in0=in0, in1=in1, op=op), r=r, w=w)

    def vts(out, in0, s1, s2, op0, op1, r, w):
        if s2 is None:
            P.add("dve", lambda e: e.tensor_scalar(out=out, in0=in0, scalar1=s1, scalar2=None, op0=op0), r=r, w=w)
        else:
            P.add("dve", lambda e: e.tensor_scalar(out=out, in0=in0, scalar1=s1, scalar2=s2, op0=op0, op1=op1),
                  r=r, w=w)

    def vstt(out, in0, scalar, in1, op0, op1, r, w, accum_out=None):
        if accum_out is None:
            P.add("dve", lambda e: e.scalar_tensor_tensor(out=out, in0=in0, scalar=scalar, in1=in1, op0=op0, op1=op1),
                  r=r, w=w)
        else:
            P.add("dve", lambda e: e.scalar_tensor_tensor(out=out, in0=in0, scalar=scalar, in1=in1, op0=op0, op1=op1,
                                                          accum_out=accum_out), r=r, w=w)

    def vcp(out, in_, r, w, eng="dve"):
        P.add(eng, lambda e: e.tensor_copy(out=out, in_=in_), r=r, w=w)

    def load_w(wap, kchunks, c0, ncols, slot_ap, slot_off, tokw):
        nk = len(kchunks)
        k0 = kchunks[0]
        assert kchunks == list(range(k0, k0 + nk))
        src = wap[k0 * 128:(k0 + nk) * 128, c0:c0 + ncols].rearrange("(c p) m -> p c m", p=128)
        dst = slot_ap[:, slot_off:slot_off + nk * ncols].rearrange("p (c m) -> p c m", c=nk)
        P.dma("pool", dst, src, w=[tokw])

    def wview(slot_ap, slot_off, nk, ncols):
        return slot_ap[:, slot_off:slot_off + nk * ncols].rearrange("p (c m) -> p c m", c=nk)

    def rmsnorm(wname, widx, final=False):
        for c in range(8):
            P.add("act", lambda e, c=c: e.activation(out=WK[:, c, :], in_=xT[:, c, :], func=AF.Square),
                  r=[("x", c)], w=[("wk", c)])
        for ti, (t0, w) in enumerate(TT):
            for c in range(8):
                P.add("pe", lambda e, c=c, w=w, t0=t0, ti=ti: e.matmul(ps[ti][:, 0:w], lhsT=ones_m[:],
                                                                     rhs=WK[:, c, t0:t0 + w],
                                                                     start=(c == 0), stop=(c == 7)),
                      r=[("wk", c), "ones"], w=[("ps", ti)])
        P.add("act", lambda e: e.activation(out=rstd[:, 0:NP].rearrange("p (a b) -> p a b", a=4),
                                            in_=psall[:, 0:4, :], func=AF.Sqrt, bias=epst[:], scale=1.0),
              r=[("ps", 0), ("ps", 1), ("ps", 2), ("ps", 3), "epst"], w=["rstd"])
        P.add("act", lambda e: e.activation(out=rstd[:, NP:T], in_=ps[4][:, 0:NS], func=AF.Sqrt,
                                            bias=epst[:], scale=1.0),
              r=[("ps", 4), "epst"], w=["rstd"])
        P.add("dve", lambda e: e.reciprocal(out=rstd[:, :], in_=rstd[:, :]), r=["rstd"], w=["rstd"])
        for (t0, w) in TT:
            for c in range(8):
                if final:
                    ap, tok = tmp_ring.next()
                    dst = ap[:, 0:w]
                else:
                    dst = hT[:, c, PADL + t0:PADL + t0 + w]
                    tok = ("h", c)
                P.add("dve", lambda e, c=c, t0=t0, w=w, dst=dst: e.scalar_tensor_tensor(
                    out=dst, in0=xT[:, c, t0:t0 + w], scalar=pcol(wname, *(widx + (c,))),
                    in1=rstd[:, t0:t0 + w], op0=ALU.mult, op1=ALU.mult),
                      r=[("x", c), "rstd", "prm"], w=[tok])
                if final:
                    P.dma("sp", yT_out[:, c, t0:t0 + w], dst, r=[tok])

    def h_dst(c, t0, w):
        return hT[:, c, PADL + t0:PADL + t0 + w]

    def add_resid(o, t0, w, bank):
        P.add("dve", lambda e: e.tensor_tensor(out=xT[:, o, t0:t0 + w], in0=xT[:, o, t0:t0 + w],
                                               in1=ps[bank][:, 0:w], op=ALU.add),
              r=[("x", o), ("ps", bank)], w=[("x", o)])

    def out_proj(wap, kchunks_all, src, src_tokf, banks, tiles=None, src_t0=0):
        tiles = TT if tiles is None else tiles
        nk = len(kchunks_all)
        gcols = 256 if nk > 8 else 512
        bi = 0
        for og in range(D // gcols):
            slot, stok = ws_ring.next()
            load_w(wap, kchunks_all, og * gcols, gcols, slot, 0, stok)
            wv = wview(slot, 0, nk, gcols)
            for oo in range(gcols // 128):
                o = og * (gcols // 128) + oo
                for (t0, w) in tiles:
                    bank = banks[bi % len(banks)]
                    bi += 1
                    for ci in range(nk):
                        P.add("pe", lambda e, ci=ci, oo=oo, t0=t0, w=w, bank=bank, wv=wv: e.matmul(
                            ps[bank][:, 0:w], lhsT=wv[:, ci, oo * 128:(oo + 1) * 128],
                            rhs=src[:, ci, t0 - src_t0:t0 - src_t0 + w],
                            start=(ci == 0), stop=(ci == nk - 1)),
                              r=[stok, src_tokf(ci)], w=[("ps", bank)])
                    add_resid(o, t0, w, bank)

    CT2 = conv_tiles(2)

    def sconv_layer(li, j):
        rmsnorm("norm_mix", (li,))
        w_in = W["sconv_w_in"][j]
        P.dma("sp", pastb[:, 0:8, :, :], sconv_past_in[j], w=["pastb"])
        yv = WK
        grp = 0
        for i in range(8):
            slot, stok = ws_ring.next()
            for q in range(3):
                load_w(w_in, list(range(8)), q * D + i * 128, 128, slot, q * 1024, stok)
            wv = [wview(slot, q * 1024, 8, 128) for q in range(3)]
            cw = [pcol("sconv_cw", j, i, k) for k in range(3)]
            tiles = [(s, n, False) for (s, n) in CT2] + [(NP, NS, True)]
            for (s, n, is_s) in tiles:
                b0 = 3 * (grp % 2)
                grp += 1
                if is_s:
                    c0, wd = PADL + NP, NS
                else:
                    c0, wd = PADL + s - 2, n + 2
                for q in range(3):
                    for c in range(8):
                        P.add("pe", lambda e, q=q, c=c, c0=c0, wd=wd, b0=b0, wv=wv: e.matmul(
                            ps[b0 + q][:, 0:wd], lhsT=wv[q][:, c, :], rhs=hT[:, c, c0:c0 + wd],
                            start=(c == 0), stop=(c == 7)),
                              r=[stok, ("h", c), "hpad"], w=[("ps", b0 + q)])
                go, gi, va = ps[b0], ps[b0 + 1], ps[b0 + 2]
                vs, vtok = tmp_ring.next()
                P.add("act", lambda e, vs=vs, va=va, wd=wd: e.copy(out=vs[:, 0:wd], in_=va[:, 0:wd]),
                      r=[("ps", b0 + 2)], w=[vtok])
                u, utok = tmp_ring.next()
                P.add("dve", lambda e, u=u, gi=gi, vs=vs, wd=wd: e.tensor_tensor(
                    out=u[:, 0:wd], in0=gi[:, 0:wd], in1=vs[:, 0:wd], op=ALU.mult),
                      r=[("ps", b0 + 1), vtok], w=[utok])
                cc, ctok = tmp_ring.next()
                if not is_s:
                    P.add("dve", lambda e, cc=cc, u=u, n=n, cw=cw: e.tensor_scalar(
                        out=cc[:, 0:n], in0=u[:, 2:n + 2], scalar1=cw[2], scalar2=None, op0=ALU.mult),
                          r=[utok, "prm"], w=[ctok])
                    P.add("dve", lambda e, cc=cc, u=u, n=n, cw=cw: e.scalar_tensor_tensor(
                        out=cc[:, 0:n], in0=u[:, 1:n + 1], scalar=cw[1], in1=cc[:, 0:n], op0=ALU.mult, op1=ALU.add),
                          r=[utok, ctok, "prm"], w=[ctok])
                    P.add("dve", lambda e, cc=cc, u=u, n=n, cw=cw: e.scalar_tensor_tensor(
                        out=cc[:, 0:n], in0=u[:, 0:n], scalar=cw[0], in1=cc[:, 0:n], op0=ALU.mult, op1=ALU.add),
                          r=[utok, ctok, "prm"], w=[ctok])
                    P.add("dve", lambda e, cc=cc, go=go, n=n, i=i, s=s: e.tensor_tensor(
                        out=yv[:, i, s:s + n], in0=go[:, 2:n + 2], in1=cc[:, 0:n], op=ALU.mult),
                          r=[("ps", b0), ctok], w=[("wk", i)])
                    if s + n == NP:
                        P.add("act", lambda e, u=u, n=n, i=i: e.copy(out=stage[:, i, 0:2], in_=u[:, n:n + 2]),
                              r=[utok], w=["stage"])
                else:
                    p0 = pastb[:, i, :, 0]
                    p1 = pastb[:, i, :, 1]
                    P.add("dve", lambda e, cc=cc, u=u, cw=cw: e.tensor_scalar(
                        out=cc[:, 0:NS], in0=u[:, 0:NS], scalar1=cw[2], scalar2=None, op0=ALU.mult),
                          r=[utok, "prm"], w=[ctok])
                    P.add("dve", lambda e, cc=cc, p1=p1, cw=cw: e.scalar_tensor_tensor(
                        out=cc[:, 0:NS], in0=p1, scalar=cw[1], in1=cc[:, 0:NS], op0=ALU.mult, op1=ALU.add),
                          r=["pastb", ctok, "prm"], w=[ctok])
                    P.add("dve", lambda e, cc=cc, p0=p0, cw=cw: e.scalar_tensor_tensor(
                        out=cc[:, 0:NS], in0=p0, scalar=cw[0], in1=cc[:, 0:NS], op0=ALU.mult, op1=ALU.add),
                          r=["pastb", ctok, "prm"], w=[ctok])
                    P.add("dve", lambda e, cc=cc, go=go, i=i: e.tensor_tensor(
                        out=yv[:, i, NP:NP + NS], in0=go[:, 0:NS], in1=cc[:, 0:NS], op=ALU.mult),
                          r=[("ps", b0), ctok], w=[("wk", i)])
                    sv = stage[:, i, 2:2 + 2 * NS].rearrange("p (b k) -> p b k", k=2)
                    P.add("act", lambda e, sv=sv, u=u: e.copy(out=sv[:, :, 1], in_=u[:, 0:NS]),
                          r=[utok], w=["stage"])
                    P.add("pool", lambda e, sv=sv, p1=p1: e.tensor_copy(out=sv[:, :, 0], in_=p1),
                          r=["pastb"], w=["stage"])
        P.dma("sp", sconv_out[j], stage[:, 0:8, :], r=["stage"])
        out_proj(W["sconv_w_out"][j], list(range(8)), yv, lambda ci: ("wk", ci), [6, 7])

    def ffn_layer(li):
        rmsnorm("norm_ffn", (li,))
        w_up = W["ffn_w_up"][li]
        P.dma("sp", pastb[:, :, :, :], ffn_past_in[li], w=["pastb"])
        act = WK
        grp = 0
        for chunks in (list(range(0, 8)), list(range(8, 15)), list(range(15, 22))):
            for ii, i in enumerate(chunks):
                slot, stok = ws_ring.next()
                load_w(w_up, list(range(8)), i * 128, 128, slot, 0, stok)
                load_w(w_up, list(range(8)), DFF + i * 128, 128, slot, 1024, stok)
                wv = [wview(slot, 0, 8, 128), wview(slot, 1024, 8, 128)]
                fidx = [i, NFC + i]
                cw = [[pcol("ffn_cw", li, f, k) for k in range(3)] for f in fidx]
                cb = [pcol("ffn_cb", li, f) for f in fidx]
                tiles = [(s, n, False) for (s, n) in CT2] + [(NP, NS, True)]
                for (s, n, is_s) in tiles:
                    b0 = 2 * (grp % 2)
                    grp += 1
                    if is_s:
                        c0, wd = PADL + NP, NS
                    else:
                        c0, wd = PADL + s - 2, n + 2
                    for q in range(2):
                        for c in range(8):
                            P.add("pe", lambda e, q=q, c=c, c0=c0, wd=wd, b0=b0, wv=wv: e.matmul(
                                ps[b0 + q][:, 0:wd], lhsT=wv[q][:, c, :], rhs=hT[:, c, c0:c0 + wd],
                                start=(c == 0), stop=(c == 7)),
                                  r=[stok, ("h", c), "hpad"], w=[("ps", b0 + q)])
                    cts = []
                    for q in range(2):
                        pq = ps[b0 + q]
                        cc, ctok = tmp_ring.next()
                        cts.append((cc, ctok))
                        f = fidx[q]
                        if not is_s:
                            P.add("act", lambda e, cc=cc, pq=pq, n=n, q=q, cw=cw, cb=cb: e.activation(
                                out=cc[:, 0:n], in_=pq[:, 2:n + 2], func=AF.Identity, bias=cb[q], scale=cw[q][2]),
                                  r=[("ps", b0 + q), "prm"], w=[ctok])
                            P.add("dve", lambda e, cc=cc, pq=pq, n=n, q=q, cw=cw: e.scalar_tensor_tensor(
                                out=cc[:, 0:n], in0=pq[:, 1:n + 1], scalar=cw[q][1], in1=cc[:, 0:n],
                                op0=ALU.mult, op1=ALU.add), r=[("ps", b0 + q), ctok, "prm"], w=[ctok])
                            P.add("dve", lambda e, cc=cc, pq=pq, n=n, q=q, cw=cw: e.scalar_tensor_tensor(
                                out=cc[:, 0:n], in0=pq[:, 0:n], scalar=cw[q][0], in1=cc[:, 0:n],
                                op0=ALU.mult, op1=ALU.add), r=[("ps", b0 + q), ctok, "prm"], w=[ctok])
                            if s + n == NP:
                                P.add("act", lambda e, pq=pq, n=n, f=f: e.copy(out=stage[:, f, 0:2], in_=pq[:, n:n + 2]),
                                      r=[("ps", b0 + q)], w=["stage"])
                        else:
                            p0 = pastb[:, f, :, 0]
                            p1 = pastb[:, f, :, 1]
                            P.add("act", lambda e, cc=cc, pq=pq, q=q, cw=cw, cb=cb: e.activation(
                                out=cc[:, 0:NS], in_=pq[:, 0:NS], func=AF.Identity, bias=cb[q], scale=cw[q][2]),
                                  r=[("ps", b0 + q), "prm"], w=[ctok])
                            P.add("dve", lambda e, cc=cc, p1=p1, q=q, cw=cw: e.scalar_tensor_tensor(
                                out=cc[:, 0:NS], in0=p1, scalar=cw[q][1], in1=cc[:, 0:NS],
                                op0=ALU.mult, op1=ALU.add), r=["pastb", ctok, "prm"], w=[ctok])
                            P.add("dve", lambda e, cc=cc, p0=p0, q=q, cw=cw: e.scalar_tensor_tensor(
                                out=cc[:, 0:NS], in0=p0, scalar=cw[q][0], in1=cc[:, 0:NS],
                                op0=ALU.mult, op1=ALU.add), r=["pastb", ctok, "prm"], w=[ctok])
                            sv = stage[:, f, 2:2 + 2 * NS].rearrange("p (b k) -> p b k", k=2)
                            P.add("act", lambda e, sv=sv, pq=pq: e.copy(out=sv[:, :, 1], in_=pq[:, 0:NS]),
                                  r=[("ps", b0 + q)], w=["stage"])
                            P.add("pool", lambda e, sv=sv, p1=p1: e.tensor_copy(out=sv[:, :, 0], in_=p1),
                                  r=["pastb"], w=["stage"])
                    (ca, catok), (cg, cgtok) = cts
                    nn = NS if is_s else n
                    P.add("act", lambda e, cg=cg, nn=nn: e.activation(out=cg[:, 0:nn], in_=cg[:, 0:nn], func=AF.Silu),
                          r=[cgtok], w=[cgtok])
                    P.add("dve", lambda e, ca=ca, cg=cg, nn=nn, ii=ii, s=s: e.tensor_tensor(
                        out=act[:, ii, s:s + nn], in0=ca[:, 0:nn], in1=cg[:, 0:nn], op=ALU.mult),
                          r=[catok, cgtok], w=[("wk", ii)])
            out_proj(W["ffn_w_down"][li], chunks, act, lambda ci: ("wk", ci), [4, 5])
        P.dma("sp", ffn_out[li], stage[:, :, :], r=["stage"])


    CT3 = conv_tiles(3)

    def ssd_layer(li):
        rmsnorm("norm_mix", (li,))
        w_in = W["ssd_w_in"][0]
        prm_dtb, _, _ = prm_ap("ssd_dtb")
        prm_alog, _, _ = prm_ap("ssd_alog")
        pst = pastb[:, :, :, :].rearrange("p c b k -> p (c b k)")[:, 0:24 * NS * 3] \
            .rearrange("p (c b k) -> p c b k", c=24, b=NS)
        P.dma("sp", pst, ssmc_past_in, w=["pastb"])
        stg = stage[:, :, :].rearrange("p c k -> p (c k)")[:, 0:24 * 51].rearrange("p (c k) -> p c k", c=24)
        slot, stok = ws_ring.next()
        load_w(w_in, list(range(8)), 5120, 32, slot, 0, stok)
        wdt = wview(slot, 0, 8, 32)
        for tc in range(17):
            t0 = tc * 128
            w = 128 if tc < 16 else NS
            bank, col = (0, tc * 32) if tc < 16 else (1, 0)
            for c in range(8):
                mm(ps[bank][0:w, col:col + 32], hT[:, c, PADL + t0:PADL + t0 + w], wdt[:, c, :], c == 0, c == 7,
                   r=[stok, ("h", c)], w=[("ps", bank)])
        dtf = dt_tok[:, :, :].rearrange("p a b -> p (a b)")
        dAf = dtA_tok[:, :, :].rearrange("p a b -> p (a b)")
        vtt(dt_tok[:, 0:16, :], ps[0].rearrange("p (a b) -> p a b", b=32),
            prm_dtb.unsqueeze(1).broadcast_to([128, 16, 32]), ALU.add, r=[("ps", 0), "prm"], w=["dt_tok"])
        vtt(dt_tok[0:NS, 16, :], ps[1][0:NS, 0:32], prm_dtb[0:NS, :], ALU.add, r=[("ps", 1), "prm"], w=["dt_tok"])
        actf(dAf, dtf, AF.Abs, r=["dt_tok"], w=["dtA_tok"])
        actf(dAf, dAf, AF.Exp, r=["dtA_tok"], w=["dtA_tok"], scale=-1.0)
        actf(dAf, dAf, AF.Ln, r=["dtA_tok", "onet"], w=["dtA_tok"], bias=onet[:], scale=1.0)
        vstt(dtf, dtf, 0.0, dAf, ALU.max, ALU.add, r=["dt_tok", "dtA_tok"], w=["dt_tok"])
        actf(a_bc[:], prm_alog, AF.Exp, r=["prm"], w=["a_bc"])
        vtt(dtA_tok[:, :, :], dt_tok[:, :, :], a_bc[:].unsqueeze(1).broadcast_to([128, 17, 32]), ALU.mult,
            r=["dt_tok", "a_bc"], w=["dtA_tok"])
        vts(dAf, dAf, -1.0, None, ALU.mult, None, r=["dtA_tok"], w=["dtA_tok"])

        Dg = rstd[:, 0:1024].rearrange("p (h l) -> p h l", h=8)
        rhs2 = rstd[:, 1024:2048].rearrange("p (h l) -> p h l", h=8)
        xg_tok = [("wk", i) for i in range(4)]
        for g in range(4):
            cols = [2048 + 512 * g + 128 * i for i in range(4)] + [4096 + 128 * g, 4608 + 128 * g]
            bi = 0
            for q, col in enumerate(cols):
                cc = (col - 2048) // 128
                slot, stok = ws_ring.next()
                load_w(w_in, list(range(8)), col, 128, slot, 0, stok)
                wv = wview(slot, 0, 8, 128)
                cw = [pcol("ssd_cw", cc, k) for k in range(4)]
                cb = pcol("ssd_cb", cc)
                tiles = [(s_, n_, False) for (s_, n_) in CT3] + [(NP, NS, True)]
                for (s_, n_, is_s) in tiles:
                    bank = bi % 4
                    bi += 1
                    if is_s:
                        c0, wd = PADL + NP, NS
                    else:
                        c0, wd = PADL + s_ - 3, n_ + 3
                    for c in range(8):
                        mm(ps[bank][:, 0:wd], wv[:, c, :], hT[:, c, c0:c0 + wd], c == 0, c == 7,
                           r=[stok, ("h", c), "hpad"], w=[("ps", bank)])
                    pq = ps[bank]
                    ct, ctok = tmp_ring.next()
                    if not is_s:
                        actf(ct[:, 0:n_], pq[:, 3:n_ + 3], AF.Identity, r=[("ps", bank), "prm"], w=[ctok],
                             bias=cb, scale=cw[3])
                        for k in (2, 1, 0):
                            vstt(ct[:, 0:n_], pq[:, k:k + n_], cw[k], ct[:, 0:n_], ALU.mult, ALU.add,
                                 r=[("ps", bank), ctok, "prm"], w=[ctok])
                        actf(WK[:, q, s_:s_ + n_], ct[:, 0:n_], AF.Silu, r=[ctok], w=[("wk", q)])
                        if s_ + n_ == NP:
                            acp(stg[:, cc, 0:3], pq[:, n_:n_ + 3], r=[("ps", bank)], w=["stage"])
                    else:
                        actf(ct[:, 0:NS], pq[:, 0:NS], AF.Identity, r=[("ps", bank), "prm"], w=[ctok],
                             bias=cb, scale=cw[3])
                        for k in (2, 1, 0):
                            vstt(ct[:, 0:NS], pst[:, cc, :, k], cw[k], ct[:, 0:NS], ALU.mult, ALU.add,
                                 r=["pastb", ctok, "prm"], w=[ctok])
                        actf(WK[:, q, NP:NP + NS], ct[:, 0:NS], AF.Silu, r=[ctok], w=[("wk", q)])
                        sv = stg[:, cc, 3:3 + 3 * NS].rearrange("p (b k) -> p b k", k=3)
                        acp(sv[:, :, 2], pq[:, 0:NS], r=[("ps", bank)], w=["stage"])
                        vcp(sv[:, :, 0], pst[:, cc, :, 1], r=["pastb"], w=["stage"], eng="pool")
                        vcp(sv[:, :, 1], pst[:, cc, :, 2], r=["pastb"], w=["stage"], eng="pool")
            BT = WK[:, 4, :]
            CTm = WK[:, 5, :]
            P.add("pool", lambda e: e.memset(prevT[:], 0.0), w=["prevT"])
            P.add("pool", lambda e: e.memset(prevT_bf[:], 0.0), w=["prevT_bf"])
            for tc in range(16):
                t0 = tc * 128
                psb = ps[0].bitcast(BF16)
                for i in range(4):
                    mm(ps[0][:, i * 128:(i + 1) * 128], WK[:, i, t0:t0 + 128], ident_b[:], True, True,
                       r=[("wk", i), "ident_b"], w=[("ps", 0)])
                mm(ps[1][:, 256:384], BT[:, t0:t0 + 128], ident_b[:], True, True, r=[("wk", 4), "ident_b"],
                   w=[("ps", 1)])
                xdt, xdtok = tmp_ring.next()
                xdtv = xdt[:, :].bitcast(BF16)[:, 0:512].rearrange("p (h d) -> p h d", h=8)
                xdtdv = xdt[:, :].bitcast(BF16)[:, 512:1024].rearrange("p (h d) -> p h d", h=8)
                vtt(xdtv, ps[0].rearrange("p (h d) -> p h d", h=8),
                    dt_tok[:, tc, 8 * g:8 * g + 8].unsqueeze(2).broadcast_to([128, 8, 64]), ALU.mult,
                    r=[("ps", 0), "dt_tok"], w=[xdtok])
                bt, bttok = tmp_ring.next()
                btv = bt[:, :].bitcast(BF16)[:, 0:128]
                cbv = bt[:, :].bitcast(BF16)[:, 128:256]
                acp(btv, ps[1][:, 256:384], r=[("ps", 1)], w=[bttok])
                dA = dtA_tok[:, tc, 8 * g:8 * g + 8]
                mm(ps[1][:, 0:8], tri_f, dA, True, True, r=["cmat", "dtA_tok"], w=[("ps", 1)])
                mm(ps[1][:, 8:16], ones_f, dA, True, True, r=["cmat", "dtA_tok"], w=[("ps", 1)])
                acp(smt[:, 0:16], ps[1][:, 0:16], r=[("ps", 1)], w=["smt"])
                actf(smt[:, 16:32], smt[:, 0:16], AF.Exp, r=["smt"], w=["smt"])
                vtt(smt[:, 32:40], smt[:, 8:16], smt[:, 0:8], ALU.subtract, r=["smt"], w=["smt"])
                actf(smt[:, 40:48], smt[:, 32:40], AF.Exp, r=["smt"], w=["smt"])
                acum = smt[:, 0:8]
                ea = smt[:, 16:24]
                cd = smt[:, 24:32]
                dte = smt[:, 40:48]
                vtt(Dg, ident_f.unsqueeze(1).broadcast_to([128, 8, 128]),
                    acum.unsqueeze(2).broadcast_to([128, 8, 128]), ALU.mult, r=["cmat", "smt"], w=["Dg"])
                vtt(rhs2, maskneg_f.unsqueeze(1).broadcast_to([128, 8, 128]),
                    acum.unsqueeze(2).broadcast_to([128, 8, 128]), ALU.subtract, r=["cmat", "smt"], w=["rhs2"])
                for hb in range(2):
                    bk = 2 + hb
                    mm(ps[bk][:, :], ones_f, Dg[:, 4 * hb:4 * hb + 4, :].rearrange("p h l -> p (h l)"), True, False,
                       r=["cmat", "Dg"], w=[("ps", bk)])
                    mm(ps[bk][:, :], ident_f, rhs2[:, 4 * hb:4 * hb + 4, :].rearrange("p h l -> p (h l)"), False, True,
                       r=["cmat", "rhs2"], w=[("ps", bk)])
                lt, lttok = tmp_ring.next()
                ltv = lt[:, :].bitcast(BF16)
                for hb in range(2):
                    actf(ltv[:, 512 * hb:512 * hb + 512], ps[2 + hb][:, :], AF.Exp, r=[("ps", 2 + hb)], w=[lttok])
                mm(ps[1][:, 128:256], BT[:, t0:t0 + 128], CTm[:, t0:t0 + 128], True, True,
                   r=[("wk", 4), ("wk", 5)], w=[("ps", 1)])
                acp(cbv, ps[1][:, 128:256], r=[("ps", 1)], w=[bttok])
                mt, mttok = tmp_ring.next()
                mtv = mt[:, :].bitcast(BF16).rearrange("p (h l) -> p h l", h=8)
                vtt(mtv, ltv.rearrange("p (h l) -> p h l", h=8), cbv.unsqueeze(1).broadcast_to([128, 8, 128]),
                    ALU.mult, r=[lttok, bttok], w=[mttok])
                for h in range(8):
                    mm(ps[4][:, h * 64:(h + 1) * 64], mtv[:, h, :], xdtv[:, h, :], True, True,
                       r=[mttok, xdtok], w=[("ps", 4)])
                mm(ps[5][:, :], CTm[:, t0:t0 + 128], prevT_bf[:], True, True, r=[("wk", 5), "prevT_bf"],
                   w=[("ps", 5)])
                yd, ydtok = tmp_ring.next()
                acp(yd[:, :], ps[4][:, :], r=[("ps", 4)], w=[ydtok])
                yt_, yttok = tmp_ring.next()
                vtt(yt_.rearrange("p (h d) -> p h d", h=8), ps[5].rearrange("p (h d) -> p h d", h=8),
                    ea.unsqueeze(2).broadcast_to([128, 8, 64]), ALU.mult, r=[("ps", 5), "smt"], w=[yttok])
                vtt(yt_[:, :], yt_[:, :], yd[:, :], ALU.add, r=[yttok, ydtok], w=[yttok])
                vtt(xdtdv, xdtv, dte.unsqueeze(2).broadcast_to([128, 8, 64]), ALU.mult, r=[xdtok, "smt"], w=[xdtok])
                mm(ps[6][:, :], btv, xdtdv.rearrange("p h d -> p (h d)"), True, True, r=[bttok, xdtok], w=[("ps", 6)])
                vtt(prevT[:].rearrange("p (h d) -> p h d", h=8), prevT[:].rearrange("p (h d) -> p h d", h=8),
                    cd.unsqueeze(2).broadcast_to([128, 8, 64]), ALU.mult, r=["prevT", "smt"], w=["prevT"])
                vtt(prevT[:], prevT[:], ps[6][:, :], ALU.add, r=["prevT", ("ps", 6)], w=["prevT"])
                acp(prevT_bf[:], prevT[:], r=["prevT"], w=["prevT_bf"])
                for i in range(4):
                    mm(ps[7][:, i * 128:(i + 1) * 128], yt_[:, i * 128:(i + 1) * 128], ident_f, True, True,
                       r=[yttok, "cmat"], w=[("ps", 7)])
                for i in range(4):
                    vstt(WK[:, i, t0:t0 + 128], WK[:, i, t0:t0 + 128], pcol("ssd_D", 4 * g + i),
                         ps[7][:, i * 128:(i + 1) * 128], ALU.mult, ALU.add,
                         r=[("wk", i), ("ps", 7), "prm"], w=[("wk", i)])
            P.dma("sp", ssm_p_out[g], prevT[:], r=["prevT"])
            dexps = []
            for which, src in ((0, dt_tok), (1, dtA_tok)):
                dx, dxtok = tmp_ring.next()
                dexps.append((dx, dxtok))
                for i in range(4):
                    vcp(dx[0:NS, i * 128:(i + 1) * 128].rearrange("p (a b) -> p a b", a=2),
                        src[0:NS, 16, 8 * g + 2 * i:8 * g + 2 * i + 2].unsqueeze(2).broadcast_to([NS, 2, 64]),
                        r=["dt_tok", "dtA_tok"], w=[dxtok])
            for i in range(4):
                for which in range(2):
                    dx, dxtok = dexps[which]
                    mm(ps[0][:, which * 64 + i * 16:which * 64 + i * 16 + 16], dx[0:NS, i * 128:(i + 1) * 128],
                       ident_f[0:NS, 0:NS], True, True, r=[dxtok, "cmat"], w=[("ps", 0)])
            dtx = sx_t[:, 0, :, :]
            dec = sx_t[:, 1, :, :]
            xdx = sx_t[:, 2, :, :]
            acp(dtx, ps[0][:, 0:64].rearrange("p (i b) -> p i b", i=4), r=[("ps", 0)], w=["sx"])
            actf(dec, ps[0][:, 64:128].rearrange("p (i b) -> p i b", i=4), AF.Exp, r=[("ps", 0)], w=["sx"])
            vtt(xdx, WK[:, 0:4, NP:NP + NS], dtx, ALU.mult, r=xg_tok + ["sx"], w=["sx"])
            P.add("pool", lambda e: e.memset(ys_t[:], 0.0), w=["ys"])
            for hb in range(2):
                bd, bdtok = tmp_ring.next()
                bdv = bd[:, :].bitcast(BF16).rearrange("p (b n) -> p b n", b=8)
                cdg, cdtok = tmp_ring.next()
                cdv = cdg[:, :].bitcast(BF16).rearrange("p (b n) -> p b n", b=8)
                vtt(bdv, ident_b[:].unsqueeze(1).broadcast_to([128, 8, 128]),
                    BT[:, NP + 8 * hb:NP + 8 * hb + 8].unsqueeze(2).broadcast_to([128, 8, 128]), ALU.mult,
                    r=["ident_b", ("wk", 4)], w=[bdtok])
                vtt(cdv, ident_b[:].unsqueeze(1).broadcast_to([128, 8, 128]),
                    CTm[:, NP + 8 * hb:NP + 8 * hb + 8].unsqueeze(2).broadcast_to([128, 8, 128]), ALU.mult,
                    r=["ident_b", ("wk", 5)], w=[cdtok])
                for k2 in range(2):
                    mm(ps[2 + k2][:, :], ones_b[:], bd[:, :].bitcast(BF16)[:, 512 * k2:512 * k2 + 512], True, True,
                       r=["ones_b", bdtok], w=[("ps", 2 + k2)])
                    mm(ps[4 + k2][:, :], ones_b[:], cdg[:, :].bitcast(BF16)[:, 512 * k2:512 * k2 + 512], True, True,
                       r=["ones_b", cdtok], w=[("ps", 4 + k2)])
                for bl in range(8):
                    b = 8 * hb + bl
                    BBv = ps[2 + bl // 4][:, (bl % 4) * 128:(bl % 4) * 128 + 128]
                    CCv = ps[4 + bl // 4][:, (bl % 4) * 128:(bl % 4) * 128 + 128]
                    sin_, sintok = sst_ring.next()
                    sout, souttok = snew_ring.next()
                    r0 = b * 2048 + g * 512
                    P.dma("sp", sin_[:], ssm_state_in[r0:r0 + 512, :].rearrange("(i p) n -> p i n", p=128), w=[sintok])
                    for i in range(4):
                        tq, tqtok = tmp_ring.next()
                        vts(tq[:, 0:128], BBv, xdx[:, i, b:b + 1], None, ALU.mult, None,
                            r=[("ps", 2 + bl // 4), "sx"], w=[tqtok])
                        vstt(sout[:, i, :], sin_[:, i, :], dec[:, i, b:b + 1], tq[:, 0:128], ALU.mult, ALU.add,
                             r=[sintok, "sx", tqtok], w=[souttok])
                        vstt(tq[:, 128:256], sout[:, i, :], 1.0, CCv, ALU.mult, ALU.mult,
                             r=[souttok, ("ps", 4 + bl // 4)], w=[tqtok, "ys"], accum_out=ys_t[:, i, b:b + 1])
                    P.dma("sp", ssm_s_out[r0:r0 + 512, :].rearrange("(i p) n -> p i n", p=128), sout[:], r=[souttok])
            for i in range(4):
                vstt(WK[:, i, NP:NP + NS], WK[:, i, NP:NP + NS], pcol("ssd_D", 4 * g + i), ys_t[:, i, :],
                     ALU.mult, ALU.add, r=[("wk", i), "ys", "prm"], w=[("wk", i)])
            slot, stok = ws_ring.next()
            load_w(w_in, list(range(8)), 512 * g, 512, slot, 0, stok)
            wz = wview(slot, 0, 8, 512)
            for (t0, w) in TT:
                for i in range(4):
                    bank = i % 2
                    for c in range(8):
                        mm(ps[bank][:, 0:w], wz[:, c, i * 128:(i + 1) * 128], hT[:, c, PADL + t0:PADL + t0 + w],
                           c == 0, c == 7, r=[stok, ("h", c)], w=[("ps", bank)])
                    sz, sztok = tmp_ring.next()
                    actf(sz[:, 0:w], ps[bank][:, 0:w], AF.Silu, r=[("ps", bank)], w=[sztok])
                    vtt(WK[:, i, t0:t0 + w], WK[:, i, t0:t0 + w], sz[:, 0:w], ALU.mult, r=[("wk", i), sztok],
                        w=[("wk", i)])
                    sq, sqtok = tmp_ring.next()
                    sqv = sq[:, :].bitcast(BF16)
                    actf(sqv[:, 0:w], WK[:, i, t0:t0 + w], AF.Square, r=[("wk", i)], w=[sqtok])
                    mm(ps[2][:, 0:w], ones_g[:], sqv[:, 0:w], i == 0, i == 3, r=["ones_g", sqtok], w=[("ps", 2)])
                rs, rstok = tmp_ring.next()
                actf(rs[:, 0:w], ps[2][:, 0:w], AF.Sqrt, r=[("ps", 2), "epst"], w=[rstok], bias=epst[:], scale=1.0)
                P.add("dve", lambda e, rs=rs, w=w: e.reciprocal(out=rs[:, 0:w], in_=rs[:, 0:w]), r=[rstok], w=[rstok])
                for i in range(4):
                    vstt(WK[:, i, t0:t0 + w], WK[:, i, t0:t0 + w], pcol("ssd_nw", 4 * g + i), rs[:, 0:w],
                         ALU.mult, ALU.mult, r=[("wk", i), rstok, "prm"], w=[("wk", i)])
            if True:
                out_proj(W["ssd_w_out"][0], list(range(4 * g, 4 * g + 4)), WK, lambda ci: ("wk", ci), [6, 7])
        P.dma("sp", ssmc_out, stg, r=["stage"])


    def barrier():
        P.add("pe", lambda e: e.matmul(ps[7][0:1, 0:1], lhsT=ident_b[:, 0:1], rhs=ident_b[:, 0:1], start=True, stop=True),
              r=["ident_b"], w=[("ps", 7), ("bar", "pe")])
        P.add("act", lambda e: e.copy(out=dmy["act"][:, 0:1], in_=dmy["act"][:, 1:2]), w=[("bar", "act")])
        P.add("dve", lambda e: e.tensor_copy(out=dmy["dve"][:, 0:1], in_=dmy["dve"][:, 1:2]), w=[("bar", "dve")])
        P.add("pool", lambda e: e.tensor_copy(out=dmy["pool"][:, 0:1], in_=dmy["pool"][:, 1:2]), w=[("bar", "pool")])
        P.dma("sp", dmy["sp"][:, 0:1], dmy["sp"][:, 1:2], w=[("bar", "sp")])

    def attn_layer(li):
        rmsnorm("norm_mix", (li,))
        barrier()
        w_qkv = W["attn_w_qkv"][0]
        WKf = WK[:, :, :].rearrange("p c t -> p (c t)")

        def reg(o, n):
            return WKf[:, o:o + n]
        QT = reg(0, T)
        KT = reg(T, T)
        VT = reg(2 * T, T)
        OT = reg(3 * T, T)
        VA = reg(4 * T, 16 * 2 * 65).rearrange("p (t h e) -> p t h e", t=16, h=2)
        NMT = reg(4 * T + 2080, T)
        CM = reg(5 * T + 2080, 512).rearrange("p (k q) -> p k q", k=2)
        o0 = 5 * T + 2080 + 512
        acm = reg(o0, 1024).bitcast(F32).rearrange("p (a b) -> p a b", a=4)
        Pm_b = reg(o0 + 1024, 128)
        kmf = reg(o0 + 1152, 16).bitcast(F32)
        kmb2 = reg(o0 + 1168, 16)
        ptf = reg(o0 + 1184, 512).bitcast(F32)
        idx_i = reg(o0 + 1696, 512).bitcast(I32)
        pti = reg(o0 + 2208, 512).bitcast(I32)
        assert o0 + 2984 <= 8 * T
        ropes = rstd[:, :].bitcast(BF16).rearrange("p (a t) -> p a t", a=2)
        cosT, sinT = ropes[:, 0, :], ropes[:, 1, :]
        SS = sst[0][:, :, :].rearrange("p a b -> p (a b)")
        QS = SS[:, 0:128].rearrange("p (c b) -> p c b", c=8)
        KS = SS[:, 128:256].rearrange("p (c b) -> p c b", c=8)
        VS = SS[:, 256:384].rearrange("p (c b) -> p c b", c=8)
        OS = SS[:, 384:512].rearrange("p (c b) -> p c b", c=8)
        OSb = sst[1][:, 0, :].bitcast(BF16)[:, 0:128].rearrange("p (c b) -> p c b", c=8)

        for a_ in range(2):
            for h_ in range(2):
                P.dma("pool", ropes[:, a_, h_ * 1032:(h_ + 1) * 1032], rope_in[:, a_, h_ * 1032:(h_ + 1) * 1032],
                      w=["rope"])
        P.dma("pool", CM, cm_in, w=["cm"])
        P.dma("sp", acm, acm_in, w=["acm"])
        P.dma("sp", pti, pt_in.partition_broadcast(128), w=["pti"])
        vcp(Pm_b, acm[:, 0, :], r=["acm"], w=["pmb"])
        vcp(ptf, pti, r=["pti"], w=["ptf"])
        vts(ptf, ptf, 128.0, acm[:, 3, 0:1], ALU.mult, ALU.add, r=["ptf", "acm"], w=["ptf"])
        vcp(idx_i, ptf, r=["ptf"], w=["idx"])
        P.add("pool", lambda e: e.memset(VA[:, :, :, 64:65], 1.0), w=["va"])
        SEL = []
        for hl_, tns in enumerate((dt_tok, dtA_tok)):
            sv_ = tns[:, :, :].rearrange("p a b -> p (a b)")[:, 0:512].bitcast(BF16).rearrange("p (r m) -> p r m", r=8)
            SEL.append(sv_)
            vcp(sv_[0:16, :, :], ident_b[0:16, 8 * hl_:8 * hl_ + 8].unsqueeze(2).broadcast_to([16, 8, 128]),
                r=["ident_b"], w=["sel"])

        import os
        for g in range(8 if os.environ.get('SK_GRP') is None else 0):
            for which, (dst, dtok) in enumerate(((QT, "qt"), (KT, "kt"), (VT, "vt"))):
                if os.environ.get('SK_W%d' % which) is not None:
                    continue
                slot, stok = ws_ring.next()
                load_w(w_qkv, list(range(8)), which * D + g * 128, 128, slot, 0, stok)
                wv = wview(slot, 0, 8, 128)
                for ti, (t0, w) in enumerate(TT):
                    bank = ti % 2
                    for c in range(8):
                        mm(ps[bank][:, 0:w], wv[:, c, :], hT[:, c, PADL + t0:PADL + t0 + w], c == 0, c == 7,
                           r=[stok, ("h", c)], w=[("ps", bank)])
                    pq = ps[bank]
                    if which == 2:
                        vf, vftok = tmp_ring.next()
                        acp(vf[:, 0:w], pq[:, 0:w], r=[("ps", bank)], w=[vftok])
                        P.dma("sp", vT_out[:, g, t0:t0 + w], vf[:, 0:w], r=[vftok])
                        vcp(VT[:, t0:t0 + w], vf[:, 0:w], r=[vftok], w=["vt"], eng="pool")
                        if t0 == NP:
                            vcp(VS[:, g, :], vf[:, 0:NS], r=[vftok], w=["ss"], eng="pool")
                        continue
                    qs, qstok = tmp_ring.next()
                    qsb = qs[:, :].bitcast(BF16)
                    acp(qsb[:, 0:w], pq[:, 0:w], r=[("ps", bank)], w=[qstok])
                    ROT = os.environ.get('SK_ROT') is None
                    if ROT:
                        mm(ps[2 + bank][:, 0:w], Pm_b, qsb[:, 0:w], True, True, r=["pmb", qstok], w=[("ps", 2 + bank)])
                    t1, t1tok = tmp_ring.next()
                    acp(t1[:, 0:w], pq[:, 0:w], r=[("ps", bank)], w=[t1tok])
                    vtt(t1[:, 0:w], t1[:, 0:w], cosT[:, t0:t0 + w], ALU.mult, r=[t1tok, "rope"], w=[t1tok])
                    t2, t2tok = tmp_ring.next()
                    if ROT:
                        acp(t2[:, 0:w], ps[2 + bank][:, 0:w], r=[("ps", 2 + bank)], w=[t2tok])
                        vtt(t2[:, 0:w], t2[:, 0:w], sinT[:, t0:t0 + w], ALU.mult, r=[t2tok, "rope"], w=[t2tok])
                        vtt(t1[:, 0:w], t1[:, 0:w], t2[:, 0:w], ALU.add, r=[t1tok, t2tok], w=[t1tok])
                    acp(dst[:, t0:t0 + w], t1[:, 0:w], r=[t1tok], w=[dtok])
                    if which == 1:
                        P.dma("sp", kT_out[:, g, t0:t0 + w], t1[:, 0:w], r=[t1tok])
                        if t0 < NP and os.environ.get('SK_RED') is None:
                            P.add("dve", lambda e, ti=ti, t1=t1: e.tensor_reduce(
                                out=kmf[:, 2 * ti:2 * ti + 2], in_=t1[:, 0:512].rearrange("p (j k) -> p j k", j=2),
                                axis=AX.X, op=ALU.add), r=[t1tok], w=["kmf"])
                    if t0 == NP:
                        vcp((QS if which == 0 else KS)[:, g, :], t1[:, 0:NS], r=[t1tok], w=["ss"], eng="pool")
            vts(kmf, kmf, 1.0 / 256.0, None, ALU.mult, None, r=["kmf"], w=["kmf"])
            vtt(kmb2.rearrange("p (h j) -> p h j", h=2), kmf.unsqueeze(1).broadcast_to([128, 2, 8]),
                acm[:, 3, 1:3].unsqueeze(2).broadcast_to([128, 2, 8]), ALU.mult, r=["kmf", "acm"], w=["kmb"])
            for tc in range(16 if os.environ.get('SK_VTOK') is None else 0):
                bank = 4 + tc % 2
                mm(ps[bank][:, 0:128], VT[:, tc * 128:(tc + 1) * 128], ident_b[:], True, True, r=["vt", "ident_b"],
                   w=[("ps", bank)])
                acp(VA[:, tc, :, 0:64], ps[bank][:, 0:128].rearrange("p (h d) -> p h d", h=2), r=[("ps", bank)],
                    w=["va"])
            vts(VT, KT, acm[:, 3, 2:3], None, ALU.mult, None, r=["kt", "vt", "acm"], w=["vt"])
            vts(KT, KT, acm[:, 3, 1:2], None, ALU.mult, None, r=["kt", "acm"], w=["kt"])
            KTh = [KT, VT]
            for qi in range(14):
                qc = qi + 2
                mm(ps[6][:, qi * 16:(qi + 1) * 16], QT[:, qc * 128:(qc + 1) * 128], kmb2, True, True,
                   r=["qt", "kmb"], w=[("ps", 6)])
            ga, gatok = tmp_ring.next()
            gb, gbtok = tmp_ring.next()
            gm = ga[:, 0:224]
            m8 = ga[:, 224:448]
            sel = gb[:, 0:224]
            nmb = gb[:, 256:368].bitcast(BF16)

            def v4(ap):
                return ap.rearrange("p (o f j) -> p o f j", o=7, f=4)

            def c4(mat):
                return acm[:, mat, 8:64].rearrange("p (o j) -> p o j", o=7).unsqueeze(2).broadcast_to([128, 7, 4, 8])
            for o_ in range(7):
                vtt(gm[:, o_ * 32:(o_ + 1) * 32].rearrange("p (f j) -> p f j", f=4),
                    ps[6][:, o_ * 32:(o_ + 1) * 32].rearrange("p (f j) -> p f j", f=4),
                    acm[:, 1, (o_ + 1) * 8:(o_ + 2) * 8].unsqueeze(1).broadcast_to([128, 4, 8]), ALU.add,
                    r=[("ps", 6), "acm"], w=[gatok])
            g3 = gm.rearrange("p (r j) -> p r j", j=8)
            t3 = m8.rearrange("p (r j) -> p r j", j=8)
            c3 = sel.rearrange("p (r j) -> p r j", j=8)
            for i_ in range(8):
                if i_ == 0:
                    vtt(c3, g3[:, :, 0:1].broadcast_to([128, 28, 8]), g3, ALU.is_gt, r=[gatok], w=[gbtok])
                else:
                    vtt(t3, g3[:, :, i_:i_ + 1].broadcast_to([128, 28, 8]), g3, ALU.is_gt, r=[gatok], w=[gatok])
                    vtt(c3, c3, t3, ALU.add, r=[gbtok, gatok], w=[gbtok])
            vts(sel, sel, -1.0, 2.5, ALU.mult, ALU.add, r=[gbtok], w=[gbtok])
            vts(sel, sel, 0.5, 0.0, ALU.min, ALU.max, r=[gbtok], w=[gbtok])
            for o_ in range(7):
                vtt(sel[:, o_ * 32:(o_ + 1) * 32].rearrange("p (f j) -> p f j", f=4),
                    sel[:, o_ * 32:(o_ + 1) * 32].rearrange("p (f j) -> p f j", f=4),
                    acm[:, 2, (o_ + 1) * 8:(o_ + 2) * 8].unsqueeze(1).broadcast_to([128, 4, 8]), ALU.mult,
                    r=[gbtok, "acm"], w=[gbtok])
            vts(nmb, sel, 60000.0, -30000.0, ALU.mult, ALU.add, r=[gbtok], w=[gbtok])
            for qi in range(14):
                pb = 6 + qi % 2
                mm(ps[pb][0:16, 0:128], nmb[:, qi * 16:(qi + 1) * 16], ident_b[:],
                   True, True, r=[gbtok, "ident_b"], w=[("ps", pb)])
                acp(NMT[0:16, 256 + qi * 128:256 + qi * 128 + 128], ps[pb][0:16, 0:128], r=[("ps", pb)], w=["nmt"])
            sbi = 0
            import os
            for qb in range(8 if os.environ.get('SK_ATT') is None else 0):
                q0 = qb * 256
                nkt = 2 * (qb + 1)
                ottok = "otb"
                otv = reg(o0 + 2720, 256).rearrange("p (a f) -> p a f", a=2)
                rz = reg(o0 + 2976, 8).bitcast(F32)
                for hl in range(2):
                    rows = slice(64 * hl, 64 * hl + 64)
                    accp = ((4, 5), (3, 7))[(qb * 2 + hl) % 2]
                    for kt in range(nkt):
                        j = kt // 2
                        sbank = sbi % 3
                        sbi += 1
                        mm(ps[sbank][:, 0:256], KTh[hl][:, kt * 128:(kt + 1) * 128], QT[:, q0:q0 + 256], True, False,
                           r=["kt", "vt", "qt"], w=[("ps", sbank)])
                        if j < qb:
                            r_ = hl * 8 + j
                            mm(ps[sbank][:, 0:256], SEL[hl][0:16, j, :],
                               NMT[0:16, q0:q0 + 256], False, True, r=["sel", "nmt"], w=[("ps", sbank)])
                        else:
                            mm(ps[sbank][:, 0:256], ident_b[:], CM[:, kt % 2, :], False, True, r=["ident_b", "cm"],
                               w=[("ps", sbank)])
                        pt_, pttok = tmp_ring.next()
                        ptb = pt_[:, :].bitcast(BF16)
                        actf(ptb[:, 0:256], ps[sbank][:, 0:256], AF.Exp, r=[("ps", sbank)], w=[pttok], scale=0.125)
                        for qh in range(2):
                            mm(ps[accp[qh]][:, 0:65], ptb[:, qh * 128:(qh + 1) * 128], VA[:, kt, hl, :],
                               kt == 0, kt == nkt - 1, r=[pttok, "va"], w=[("ps", accp[qh])])
                    for qh in range(2):
                        P.add("dve", lambda e, qh=qh, hl=hl, rz=rz, accp=accp: e.reciprocal(
                            out=rz[:, 2 * hl + qh:2 * hl + qh + 1], in_=ps[accp[qh]][:, 64:65]),
                              r=[("ps", accp[qh])], w=[ottok])
                        vts(otv[:, qh, 64 * hl:64 * hl + 64], ps[accp[qh]][:, 0:64],
                            rz[:, 2 * hl + qh:2 * hl + qh + 1], None, ALU.mult, None, r=[("ps", accp[qh]), ottok],
                            w=[ottok])
                for qh in range(2):
                    mm(ps[6][:, qh * 128:(qh + 1) * 128], otv[:, qh, :], ident_b[:], True, True,
                       r=[ottok, "ident_b"], w=[("ps", 6)])
                acp(OT[:, q0:q0 + 256], ps[6][:, 0:256], r=[("ps", 6)], w=["ot"])
            if os.environ.get('SK_OPJ') is None:
              out_proj(W["attn_w_o"][0], [g], OT.rearrange("p (c t) -> p c t", c=1), lambda ci: "ot", [6, 7],
                     tiles=TT[0:4])

        barrier()
        kt_b = [reg(0, 1024), reg(1024, 1024), reg(2048, 1024)]
        vt_b = [reg(3072, 1024), reg(4096, 1024), reg(5120, 1024)]
        qbc = reg(6144, 1024)
        vnb = reg(7168, 1024)
        prod = reg(8192, 1024)
        prodv = reg(9216, 1024)
        Qd = prodv
        Lall = reg(10240, 512).bitcast(F32)
        Pm_ = reg(10752, 512).bitcast(F32)
        gate = reg(11264, 256).bitcast(F32)
        bias = reg(11520, 256).bitcast(F32)
        m8s = reg(11776, 256).bitcast(F32)
        sm2 = reg(12032, 256).bitcast(F32)
        own, pown, Zt, rZ = sm2[:, 0:16], sm2[:, 16:32], sm2[:, 32:48], sm2[:, 48:64]
        qk, qk2 = sm2[:, 64:72], sm2[:, 72:88]
        kt_ring = Ring("ktile", kt_b)
        vt_ring = Ring("vtile", vt_b)
        import os
        def gather(cache_ap, ring, col):
            tile_, tok_ = ring.next()
            P.add("pool", lambda e: e.indirect_dma_start(
                out=tile_, out_offset=None, in_=cache_ap,
                in_offset=bass.IndirectOffsetOnAxis(ap=idx_i[:, col:col + 1], axis=0)),
                  r=["idx"], w=[tok_], dma=True)
            return tile_, tok_
        kq, vq = [], []
        if os.environ.get('SK_DEC') is None:
            for c3 in range(3):
                kq.append(gather(cache_k_in, kt_ring, c3))
        for b in range(NS if os.environ.get('SK_DEC') is None else 0):
            for (src, dstb, dtok) in ((QS, qbc, "qbc"), (VS, vnb, "vnb")):
                vtt(Qd.rearrange("p (c n) -> p c n", c=8), ident_b[:].unsqueeze(1).broadcast_to([128, 8, 128]),
                    src[:, :, b:b + 1].broadcast_to([128, 8, 128]), ALU.mult, r=["ident_b", "ss"], w=["prodv"])
                for k2 in range(2):
                    mm(ps[k2][:, :], ones_b[:], Qd[:, 512 * k2:512 * k2 + 512], True, True, r=["ones_b", "prodv"],
                       w=[("ps", k2)])
                    acp(dstb[:, 512 * k2:512 * k2 + 512], ps[k2][:, :], r=[("ps", k2)], w=[dtok])
            vtt(qk, QS[:, :, b], KS[:, :, b], ALU.mult, r=["ss"], w=["sm2"])
            vtt(qk2.rearrange("p (c h) -> p c h", c=8), qk.unsqueeze(2).broadcast_to([128, 8, 2]),
                acm[:, 3, 1:3].unsqueeze(1).broadcast_to([128, 8, 2]), ALU.mult, r=["sm2", "acm"], w=["sm2"])
            mm(ps[2][:, 0:16], ones_f, qk2, True, True, r=["cmat", "sm2"], w=[("ps", 2)])
            acp(own, ps[2][:, 0:16], r=[("ps", 2)], w=["sm2"])
            for pg in range(16):
                ktile, kttok = kq.pop(0)
                vtt(prod, ktile, qbc, ALU.mult, r=[kttok, "qbc"], w=["prod"])
                P.add("dve", lambda e, pg=pg: e.tensor_reduce(
                    out=Lall[:, pg * 16:(pg + 1) * 16], in_=prod.rearrange("p (h d) -> p h d", h=16),
                    axis=AX.X, op=ALU.add), r=["prod"], w=["lall"])
                nxt = b * 16 + pg + 3
                if nxt < NS * 16:
                    kq.append(gather(cache_k_in, kt_ring, nxt))
                if pg == 12:
                    for c3 in range(3):
                        vq.append(gather(cache_v_in, vt_ring, b * 16 + c3))
            mm(ps[3][:, 0:256], ones_f, Lall, True, True, r=["cmat", "lall"], w=[("ps", 3)])
            acp(Pm_, ps[3][:, 0:256], r=[("ps", 3)], w=["pm"])
            g4 = Pm_.rearrange("p (j two h) -> p j two h", two=2, h=16)
            vtt(gate.rearrange("p (j h) -> p j h", j=8), g4[:, :, 0, :], g4[:, :, 1, :], ALU.add,
                r=["pm"], w=["gate"])
            for h in range(16):
                P.add("dve", lambda e, h=h: e.max(out=m8s[:, h * 8:(h + 1) * 8],
                                                   in_=gate.rearrange("p (j h) -> p h j", j=8)[:, h, :]),
                      r=["gate"], w=["m8s"])
            vtt(bias.rearrange("p (j h) -> p j h", j=8), gate.rearrange("p (j h) -> p j h", j=8),
                m8s.rearrange("p (h k) -> p h k", k=8)[:, :, 2].unsqueeze(1).broadcast_to([128, 8, 16]), ALU.is_ge,
                r=["gate", "m8s"], w=["bias"])
            vts(bias, bias, 30000.0, -30000.0, ALU.mult, ALU.add, r=["bias"], w=["bias"])
            vtt(Lall.rearrange("p (j two h) -> p j two h", two=2, h=16),
                Lall.rearrange("p (j two h) -> p j two h", two=2, h=16),
                bias.rearrange("p (j h) -> p j h", j=8).unsqueeze(2).broadcast_to([128, 8, 2, 16]), ALU.add,
                r=["lall", "bias"], w=["lall"])
            actf(Pm_, Lall, AF.Exp, r=["lall"], w=["pm"], scale=0.125)
            actf(pown, own, AF.Exp, r=["sm2"], w=["sm2"], scale=0.125)
            mm(ps[3][:, 256:512], ones_f, Pm_, True, True, r=["cmat", "pm"], w=[("ps", 3)])
            P.add("dve", lambda e: e.tensor_reduce(out=Zt, in_=ps[3][:, 256:512].rearrange("p (g h) -> p h g", h=16),
                                                   axis=AX.X, op=ALU.add), r=[("ps", 3)], w=["sm2"])
            vtt(Zt, Zt, pown, ALU.add, r=["sm2"], w=["sm2"])
            P.add("dve", lambda e: e.reciprocal(out=rZ, in_=Zt), r=["sm2"], w=["sm2"])
            for pg in range(16):
                vtile, vttok = vq.pop(0)
                vtt(prodv.rearrange("p (h d) -> p h d", h=16), vtile.rearrange("p (h d) -> p h d", h=16),
                    Pm_[:, pg * 16:(pg + 1) * 16].unsqueeze(2).broadcast_to([128, 16, 64]), ALU.mult,
                    r=[vttok, "pm"], w=["prodv"])
                for k2 in range(2):
                    mm(ps[4 + k2][:, :], ones_b[:], prodv[:, 512 * k2:512 * k2 + 512], pg == 0, pg == 15,
                       r=["ones_b", "prodv"], w=[("ps", 4 + k2)])
                if pg + 3 < 16:
                    vq.append(gather(cache_v_in, vt_ring, b * 16 + pg + 3))
            for k2 in range(2):
                tq, tqtok = tmp_ring.next()
                tq3 = tq[:, :].rearrange("p (h d) -> p h d", h=8)
                vtt(tq3, vnb[:, 512 * k2:512 * k2 + 512].rearrange("p (h d) -> p h d", h=8),
                    pown[:, 8 * k2:8 * k2 + 8].unsqueeze(2).broadcast_to([128, 8, 64]), ALU.mult,
                    r=["vnb", "sm2"], w=[tqtok])
                vtt(tq[:, :], tq[:, :], ps[4 + k2][:, :], ALU.add, r=[tqtok, ("ps", 4 + k2)], w=[tqtok])
                vtt(tq3, tq3, rZ[:, 8 * k2:8 * k2 + 8].unsqueeze(2).broadcast_to([128, 8, 64]), ALU.mult,
                    r=[tqtok, "sm2"], w=[tqtok])
                tq4 = tq[:, :].rearrange("p (c n) -> p c n", c=4)
                vtt(tq4, tq4, ident_f.unsqueeze(1).broadcast_to([128, 4, 128]), ALU.mult, r=[tqtok, "cmat"], w=[tqtok])
                P.add("dve", lambda e, tq4=tq4, k2=k2, b=b: e.tensor_reduce(
                    out=OS[:, 4 * k2:4 * k2 + 4, b], in_=tq4, axis=AX.X, op=ALU.add), r=[tqtok], w=["os"])
        vcp(OSb, OS, r=["os"], w=["osb"])
        out_proj(W["attn_w_o"][0], list(range(8)), OSb, lambda ci: "osb", [6, 7], tiles=[TT[4]], src_t0=NP)
        barrier()

    for li in range(nlayers):
        kind, j = li % 3, li // 3
        if kind == 0:
            sconv_layer(li, j)
        elif kind == 1:
            ssd_layer(li)
        else:
            attn_layer(li)
        ffn_layer(li)

    rmsnorm("norm_final", (), final=True)

    P.emit(stack)
    stack.close()
    return nc


def make_cmat():
    i = np.arange(128)
    ident = (i[:, None] == i[None, :]).astype(np.float32)
    tri = (i[:, None] <= i[None, :]).astype(np.float32)
    maskneg = np.where(i[None, :] >= i[:, None], 0.0, -30000.0).astype(np.float32)
    ones = np.ones((128, 128), np.float32)
    return np.ascontiguousarray(np.stack([ident, tri, maskneg, ones], axis=1))


def make_rope():
    theta, rot = 500000.0, 16
    half = rot // 2
    inv_freq = (np.float32(theta) ** (-(np.arange(half, dtype=np.float32) * np.float32(2.0)) / np.float32(rot))).astype(np.float32)
    pos = np.concatenate([np.arange(NP), np.full(NS, NP)]).astype(np.float32)
    out = np.zeros((128, 2, T), np.float32)
    out[:, 0, :] = 1.0
    for p in range(128):
        d = p % 64
        if d < rot:
            ang = (pos * inv_freq[d % half]).astype(np.float32)
            out[p, 0] = np.cos(ang)
            out[p, 1] = np.sin(ang)
    return out


def make_cm():
    k = np.arange(128)[:, None, None]
    kt = np.arange(2)[None, :, None]
    q = np.arange(256)[None, None, :]
    return np.ascontiguousarray(np.where(kt * 128 + k <= q, 0.0, -30000.0).astype(np.float32))


def make_acm():
    a = np.zeros((128, 4, 128), np.float32)
    for m in range(128):
        d = m % 64
        if d < 8:
            a[m + 8, 0, m] = -1.0
        elif d < 16:
            a[m - 8, 0, m] = 1.0
    for ob in range(8):
        for j in range(8):
            a[:, 1, ob * 8 + j] = 0.0 if j < ob else -1e9
            a[:, 2, ob * 8 + j] = 1.0 if j < ob else 0.0
    a[:, 3, 0] = np.arange(128)
    a[:64, 3, 1] = 1.0
    a[64:, 3, 2] = 1.0
    return a


def fm_tokens(a):
    r, F = a.shape
    return np.ascontiguousarray(a.reshape(r, F // 128, 128).transpose(2, 1, 0))


_CACHE = {}


def kernel(**inp):
    nlayers = int(inp.pop("_nlayers", 4))
    small_cache = bool(inp.pop("_small_cache", False))
    f32 = lambda k: np.asarray(inp[k], dtype=np.float32)
    pk = pack_params(inp)
    params = pk.build()
    nc = build_program(pk.items, pk.cols, nlayers=nlayers, npool=(2 if small_cache else NPOOL))

    x_prompt = f32("x_prompt")
    x_sample = f32("x_sample")
    st_sconv = f32("state_sconv")
    st_ffn = f32("state_ffn_conv")
    shared = {
        "params": params,
        "sconv_w_in": f32("sconv_w_in"), "sconv_w_out": f32("sconv_w_out"),
        "ffn_w_up": f32("ffn_w_up"), "ffn_w_down": f32("ffn_w_down"),
        "ssd_w_in": f32("ssd_w_in"), "ssd_w_out": f32("ssd_w_out"),
        "cmat": make_cmat(),
        "attn_w_qkv": f32("attn_w_qkv"), "attn_w_o": f32("attn_w_o"),
        "rope": make_rope(), "cm_mask": make_cm(), "acm": make_acm(),
    }
    if small_cache:
        shared["cache_k"] = np.zeros((256, D), np.float32)
        shared["cache_v"] = np.zeros((256, D), np.float32)
    else:
        shared["cache_k"] = f32("cache_k")[0].reshape(NPOOL * 128, D)
        shared["cache_v"] = f32("cache_v")[0].reshape(NPOOL * 128, D)
    page_table = np.asarray(inp["page_table"]).astype(np.int32)
    if small_cache:
        page_table = np.zeros_like(page_table)
    st_ssm = f32("state_ssm")
    st_ssmc = f32("state_ssm_conv")
    in_maps = []
    for core in range(NCORES):
        sl = slice(core * NS, (core + 1) * NS)
        xcat = np.concatenate([x_prompt[core], x_sample[sl, 0]], axis=0)
        m = dict(shared)
        m["xT_in"] = fm_tokens(xcat)
        sp = st_sconv[:, sl].reshape(2, NS, 2, 8, 128).transpose(0, 4, 3, 1, 2)
        m["sconv_past"] = np.ascontiguousarray(sp)
        fp = st_ffn[:, sl].reshape(4, NS, 2, 44, 128).transpose(0, 4, 3, 1, 2)
        m["ffn_past"] = np.ascontiguousarray(fp)
        m["ssmc_past"] = np.ascontiguousarray(st_ssmc[0, sl].reshape(NS, 3, 24, 128).transpose(3, 2, 0, 1))
        m["ssm_state"] = np.ascontiguousarray(st_ssm[0, sl].reshape(NS * 2048, 128))
        m["page_tab"] = np.ascontiguousarray(page_table[sl].reshape(1, NS * 16))
        in_maps.append(m)
    res = run_bass_kernel_spmd(nc, in_maps, core_ids=list(range(NCORES)))
    R = res.results
    global _DBG
    _DBG = R

    y_prompt = np.zeros((8, NP, D), np.float32)
    y_sample = np.zeros((128, 1, D), np.float32)
    sconv_p = np.zeros((2, 8, 2, D), np.float32)
    sconv_s = np.zeros((2, 128, 2, D), np.float32)
    ffn_p = np.zeros((4, 8, 2, 2 * DFF), np.float32)
    ffn_s = np.zeros((4, 128, 2, 2 * DFF), np.float32)
    ssm_p = np.zeros((1, 8, 32, 64, 128), np.float32)
    ssm_s = np.zeros((1, 128, 32, 64, 128), np.float32)
    ssmc_p = np.zeros((1, 8, 3, 3072), np.float32)
    ssmc_s = np.zeros((1, 128, 3, 3072), np.float32)
    k_p = np.zeros((1, 8, NP, 16, 64), np.float32)
    v_p = np.zeros((1, 8, NP, 16, 64), np.float32)
    k_s = np.zeros((1, 128, 1, 16, 64), np.float32)
    v_s = np.zeros((1, 128, 1, 16, 64), np.float32)
    for core in range(NCORES):
        sl = slice(core * NS, (core + 1) * NS)
        r = R[core]
        yT = r["yT_out"]
        yt = yT.transpose(2, 1, 0).reshape(T, D)
        y_prompt[core] = yt[:NP]
        y_sample[sl, 0] = yt[NP:]
        so = r["sconv_out"]
        sconv_p[:, core] = so[:, :, :, 0:2].transpose(0, 3, 2, 1).reshape(2, 2, D)
        ss = so[:, :, :, 2:].reshape(2, 128, 8, NS, 2).transpose(0, 3, 4, 2, 1).reshape(2, NS, 2, D)
        sconv_s[:, sl] = ss
        fo = r["ffn_out"]
        ffn_p[:, core] = fo[:, :, :, 0:2].transpose(0, 3, 2, 1).reshape(4, 2, 2 * DFF)
        fs = fo[:, :, :, 2:].reshape(4, 128, 44, NS, 2).transpose(0, 3, 4, 2, 1).reshape(4, NS, 2, 2 * DFF)
        ffn_s[:, sl] = fs
        if nlayers >= 2:
            ssm_p[0, core] = r["ssm_p_out"].reshape(4, 128, 8, 64).transpose(0, 2, 3, 1).reshape(32, 64, 128)
            ssm_s[0, sl] = r["ssm_s_out"].reshape(NS, 32, 64, 128)
            co = r["ssmc_out"]
            ssmc_p[0, core] = co[:, :, 0:3].transpose(2, 1, 0).reshape(3, 3072)
            ssmc_s[0, sl] = co[:, :, 3:].reshape(128, 24, NS, 3).transpose(2, 3, 1, 0).reshape(NS, 3, 3072)
        if nlayers >= 3:
            for (dstp, dsts, nm) in ((k_p, k_s, "kT_out"), (v_p, v_s, "vT_out")):
                kt_ = r[nm].transpose(2, 1, 0).reshape(T, D)
                dstp[0, core] = kt_[:NP].reshape(NP, 16, 64)
                dsts[0, sl, 0] = kt_[NP:].reshape(NS, 16, 64)
    H, Pd, N = 32, 64, 128
    return (y_prompt, y_sample, sconv_p, sconv_s,
            ssm_p, ssm_s, ssmc_p, ssmc_s,
            k_p, v_p, k_s, v_s,
            ffn_p, ffn_s)
```

```python
import contextlib
import numpy as np
import concourse.bass as bass
import concourse.mybir as mybir
from concourse.bass_utils import run_bass_kernel_spmd
from concourse.ap import AP

F32 = mybir.dt.float32
BF16 = mybir.dt.bfloat16
I32 = mybir.dt.int32
AF = mybir.ActivationFunctionType
ALU = mybir.AluOpType
AX = mybir.AxisListType

NCORES = 8
D = 1024
NP = 2048
NS = 16
T = NP + NS
PADL = 3
TP = T + PADL + 1
DFF = 2816
NFC = 22
NPOOL = 2560
EPS = 1e-6
TT = [(0, 512), (512, 512), (1024, 512), (1536, 512), (2048, 16)]


def conv_tiles(halo):
    n = 512 - halo
    out = []
    s = 0
    while s < NP:
        out.append((s, min(n, NP - s)))
        s += n
    return out


class Op:
    __slots__ = ("eng", "fn", "dma", "deps", "inc", "sem", "val", "idx")


class Prog:
    ENGS = ("pe", "act", "dve", "pool", "sp")
    NDMASEM = {"sp": 8, "pool": 8, "act": 4}

    def __init__(self, nc):
        self.nc = nc
        self.ops = []
        self.last_w = {}
        self.readers = {}

    BAR = tuple(("bar", e) for e in ("pe", "act", "dve", "pool", "sp"))

    def add(self, eng, fn, r=(), w=(), dma=False):
        r = list(r) + list(self.BAR)
        op = Op()
        op.eng, op.fn, op.dma = eng, fn, dma
        op.idx = len(self.ops)
        op.inc = dma
        op.sem = None
        op.val = 0
        deps = {}
        for k in r:
            lw = self.last_w.get(k)
            if lw is not None:
                deps[lw] = True
            self.readers.setdefault(k, []).append(op.idx)
        for k in w:
            lw = self.last_w.get(k)
            if lw is not None and lw not in deps:
                deps[lw] = False
            for rd in self.readers.get(k, ()):
                if rd != op.idx and rd not in deps:
                    deps[rd] = False
            self.readers[k] = []
            self.last_w[k] = op.idx
        op.deps = deps
        self.ops.append(op)
        return op

    def dma(self, eng, out, in_, r=(), w=()):
        return self.add(eng, lambda e: e.dma_start(out=out, in_=in_), r=r, w=w, dma=True)

    def emit(self, stack):
        nc = self.nc
        ops = self.ops
        for op in ops:
            for d, raw in op.deps.items():
                y = ops[d]
                if y.dma:
                    continue
                if y.eng == op.eng and not raw:
                    continue
                y.inc = True
        esem = {e: stack.enter_context(nc.semaphore("es_" + e)) for e in ("pe", "act", "dve", "pool")}
        dsem = {q: [stack.enter_context(nc.semaphore("ds_%s%d" % (q, i))) for i in range(n)]
                for q, n in self.NDMASEM.items()}
        cnt = {e: 0 for e in esem}
        dcnt = {q: 0 for q in dsem}
        for op in ops:
            if op.dma:
                m = dcnt[op.eng]
                dcnt[op.eng] += 1
                n = self.NDMASEM[op.eng]
                op.sem = dsem[op.eng][m % n]
                op.val = 16 * (m // n + 1)
            elif op.inc:
                cnt[op.eng] += 1
                op.sem = esem[op.eng]
                op.val = cnt[op.eng]
        block = stack.enter_context(nc.Block())

        def run(engname):
            def body(e):
                waited = {}
                for op in ops:
                    if op.eng != engname:
                        continue
                    need = {}
                    for d, raw in op.deps.items():
                        y = ops[d]
                        if (not y.dma) and y.eng == engname and not raw:
                            continue
                        key = id(y.sem)
                        if key not in need or need[key][1] < y.val:
                            need[key] = (y.sem, y.val)
                    if op.dma and op.val > 16:
                        key = id(op.sem)
                        v = op.val - 16
                        if key not in need or need[key][1] < v:
                            need[key] = (op.sem, v)
                    for key, (s, v) in need.items():
                        if waited.get(key, 0) >= v:
                            continue
                        e.wait_ge(s, v)
                        waited[key] = v
                    ins = op.fn(e)
                    if op.dma:
                        ins.then_inc(op.sem, 16)
                    elif op.inc:
                        ins.then_inc(op.sem, 1)
                if engname in dsem:
                    m = dcnt[engname]
                    n = self.NDMASEM[engname]
                    for i in range(min(m, n)):
                        uses = (m - 1 - i) // n + 1
                        e.wait_ge(dsem[engname][i], 16 * uses)
            return body

        block.tensor(run("pe"))
        block.scalar(run("act"))
        block.vector(run("dve"))
        block.gpsimd(run("pool"))
        block.sync(run("sp"))


class Ring:
    def __init__(self, name, aps):
        self.name = name
        self.aps = aps
        self.i = 0

    def next(self):
        k = self.i % len(self.aps)
        self.i += 1
        return self.aps[k], (self.name, k)


class Pack:
    def __init__(self):
        self.cols = 0
        self.items = {}
        self.arrs = []

    def put(self, name, arr):
        arr = np.ascontiguousarray(arr, dtype=np.float32)
        assert arr.shape[0] == 128
        a2 = arr.reshape(128, -1)
        self.items[name] = (self.cols, arr.shape[1:])
        self.cols += a2.shape[1]
        self.arrs.append(a2)

    def build(self):
        return np.ascontiguousarray(np.concatenate(self.arrs, axis=1))


def col_layout(v):
    v = np.asarray(v, dtype=np.float32)
    F = v.shape[-1]
    lead = v.shape[:-1]
    a = v.reshape(lead + (F // 128, 128))
    return np.moveaxis(a, -1, 0)


def pack_params(inp, with_values=True):
    pk = Pack()
    g = (lambda k: np.asarray(inp[k], dtype=np.float32))
    pk.put("norm_mix", col_layout(g("norm_mix_w")))
    pk.put("norm_ffn", col_layout(g("norm_ffn_w")))
    pk.put("norm_final", col_layout(g("norm_final_w")))
    pk.put("sconv_cw", np.moveaxis(col_layout(g("sconv_conv_w")), 2, 3))
    pk.put("ffn_cw", np.moveaxis(col_layout(g("ffn_conv_w")), 2, 3))
    pk.put("ffn_cb", col_layout(g("ffn_conv_b")))
    pk.put("ssd_cw", np.moveaxis(col_layout(g("ssd_conv_w")[0]), 1, 2))
    pk.put("ssd_cb", col_layout(g("ssd_conv_b")[0]))
    pk.put("ssd_D", col_layout(np.repeat(g("ssd_d")[0], 64)))
    pk.put("ssd_nw", col_layout(g("ssd_norm_w")[0]))
    pk.put("ssd_dtb", np.broadcast_to(g("ssd_dt_bias")[0][None, :], (128, 32)))
    pk.put("ssd_alog", np.broadcast_to(g("ssd_a_log")[0][None, :], (128, 32)))
    return pk


def build_program(pk_items, pk_cols, nlayers=4, npool=NPOOL):
    nc = bass.Bass("TRN2", target_bir_lowering=False)
    P = Prog(nc)
    stack = contextlib.ExitStack()

    def din(name, shape, dt=F32):
        return nc.dram_tensor(name, list(shape), dt, kind="ExternalInput").ap()

    def dout(name, shape, dt=F32):
        return nc.dram_tensor(name, list(shape), dt, kind="ExternalOutput").ap()

    def sb(name, shape, dt):
        return stack.enter_context(nc.sbuf_tensor(name, list(shape), dt))

    xT_in = din("xT_in", [128, 8, T])
    params_in = din("params", [128, pk_cols])
    sconv_past_in = din("sconv_past", [2, 128, 8, NS, 2])
    ffn_past_in = din("ffn_past", [4, 128, 44, NS, 2])
    W = {
        "sconv_w_in": din("sconv_w_in", [2, D, 3 * D]),
        "sconv_w_out": din("sconv_w_out", [2, D, D]),
        "ffn_w_up": din("ffn_w_up", [4, D, 2 * DFF]),
        "ffn_w_down": din("ffn_w_down", [4, DFF, D]),
    }
    W["ssd_w_in"] = din("ssd_w_in", [1, D, 5152])
    W["ssd_w_out"] = din("ssd_w_out", [1, 2048, D])
    ssmc_past_in = din("ssmc_past", [128, 24, NS, 3])
    ssm_state_in = din("ssm_state", [NS * 2048, 128])
    cmat_in = din("cmat", [128, 4, 128])
    ssmc_out = dout("ssmc_out", [128, 24, 3 + 3 * NS])
    ssm_p_out = dout("ssm_p_out", [4, 128, 512])
    ssm_s_out = dout("ssm_s_out", [NS * 2048, 128])
    W["attn_w_qkv"] = din("attn_w_qkv", [1, D, 3 * D])
    W["attn_w_o"] = din("attn_w_o", [1, D, D])
    rope_in = din("rope", [128, 2, T])
    cm_in = din("cm_mask", [128, 2, 256])
    acm_in = din("acm", [128, 4, 128])
    pt_in = din("page_tab", [1, NS * 16], I32)
    cache_k_in = din("cache_k", [npool * 128, D])
    cache_v_in = din("cache_v", [npool * 128, D])
    kT_out = dout("kT_out", [128, 8, T])
    vT_out = dout("vT_out", [128, 8, T])
    yT_out = dout("yT_out", [128, 8, T])
    sconv_out = dout("sconv_out", [2, 128, 8, 2 + 2 * NS])
    ffn_out = dout("ffn_out", [4, 128, 44, 2 + 2 * NS])

    xT = sb("xT", [128, 8, T], F32)
    hT = sb("hT", [128, 8, TP], BF16)
    WK = sb("WK", [128, 8, T], BF16)
    prm = sb("prm", [128, pk_cols], F32)
    NWS = 3
    wsl = [sb("wsl%d" % i, [128, 4096], BF16) for i in range(NWS)]
    ones_m = sb("ones_m", [128, 128], BF16)
    rstd = sb("rstd", [128, T], F32)
    epst = sb("epst", [128, 1], F32)
    NTMP = 6
    tmpf = [sb("tmpf%d" % i, [128, 512], F32) for i in range(NTMP)]
    stage = sb("stage", [128, 44, 2 + 2 * NS], F32)
    pastb = sb("pastb", [128, 44, NS, 2], F32)
    psall = stack.enter_context(nc.psum_tensor("psall", [128, 8, 512], F32))
    ps = [psall[:, i, :] for i in range(8)]

    cmat = sb("cmat_sb", [128, 4, 128], F32)
    ident_f, tri_f, maskneg_f, ones_f = cmat[:, 0, :], cmat[:, 1, :], cmat[:, 2, :], cmat[:, 3, :]
    ident_b = sb("ident_b", [128, 128], BF16)
    ones_b = sb("ones_b", [128, 128], BF16)
    ones_g = sb("ones_g", [128, 128], BF16)
    onet = sb("onet", [128, 1], F32)
    dt_tok = sb("dt_tok", [128, 17, 32], F32)
    dtA_tok = sb("dtA_tok", [128, 17, 32], F32)
    a_bc = sb("a_bc", [128, 32], F32)
    prevT = sb("prevT", [128, 512], F32)
    prevT_bf = sb("prevT_bf", [128, 512], BF16)
    smt = sb("smt", [128, 64], F32)
    sst = [sb("sst%d" % i, [128, 4, 128], F32) for i in range(2)]
    snew = [sb("snew%d" % i, [128, 4, 128], F32) for i in range(1)]
    ys_t = sb("ys_t", [128, 4, NS], F32)
    sx_t = sb("sx_t", [128, 3, 4, NS], F32)
    dmy = {e: sb("dmy_" + e, [128, 2], F32) for e in ("act", "dve", "pool", "sp")}
    tmp_ring = Ring("tmpf", [t for t in tmpf])
    sst_ring = Ring("sst", sst)
    snew_ring = Ring("snew", snew)
    ws_ring = Ring("wsl", [t for t in wsl])

    def prm_ap(name):
        off, shp = pk_items[name]
        n = int(np.prod(shp))
        a = prm[:, off:off + n]
        return a, off, shp

    def pcol(name, *idx):
        off, shp = pk_items[name]
        flat = 0
        for i, s in zip(idx, shp):
            flat = flat * s + i
        return prm[:, off + flat:off + flat + 1]

    P.dma("sp", prm[:], params_in, w=["prm"])
    for c in range(8):
        P.dma("sp", xT[:, c, :], xT_in[:, c, :], w=[("x", c)])
    P.add("pool", lambda e: e.memset(ones_m[:], 1.0 / 1024.0), w=["ones"])
    P.add("pool", lambda e: e.memset(epst[:], EPS), w=["epst"])
    P.add("pool", lambda e: e.memset(onet[:], 1.0), w=["onet"])
    P.add("pool", lambda e: e.memset(ones_b[:], 1.0), w=["ones_b"])
    P.add("pool", lambda e: e.memset(ones_g[:], 1.0 / 512.0), w=["ones_g"])
    P.dma("sp", cmat[:], cmat_in, w=["cmat"])
    P.add("dve", lambda e: e.tensor_copy(out=ident_b[:], in_=ident_f), r=["cmat"], w=["ident_b"])
    P.add("pool", lambda e: e.memset(hT[:, :, 0:PADL], 0.0), w=["hpad"])


    def mm(out, lhsT, rhs, start, stop, r, w):
        P.add("pe", lambda e: e.matmul(out, lhsT=lhsT, rhs=rhs, start=start, stop=stop), r=r, w=w)

    def trp(out, in_, ident, r, w):
        P.add("pe", lambda e: e.transpose(out, in_, ident), r=r, w=w)

    def actf(out, in_, func, r, w, bias=None, scale=1.0):
        if bias is None:
            P.add("act", lambda e: e.activation(out=out, in_=in_, func=func, scale=scale), r=r, w=w)
        else:
            P.add("act", lambda e: e.activation(out=out, in_=in_, func=func, bias=bias, scale=scale), r=r, w=w)

    def acp(out, in_, r, w):
        P.add("act", lambda e: e.copy(out=out, in_=in_), r=r, w=w)

    def vtt(out, in0, in1, op, r, w, eng="dve"):
        P.add(eng, lambda e: e.tensor_tensor(out=out, in0=in0, in1=in1, op=op), r=r, w=w)

    def vts(out, in0, s1, s2, op0, op1, r, w):
        if s2 is None:
            P.add("dve", lambda e: e.tensor_scalar(out=out, in0=in0, scalar1=s1, scalar2=None, op0=op0), r=r, w=w)
        else:
            P.add("dve", lambda e: e.tensor_scalar(out=out, in0=in0, scalar1=s1, scalar2=s2, op0=op0, op1=op1),
                  r=r, w=w)

    def vstt(out, in0, scalar, in1, op0, op1, r, w, accum_out=None):
        if accum_out is None:
            P.add("dve", lambda e: e.scalar_tensor_tensor(out=out, in0=in0, scalar=scalar, in1=in1, op0=op0, op1=op1),
                  r=r, w=w)
        else:
            P.add("dve", lambda e: e.scalar_tensor_tensor(out=out, in0=in0, scalar=scalar, in1=in1, op0=op0, op1=op1,
                                                          accum_out=accum_out), r=r, w=w)

    def vcp(out, in_, r, w, eng="dve"):
        P.add(eng, lambda e: e.tensor_copy(out=out, in_=in_), r=r, w=w)

    def load_w(wap, kchunks, c0, ncols, slot_ap, slot_off, tokw):
        nk = len(kchunks)
        k0 = kchunks[0]
        assert kchunks == list(range(k0, k0 + nk))
        src = wap[k0 * 128:(k0 + nk) * 128, c0:c0 + ncols].rearrange("(c p) m -> p c m", p=128)
        dst = slot_ap[:, slot_off:slot_off + nk * ncols].rearrange("p (c m) -> p c m", c=nk)
        P.dma("pool", dst, src, w=[tokw])

    def wview(slot_ap, slot_off, nk, ncols):
        return slot_ap[:, slot_off:slot_off + nk * ncols].rearrange("p (c m) -> p c m", c=nk)

    def rmsnorm(wname, widx, final=False):
        for c in range(8):
            P.add("act", lambda e, c=c: e.activation(out=WK[:, c, :], in_=xT[:, c, :], func=AF.Square),
                  r=[("x", c)], w=[("wk", c)])
        for ti, (t0, w) in enumerate(TT):
            for c in range(8):
                P.add("pe", lambda e, c=c, w=w, t0=t0, ti=ti: e.matmul(ps[ti][:, 0:w], lhsT=ones_m[:],
                                                                     rhs=WK[:, c, t0:t0 + w],
                                                                     start=(c == 0), stop=(c == 7)),
                      r=[("wk", c), "ones"], w=[("ps", ti)])
        P.add("act", lambda e: e.activation(out=rstd[:, 0:NP].rearrange("p (a b) -> p a b", a=4),
                                            in_=psall[:, 0:4, :], func=AF.Sqrt, bias=epst[:], scale=1.0),
              r=[("ps", 0), ("ps", 1), ("ps", 2), ("ps", 3), "epst"], w=["rstd"])
        P.add("act", lambda e: e.activation(out=rstd[:, NP:T], in_=ps[4][:, 0:NS], func=AF.Sqrt,
                                            bias=epst[:], scale=1.0),
              r=[("ps", 4), "epst"], w=["rstd"])
        P.add("dve", lambda e: e.reciprocal(out=rstd[:, :], in_=rstd[:, :]), r=["rstd"], w=["rstd"])
        for (t0, w) in TT:
            for c in range(8):
                if final:
                    ap, tok = tmp_ring.next()
                    dst = ap[:, 0:w]
                else:
                    dst = hT[:, c, PADL + t0:PADL + t0 + w]
                    tok = ("h", c)
                P.add("dve", lambda e, c=c, t0=t0, w=w, dst=dst: e.scalar_tensor_tensor(
                    out=dst, in0=xT[:, c, t0:t0 + w], scalar=pcol(wname, *(widx + (c,))),
                    in1=rstd[:, t0:t0 + w], op0=ALU.mult, op1=ALU.mult),
                      r=[("x", c), "rstd", "prm"], w=[tok])
                if final:
                    P.dma("sp", yT_out[:, c, t0:t0 + w], dst, r=[tok])

    def h_dst(c, t0, w):
        return hT[:, c, PADL + t0:PADL + t0 + w]

    def add_resid(o, t0, w, bank):
        P.add("dve", lambda e: e.tensor_tensor(out=xT[:, o, t0:t0 + w], in0=xT[:, o, t0:t0 + w],
                                               in1=ps[bank][:, 0:w], op=ALU.add),
              r=[("x", o), ("ps", bank)], w=[("x", o)])

    def out_proj(wap, kchunks_all, src, src_tokf, banks, tiles=None, src_t0=0):
        tiles = TT if tiles is None else tiles
        nk = len(kchunks_all)
        gcols = 256 if nk > 8 else 512
        bi = 0
        for og in range(D // gcols):
            slot, stok = ws_ring.next()
            load_w(wap, kchunks_all, og * gcols, gcols, slot, 0, stok)
            wv = wview(slot, 0, nk, gcols)
            for oo in range(gcols // 128):
                o = og * (gcols // 128) + oo
                for (t0, w) in tiles:
                    bank = banks[bi % len(banks)]
                    bi += 1
                    for ci in range(nk):
                        P.add("pe", lambda e, ci=ci, oo=oo, t0=t0, w=w, bank=bank, wv=wv: e.matmul(
                            ps[bank][:, 0:w], lhsT=wv[:, ci, oo * 128:(oo + 1) * 128],
                            rhs=src[:, ci, t0 - src_t0:t0 - src_t0 + w],
                            start=(ci == 0), stop=(ci == nk - 1)),
                              r=[stok, src_tokf(ci)], w=[("ps", bank)])
                    add_resid(o, t0, w, bank)

    CT2 = conv_tiles(2)

    def sconv_layer(li, j):
        rmsnorm("norm_mix", (li,))
        w_in = W["sconv_w_in"][j]
        P.dma("sp", pastb[:, 0:8, :, :], sconv_past_in[j], w=["pastb"])
        yv = WK
        grp = 0
        for i in range(8):
            slot, stok = ws_ring.next()
            for q in range(3):
                load_w(w_in, list(range(8)), q * D + i * 128, 128, slot, q * 1024, stok)
            wv = [wview(slot, q * 1024, 8, 128) for q in range(3)]
            cw = [pcol("sconv_cw", j, i, k) for k in range(3)]
            tiles = [(s, n, False) for (s, n) in CT2] + [(NP, NS, True)]
            for (s, n, is_s) in tiles:
                b0 = 3 * (grp % 2)
                grp += 1
                if is_s:
                    c0, wd = PADL + NP, NS
                else:
                    c0, wd = PADL + s - 2, n + 2
                for q in range(3):
                    for c in range(8):
                        P.add("pe", lambda e, q=q, c=c, c0=c0, wd=wd, b0=b0, wv=wv: e.matmul(
                            ps[b0 + q][:, 0:wd], lhsT=wv[q][:, c, :], rhs=hT[:, c, c0:c0 + wd],
                            start=(c == 0), stop=(c == 7)),
                              r=[stok, ("h", c), "hpad"], w=[("ps", b0 + q)])
                go, gi, va = ps[b0], ps[b0 + 1], ps[b0 + 2]
                vs, vtok = tmp_ring.next()
                P.add("act", lambda e, vs=vs, va=va, wd=wd: e.copy(out=vs[:, 0:wd], in_=va[:, 0:wd]),
                      r=[("ps", b0 + 2)], w=[vtok])
                u, utok = tmp_ring.next()
                P.add("dve", lambda e, u=u, gi=gi, vs=vs, wd=wd: e.tensor_tensor(
                    out=u[:, 0:wd], in0=gi[:, 0:wd], in1=vs[:, 0:wd], op=ALU.mult),
                      r=[("ps", b0 + 1), vtok], w=[utok])
                cc, ctok = tmp_ring.next()
                if not is_s:
                    P.add("dve", lambda e, cc=cc, u=u, n=n, cw=cw: e.tensor_scalar(
                        out=cc[:, 0:n], in0=u[:, 2:n + 2], scalar1=cw[2], scalar2=None, op0=ALU.mult),
                          r=[utok, "prm"], w=[ctok])
                    P.add("dve", lambda e, cc=cc, u=u, n=n, cw=cw: e.scalar_tensor_tensor(
                        out=cc[:, 0:n], in0=u[:, 1:n + 1], scalar=cw[1], in1=cc[:, 0:n], op0=ALU.mult, op1=ALU.add),
                          r=[utok, ctok, "prm"], w=[ctok])
                    P.add("dve", lambda e, cc=cc, u=u, n=n, cw=cw: e.scalar_tensor_tensor(
                        out=cc[:, 0:n], in0=u[:, 0:n], scalar=cw[0], in1=cc[:, 0:n], op0=ALU.mult, op1=ALU.add),
                          r=[utok, ctok, "prm"], w=[ctok])
                    P.add("dve", lambda e, cc=cc, go=go, n=n, i=i, s=s: e.tensor_tensor(
                        out=yv[:, i, s:s + n], in0=go[:, 2:n + 2], in1=cc[:, 0:n], op=ALU.mult),
                          r=[("ps", b0), ctok], w=[("wk", i)])
                    if s + n == NP:
                        P.add("act", lambda e, u=u, n=n, i=i: e.copy(out=stage[:, i, 0:2], in_=u[:, n:n + 2]),
                              r=[utok], w=["stage"])
                else:
                    p0 = pastb[:, i, :, 0]
                    p1 = pastb[:, i, :, 1]
                    P.add("dve", lambda e, cc=cc, u=u, cw=cw: e.tensor_scalar(
                        out=cc[:, 0:NS], in0=u[:, 0:NS], scalar1=cw[2], scalar2=None, op0=ALU.mult),
                          r=[utok, "prm"], w=[ctok])
                    P.add("dve", lambda e, cc=cc, p1=p1, cw=cw: e.scalar_tensor_tensor(
                        out=cc[:, 0:NS], in0=p1, scalar=cw[1], in1=cc[:, 0:NS], op0=ALU.mult, op1=ALU.add),
                          r=["pastb", ctok, "prm"], w=[ctok])
                    P.add("dve", lambda e, cc=cc, p0=p0, cw=cw: e.scalar_tensor_tensor(
                        out=cc[:, 0:NS], in0=p0, scalar=cw[0], in1=cc[:, 0:NS], op0=ALU.mult, op1=ALU.add),
                          r=["pastb", ctok, "prm"], w=[ctok])
                    P.add("dve", lambda e, cc=cc, go=go, i=i: e.tensor_tensor(
                        out=yv[:, i, NP:NP + NS], in0=go[:, 0:NS], in1=cc[:, 0:NS], op=ALU.mult),
                          r=[("ps", b0), ctok], w=[("wk", i)])
                    sv = stage[:, i, 2:2 + 2 * NS].rearrange("p (b k) -> p b k", k=2)
                    P.add("act", lambda e, sv=sv, u=u: e.copy(out=sv[:, :, 1], in_=u[:, 0:NS]),
                          r=[utok], w=["stage"])
                    P.add("pool", lambda e, sv=sv, p1=p1: e.tensor_copy(out=sv[:, :, 0], in_=p1),
                          r=["pastb"], w=["stage"])
        P.dma("sp", sconv_out[j], stage[:, 0:8, :], r=["stage"])
        out_proj(W["sconv_w_out"][j], list(range(8)), yv, lambda ci: ("wk", ci), [6, 7])

    def ffn_layer(li):
        rmsnorm("norm_ffn", (li,))
        w_up = W["ffn_w_up"][li]
        P.dma("sp", pastb[:, :, :, :], ffn_past_in[li], w=["pastb"])
        act = WK
        grp = 0
        for chunks in (list(range(0, 8)), list(range(8, 15)), list(range(15, 22))):
            for ii, i in enumerate(chunks):
                if ii % 2 == 0:
                    np_ = min(2, len(chunks) - ii)
                    slot, stok = ws_ring.next()
                    load_w(w_up, list(range(8)), i * 128, 128 * np_, slot, 0, stok)
                    load_w(w_up, list(range(8)), DFF + i * 128, 128 * np_, slot, 2048, stok)
                    wpair = [wview(slot, 0, 8, 128 * np_), wview(slot, 2048, 8, 128 * np_)]
                e_ = ii % 2
                wv = [wp[:, :, e_ * 128:(e_ + 1) * 128] for wp in wpair]
                fidx = [i, NFC + i]
                cw = [[pcol("ffn_cw", li, f, k) for k in range(3)] for f in fidx]
                cb = [pcol("ffn_cb", li, f) for f in fidx]
                tiles = [(s, n, False) for (s, n) in CT2] + [(NP, NS, True)]
                for (s, n, is_s) in tiles:
                    b0 = 2 * (grp % 2)
                    grp += 1
                    if is_s:
                        c0, wd = PADL + NP, NS
                    else:
                        c0, wd = PADL + s - 2, n + 2
                    for q in range(2):
                        for c in range(8):
                            P.add("pe", lambda e, q=q, c=c, c0=c0, wd=wd, b0=b0, wv=wv: e.matmul(
                                ps[b0 + q][:, 0:wd], lhsT=wv[q][:, c, :], rhs=hT[:, c, c0:c0 + wd],
                                start=(c == 0), stop=(c == 7)),
                                  r=[stok, ("h", c), "hpad"], w=[("ps", b0 + q)])
                    cts = []
                    for q in range(2):
                        pq = ps[b0 + q]
                        cc, ctok = tmp_ring.next()
                        cts.append((cc, ctok))
                        f = fidx[q]
                        if not is_s:
                            P.add("act", lambda e, cc=cc, pq=pq, n=n, q=q, cw=cw, cb=cb: e.activation(
                                out=cc[:, 0:n], in_=pq[:, 2:n + 2], func=AF.Identity, bias=cb[q], scale=cw[q][2]),
                                  r=[("ps", b0 + q), "prm"], w=[ctok])
                            P.add("dve", lambda e, cc=cc, pq=pq, n=n, q=q, cw=cw: e.scalar_tensor_tensor(
                                out=cc[:, 0:n], in0=pq[:, 1:n + 1], scalar=cw[q][1], in1=cc[:, 0:n],
                                op0=ALU.mult, op1=ALU.add), r=[("ps", b0 + q), ctok, "prm"], w=[ctok])
                            P.add("dve", lambda e, cc=cc, pq=pq, n=n, q=q, cw=cw: e.scalar_tensor_tensor(
                                out=cc[:, 0:n], in0=pq[:, 0:n], scalar=cw[q][0], in1=cc[:, 0:n],
                                op0=ALU.mult, op1=ALU.add), r=[("ps", b0 + q), ctok, "prm"], w=[ctok])
                            if s + n == NP:
                                P.add("act", lambda e, pq=pq, n=n, f=f: e.copy(out=stage[:, f, 0:2], in_=pq[:, n:n + 2]),
                                      r=[("ps", b0 + q)], w=["stage"])
                        else:
                            p0 = pastb[:, f, :, 0]
                            p1 = pastb[:, f, :, 1]
                            P.add("act", lambda e, cc=cc, pq=pq, q=q, cw=cw, cb=cb: e.activation(
                                out=cc[:, 0:NS], in_=pq[:, 0:NS], func=AF.Identity, bias=cb[q], scale=cw[q][2]),
                                  r=[("ps", b0 + q), "prm"], w=[ctok])
                            P.add("dve", lambda e, cc=cc, p1=p1, q=q, cw=cw: e.scalar_tensor_tensor(
                                out=cc[:, 0:NS], in0=p1, scalar=cw[q][1], in1=cc[:, 0:NS],
                                op0=ALU.mult, op1=ALU.add), r=["pastb", ctok, "prm"], w=[ctok])
                            P.add("dve", lambda e, cc=cc, p0=p0, q=q, cw=cw: e.scalar_tensor_tensor(
                                out=cc[:, 0:NS], in0=p0, scalar=cw[q][0], in1=cc[:, 0:NS],
                                op0=ALU.mult, op1=ALU.add), r=["pastb", ctok, "prm"], w=[ctok])
                            sv = stage[:, f, 2:2 + 2 * NS].rearrange("p (b k) -> p b k", k=2)
                            P.add("act", lambda e, sv=sv, pq=pq: e.copy(out=sv[:, :, 1], in_=pq[:, 0:NS]),
                                  r=[("ps", b0 + q)], w=["stage"])
                            P.add("pool", lambda e, sv=sv, p1=p1: e.tensor_copy(out=sv[:, :, 0], in_=p1),
                                  r=["pastb"], w=["stage"])
                    (ca, catok), (cg, cgtok) = cts
                    nn = NS if is_s else n
                    P.add("act", lambda e, cg=cg, nn=nn: e.activation(out=cg[:, 0:nn], in_=cg[:, 0:nn], func=AF.Silu),
                          r=[cgtok], w=[cgtok])
                    P.add("dve", lambda e, ca=ca, cg=cg, nn=nn, ii=ii, s=s: e.tensor_tensor(
                        out=act[:, ii, s:s + nn], in0=ca[:, 0:nn], in1=cg[:, 0:nn], op=ALU.mult),
                          r=[catok, cgtok], w=[("wk", ii)])
            out_proj(W["ffn_w_down"][li], chunks, act, lambda ci: ("wk", ci), [4, 5])
        P.dma("sp", ffn_out[li], stage[:, :, :], r=["stage"])


    CT3 = conv_tiles(3)

    def ssd_layer(li):
        rmsnorm("norm_mix", (li,))
        w_in = W["ssd_w_in"][0]
        prm_dtb, _, _ = prm_ap("ssd_dtb")
        prm_alog, _, _ = prm_ap("ssd_alog")
        pst = pastb[:, :, :, :].rearrange("p c b k -> p (c b k)")[:, 0:24 * NS * 3] \
            .rearrange("p (c b k) -> p c b k", c=24, b=NS)
        P.dma("sp", pst, ssmc_past_in, w=["pastb"])
        stg = stage[:, :, :].rearrange("p c k -> p (c k)")[:, 0:24 * 51].rearrange("p (c k) -> p c k", c=24)
        slot, stok = ws_ring.next()
        load_w(w_in, list(range(8)), 5120, 32, slot, 0, stok)
        wdt = wview(slot, 0, 8, 32)
        for tc in range(17):
            t0 = tc * 128
            w = 128 if tc < 16 else NS
            bank, col = (0, tc * 32) if tc < 16 else (1, 0)
            for c in range(8):
                mm(ps[bank][0:w, col:col + 32], hT[:, c, PADL + t0:PADL + t0 + w], wdt[:, c, :], c == 0, c == 7,
                   r=[stok, ("h", c)], w=[("ps", bank)])
        dtf = dt_tok[:, :, :].rearrange("p a b -> p (a b)")
        dAf = dtA_tok[:, :, :].rearrange("p a b -> p (a b)")
        vtt(dt_tok[:, 0:16, :], ps[0].rearrange("p (a b) -> p a b", b=32),
            prm_dtb.unsqueeze(1).broadcast_to([128, 16, 32]), ALU.add, r=[("ps", 0), "prm"], w=["dt_tok"])
        vtt(dt_tok[0:NS, 16, :], ps[1][0:NS, 0:32], prm_dtb[0:NS, :], ALU.add, r=[("ps", 1), "prm"], w=["dt_tok"])
        actf(dAf, dtf, AF.Abs, r=["dt_tok"], w=["dtA_tok"])
        actf(dAf, dAf, AF.Exp, r=["dtA_tok"], w=["dtA_tok"], scale=-1.0)
        actf(dAf, dAf, AF.Ln, r=["dtA_tok", "onet"], w=["dtA_tok"], bias=onet[:], scale=1.0)
        vstt(dtf, dtf, 0.0, dAf, ALU.max, ALU.add, r=["dt_tok", "dtA_tok"], w=["dt_tok"])
        actf(a_bc[:], prm_alog, AF.Exp, r=["prm"], w=["a_bc"])
        vtt(dtA_tok[:, :, :], dt_tok[:, :, :], a_bc[:].unsqueeze(1).broadcast_to([128, 17, 32]), ALU.mult,
            r=["dt_tok", "a_bc"], w=["dtA_tok"])
        vts(dAf, dAf, -1.0, None, ALU.mult, None, r=["dtA_tok"], w=["dtA_tok"])

        Dg = rstd[:, 0:1024].rearrange("p (h l) -> p h l", h=8)
        rhs2 = rstd[:, 1024:2048].rearrange("p (h l) -> p h l", h=8)
        xg_tok = [("wk", i) for i in range(4)]
        for g in range(4):
            cols = [2048 + 512 * g + 128 * i for i in range(4)] + [4096 + 128 * g, 4608 + 128 * g]
            bi = 0
            for q, col in enumerate(cols):
                cc = (col - 2048) // 128
                slot, stok = ws_ring.next()
                load_w(w_in, list(range(8)), col, 128, slot, 0, stok)
                wv = wview(slot, 0, 8, 128)
                cw = [pcol("ssd_cw", cc, k) for k in range(4)]
                cb = pcol("ssd_cb", cc)
                tiles = [(s_, n_, False) for (s_, n_) in CT3] + [(NP, NS, True)]
                for (s_, n_, is_s) in tiles:
                    bank = bi % 4
                    bi += 1
                    if is_s:
                        c0, wd = PADL + NP, NS
                    else:
                        c0, wd = PADL + s_ - 3, n_ + 3
                    for c in range(8):
                        mm(ps[bank][:, 0:wd], wv[:, c, :], hT[:, c, c0:c0 + wd], c == 0, c == 7,
                           r=[stok, ("h", c), "hpad"], w=[("ps", bank)])
                    pq = ps[bank]
                    ct, ctok = tmp_ring.next()
                    if not is_s:
                        actf(ct[:, 0:n_], pq[:, 3:n_ + 3], AF.Identity, r=[("ps", bank), "prm"], w=[ctok],
                             bias=cb, scale=cw[3])
                        for k in (2, 1, 0):
                            vstt(ct[:, 0:n_], pq[:, k:k + n_], cw[k], ct[:, 0:n_], ALU.mult, ALU.add,
                                 r=[("ps", bank), ctok, "prm"], w=[ctok])
                        actf(WK[:, q, s_:s_ + n_], ct[:, 0:n_], AF.Silu, r=[ctok], w=[("wk", q)])
                        if s_ + n_ == NP:
                            acp(stg[:, cc, 0:3], pq[:, n_:n_ + 3], r=[("ps", bank)], w=["stage"])
                    else:
                        actf(ct[:, 0:NS], pq[:, 0:NS], AF.Identity, r=[("ps", bank), "prm"], w=[ctok],
                             bias=cb, scale=cw[3])
                        for k in (2, 1, 0):
                            vstt(ct[:, 0:NS], pst[:, cc, :, k], cw[k], ct[:, 0:NS], ALU.mult, ALU.add,
                                 r=["pastb", ctok, "prm"], w=[ctok])
                        actf(WK[:, q, NP:NP + NS], ct[:, 0:NS], AF.Silu, r=[ctok], w=[("wk", q)])
                        sv = stg[:, cc, 3:3 + 3 * NS].rearrange("p (b k) -> p b k", k=3)
                        acp(sv[:, :, 2], pq[:, 0:NS], r=[("ps", bank)], w=["stage"])
                        vcp(sv[:, :, 0], pst[:, cc, :, 1], r=["pastb"], w=["stage"], eng="pool")
                        vcp(sv[:, :, 1], pst[:, cc, :, 2], r=["pastb"], w=["stage"], eng="pool")
            BT = WK[:, 4, :]
            CTm = WK[:, 5, :]
            P.add("pool", lambda e: e.memset(prevT[:], 0.0), w=["prevT"])
            P.add("pool", lambda e: e.memset(prevT_bf[:], 0.0), w=["prevT_bf"])
            for tc in range(16):
                t0 = tc * 128
                psb = ps[0].bitcast(BF16)
                for i in range(4):
                    mm(ps[0][:, i * 128:(i + 1) * 128], WK[:, i, t0:t0 + 128], ident_b[:], True, True,
                       r=[("wk", i), "ident_b"], w=[("ps", 0)])
                mm(ps[1][:, 256:384], BT[:, t0:t0 + 128], ident_b[:], True, True, r=[("wk", 4), "ident_b"],
                   w=[("ps", 1)])
                xdt, xdtok = tmp_ring.next()
                xdtv = xdt[:, :].bitcast(BF16)[:, 0:512].rearrange("p (h d) -> p h d", h=8)
                xdtdv = xdt[:, :].bitcast(BF16)[:, 512:1024].rearrange("p (h d) -> p h d", h=8)
                vtt(xdtv, ps[0].rearrange("p (h d) -> p h d", h=8),
                    dt_tok[:, tc, 8 * g:8 * g + 8].unsqueeze(2).broadcast_to([128, 8, 64]), ALU.mult,
                    r=[("ps", 0), "dt_tok"], w=[xdtok])
                bt, bttok = tmp_ring.next()
                btv = bt[:, :].bitcast(BF16)[:, 0:128]
                cbv = bt[:, :].bitcast(BF16)[:, 128:256]
                acp(btv, ps[1][:, 256:384], r=[("ps", 1)], w=[bttok])
                dA = dtA_tok[:, tc, 8 * g:8 * g + 8]
                mm(ps[1][:, 0:8], tri_f, dA, True, True, r=["cmat", "dtA_tok"], w=[("ps", 1)])
                mm(ps[1][:, 8:16], ones_f, dA, True, True, r=["cmat", "dtA_tok"], w=[("ps", 1)])
                acp(smt[:, 0:16], ps[1][:, 0:16], r=[("ps", 1)], w=["smt"])
                actf(smt[:, 16:32], smt[:, 0:16], AF.Exp, r=["smt"], w=["smt"])
                vtt(smt[:, 32:40], smt[:, 8:16], smt[:, 0:8], ALU.subtract, r=["smt"], w=["smt"])
                actf(smt[:, 40:48], smt[:, 32:40], AF.Exp, r=["smt"], w=["smt"])
                acum = smt[:, 0:8]
                ea = smt[:, 16:24]
                cd = smt[:, 24:32]
                dte = smt[:, 40:48]
                vtt(Dg, ident_f.unsqueeze(1).broadcast_to([128, 8, 128]),
                    acum.unsqueeze(2).broadcast_to([128, 8, 128]), ALU.mult, r=["cmat", "smt"], w=["Dg"])
                vtt(rhs2, maskneg_f.unsqueeze(1).broadcast_to([128, 8, 128]),
                    acum.unsqueeze(2).broadcast_to([128, 8, 128]), ALU.subtract, r=["cmat", "smt"], w=["rhs2"])
                for hb in range(2):
                    bk = 2 + hb
                    mm(ps[bk][:, :], ones_f, Dg[:, 4 * hb:4 * hb + 4, :].rearrange("p h l -> p (h l)"), True, False,
                       r=["cmat", "Dg"], w=[("ps", bk)])
                    mm(ps[bk][:, :], ident_f, rhs2[:, 4 * hb:4 * hb + 4, :].rearrange("p h l -> p (h l)"), False, True,
                       r=["cmat", "rhs2"], w=[("ps", bk)])
                lt, lttok = tmp_ring.next()
                ltv = lt[:, :].bitcast(BF16)
                for hb in range(2):
                    actf(ltv[:, 512 * hb:512 * hb + 512], ps[2 + hb][:, :], AF.Exp, r=[("ps", 2 + hb)], w=[lttok])
                mm(ps[1][:, 128:256], BT[:, t0:t0 + 128], CTm[:, t0:t0 + 128], True, True,
                   r=[("wk", 4), ("wk", 5)], w=[("ps", 1)])
                acp(cbv, ps[1][:, 128:256], r=[("ps", 1)], w=[bttok])
                mt, mttok = tmp_ring.next()
                mtv = mt[:, :].bitcast(BF16).rearrange("p (h l) -> p h l", h=8)
                vtt(mtv, ltv.rearrange("p (h l) -> p h l", h=8), cbv.unsqueeze(1).broadcast_to([128, 8, 128]),
                    ALU.mult, r=[lttok, bttok], w=[mttok])
                for h in range(8):
                    mm(ps[4][:, h * 64:(h + 1) * 64], mtv[:, h, :], xdtv[:, h, :], True, True,
                       r=[mttok, xdtok], w=[("ps", 4)])
                mm(ps[5][:, :], CTm[:, t0:t0 + 128], prevT_bf[:], True, True, r=[("wk", 5), "prevT_bf"],
                   w=[("ps", 5)])
                yd, ydtok = tmp_ring.next()
                acp(yd[:, :], ps[4][:, :], r=[("ps", 4)], w=[ydtok])
                yt_, yttok = tmp_ring.next()
                vtt(yt_.rearrange("p (h d) -> p h d", h=8), ps[5].rearrange("p (h d) -> p h d", h=8),
                    ea.unsqueeze(2).broadcast_to([128, 8, 64]), ALU.mult, r=[("ps", 5), "smt"], w=[yttok])
                vtt(yt_[:, :], yt_[:, :], yd[:, :], ALU.add, r=[yttok, ydtok], w=[yttok])
                vtt(xdtdv, xdtv, dte.unsqueeze(2).broadcast_to([128, 8, 64]), ALU.mult, r=[xdtok, "smt"], w=[xdtok])
                mm(ps[6][:, :], btv, xdtdv.rearrange("p h d -> p (h d)"), True, True, r=[bttok, xdtok], w=[("ps", 6)])
                vtt(prevT[:].rearrange("p (h d) -> p h d", h=8), prevT[:].rearrange("p (h d) -> p h d", h=8),
                    cd.unsqueeze(2).broadcast_to([128, 8, 64]), ALU.mult, r=["prevT", "smt"], w=["prevT"])
                vtt(prevT[:], prevT[:], ps[6][:, :], ALU.add, r=["prevT", ("ps", 6)], w=["prevT"])
                acp(prevT_bf[:], prevT[:], r=["prevT"], w=["prevT_bf"])
                for i in range(4):
                    mm(ps[7][:, i * 128:(i + 1) * 128], yt_[:, i * 128:(i + 1) * 128], ident_f, True, True,
                       r=[yttok, "cmat"], w=[("ps", 7)])
                for i in range(4):
                    vstt(WK[:, i, t0:t0 + 128], WK[:, i, t0:t0 + 128], pcol("ssd_D", 4 * g + i),
                         ps[7][:, i * 128:(i + 1) * 128], ALU.mult, ALU.add,
                         r=[("wk", i), ("ps", 7), "prm"], w=[("wk", i)])
            P.dma("sp", ssm_p_out[g], prevT[:], r=["prevT"])
            dexps = []
            for which, src in ((0, dt_tok), (1, dtA_tok)):
                dx, dxtok = tmp_ring.next()
                dexps.append((dx, dxtok))
                for i in range(4):
                    vcp(dx[0:NS, i * 128:(i + 1) * 128].rearrange("p (a b) -> p a b", a=2),
                        src[0:NS, 16, 8 * g + 2 * i:8 * g + 2 * i + 2].unsqueeze(2).broadcast_to([NS, 2, 64]),
                        r=["dt_tok", "dtA_tok"], w=[dxtok])
            for i in range(4):
                for which in range(2):
                    dx, dxtok = dexps[which]
                    mm(ps[0][:, which * 64 + i * 16:which * 64 + i * 16 + 16], dx[0:NS, i * 128:(i + 1) * 128],
                       ident_f[0:NS, 0:NS], True, True, r=[dxtok, "cmat"], w=[("ps", 0)])
            dtx = sx_t[:, 0, :, :]
            dec = sx_t[:, 1, :, :]
            xdx = sx_t[:, 2, :, :]
            acp(dtx, ps[0][:, 0:64].rearrange("p (i b) -> p i b", i=4), r=[("ps", 0)], w=["sx"])
            actf(dec, ps[0][:, 64:128].rearrange("p (i b) -> p i b", i=4), AF.Exp, r=[("ps", 0)], w=["sx"])
            vtt(xdx, WK[:, 0:4, NP:NP + NS], dtx, ALU.mult, r=xg_tok + ["sx"], w=["sx"])
            P.add("pool", lambda e: e.memset(ys_t[:], 0.0), w=["ys"])
            for hb in range(2):
                bd, bdtok = tmp_ring.next()
                bdv = bd[:, :].bitcast(BF16).rearrange("p (b n) -> p b n", b=8)
                cdg, cdtok = tmp_ring.next()
                cdv = cdg[:, :].bitcast(BF16).rearrange("p (b n) -> p b n", b=8)
                vtt(bdv, ident_b[:].unsqueeze(1).broadcast_to([128, 8, 128]),
                    BT[:, NP + 8 * hb:NP + 8 * hb + 8].unsqueeze(2).broadcast_to([128, 8, 128]), ALU.mult,
                    r=["ident_b", ("wk", 4)], w=[bdtok])
                vtt(cdv, ident_b[:].unsqueeze(1).broadcast_to([128, 8, 128]),
                    CTm[:, NP + 8 * hb:NP + 8 * hb + 8].unsqueeze(2).broadcast_to([128, 8, 128]), ALU.mult,
                    r=["ident_b", ("wk", 5)], w=[cdtok])
                for k2 in range(2):
                    mm(ps[2 + k2][:, :], ones_b[:], bd[:, :].bitcast(BF16)[:, 512 * k2:512 * k2 + 512], True, True,
                       r=["ones_b", bdtok], w=[("ps", 2 + k2)])
                    mm(ps[4 + k2][:, :], ones_b[:], cdg[:, :].bitcast(BF16)[:, 512 * k2:512 * k2 + 512], True, True,
                       r=["ones_b", cdtok], w=[("ps", 4 + k2)])
                for bl in range(8):
                    b = 8 * hb + bl
                    BBv = ps[2 + bl // 4][:, (bl % 4) * 128:(bl % 4) * 128 + 128]
                    CCv = ps[4 + bl // 4][:, (bl % 4) * 128:(bl % 4) * 128 + 128]
                    sin_, sintok = sst_ring.next()
                    sout, souttok = snew_ring.next()
                    r0 = b * 2048 + g * 512
                    P.dma("sp", sin_[:], ssm_state_in[r0:r0 + 512, :].rearrange("(i p) n -> p i n", p=128), w=[sintok])
                    for i in range(4):
                        tq, tqtok = tmp_ring.next()
                        vts(tq[:, 0:128], BBv, xdx[:, i, b:b + 1], None, ALU.mult, None,
                            r=[("ps", 2 + bl // 4), "sx"], w=[tqtok])
                        vstt(sout[:, i, :], sin_[:, i, :], dec[:, i, b:b + 1], tq[:, 0:128], ALU.mult, ALU.add,
                             r=[sintok, "sx", tqtok], w=[souttok])
                        vstt(tq[:, 128:256], sout[:, i, :], 1.0, CCv, ALU.mult, ALU.mult,
                             r=[souttok, ("ps", 4 + bl // 4)], w=[tqtok, "ys"], accum_out=ys_t[:, i, b:b + 1])
                    P.dma("sp", ssm_s_out[r0:r0 + 512, :].rearrange("(i p) n -> p i n", p=128), sout[:], r=[souttok])
            for i in range(4):
                vstt(WK[:, i, NP:NP + NS], WK[:, i, NP:NP + NS], pcol("ssd_D", 4 * g + i), ys_t[:, i, :],
                     ALU.mult, ALU.add, r=[("wk", i), "ys", "prm"], w=[("wk", i)])
            slot, stok = ws_ring.next()
            load_w(w_in, list(range(8)), 512 * g, 512, slot, 0, stok)
            wz = wview(slot, 0, 8, 512)
            for (t0, w) in TT:
                for i in range(4):
                    bank = i % 2
                    for c in range(8):
                        mm(ps[bank][:, 0:w], wz[:, c, i * 128:(i + 1) * 128], hT[:, c, PADL + t0:PADL + t0 + w],
                           c == 0, c == 7, r=[stok, ("h", c)], w=[("ps", bank)])
                    sz, sztok = tmp_ring.next()
                    actf(sz[:, 0:w], ps[bank][:, 0:w], AF.Silu, r=[("ps", bank)], w=[sztok])
                    vtt(WK[:, i, t0:t0 + w], WK[:, i, t0:t0 + w], sz[:, 0:w], ALU.mult, r=[("wk", i), sztok],
                        w=[("wk", i)])
                    sq, sqtok = tmp_ring.next()
                    sqv = sq[:, :].bitcast(BF16)
                    actf(sqv[:, 0:w], WK[:, i, t0:t0 + w], AF.Square, r=[("wk", i)], w=[sqtok])
                    mm(ps[2][:, 0:w], ones_g[:], sqv[:, 0:w], i == 0, i == 3, r=["ones_g", sqtok], w=[("ps", 2)])
                rs, rstok = tmp_ring.next()
                actf(rs[:, 0:w], ps[2][:, 0:w], AF.Sqrt, r=[("ps", 2), "epst"], w=[rstok], bias=epst[:], scale=1.0)
                P.add("dve", lambda e, rs=rs, w=w: e.reciprocal(out=rs[:, 0:w], in_=rs[:, 0:w]), r=[rstok], w=[rstok])
                for i in range(4):
                    vstt(WK[:, i, t0:t0 + w], WK[:, i, t0:t0 + w], pcol("ssd_nw", 4 * g + i), rs[:, 0:w],
                         ALU.mult, ALU.mult, r=[("wk", i), rstok, "prm"], w=[("wk", i)])
            if True:
                out_proj(W["ssd_w_out"][0], list(range(4 * g, 4 * g + 4)), WK, lambda ci: ("wk", ci), [6, 7])
        P.dma("sp", ssmc_out, stg, r=["stage"])


    def barrier():
        P.add("pe", lambda e: e.matmul(ps[7][0:1, 0:1], lhsT=ident_b[:, 0:1], rhs=ident_b[:, 0:1], start=True, stop=True),
              r=["ident_b"], w=[("ps", 7), ("bar", "pe")])
        P.add("act", lambda e: e.copy(out=dmy["act"][:, 0:1], in_=dmy["act"][:, 1:2]), w=[("bar", "act")])
        P.add("dve", lambda e: e.tensor_copy(out=dmy["dve"][:, 0:1], in_=dmy["dve"][:, 1:2]), w=[("bar", "dve")])
        P.add("pool", lambda e: e.tensor_copy(out=dmy["pool"][:, 0:1], in_=dmy["pool"][:, 1:2]), w=[("bar", "pool")])
        P.dma("sp", dmy["sp"][:, 0:1], dmy["sp"][:, 1:2], w=[("bar", "sp")])

    def attn_layer(li):
        rmsnorm("norm_mix", (li,))
        barrier()
        w_qkv = W["attn_w_qkv"][0]
        WKf = WK[:, :, :].rearrange("p c t -> p (c t)")

        def reg(o, n):
            return WKf[:, o:o + n]
        QT = reg(0, T)
        KT = reg(T, T)
        VT = reg(2 * T, T)
        OT = reg(3 * T, T)
        VA = reg(4 * T, 16 * 2 * 65).rearrange("p (t h e) -> p t h e", t=16, h=2)
        NMT = reg(4 * T + 2080, T)
        CM = reg(5 * T + 2080, 512).rearrange("p (k q) -> p k q", k=2)
        o0 = 5 * T + 2080 + 512
        acm = reg(o0, 1024).bitcast(F32).rearrange("p (a b) -> p a b", a=4)
        Pm_b = reg(o0 + 1024, 128)
        kmf = reg(o0 + 1152, 16).bitcast(F32)
        kmb2 = reg(o0 + 1168, 16)
        ptf = reg(o0 + 1184, 512).bitcast(F32)
        idx_i = reg(o0 + 1696, 512).bitcast(I32)
        pti = reg(o0 + 2208, 512).bitcast(I32)
        assert o0 + 2984 <= 8 * T
        ropes = rstd[:, :].bitcast(BF16).rearrange("p (a t) -> p a t", a=2)
        cosT, sinT = ropes[:, 0, :], ropes[:, 1, :]
        SS = sst[0][:, :, :].rearrange("p a b -> p (a b)")
        QS = SS[:, 0:128].rearrange("p (c b) -> p c b", c=8)
        KS = SS[:, 128:256].rearrange("p (c b) -> p c b", c=8)
        VS = SS[:, 256:384].rearrange("p (c b) -> p c b", c=8)
        OS = SS[:, 384:512].rearrange("p (c b) -> p c b", c=8)
        OSb = sst[1][:, 0, :].bitcast(BF16)[:, 0:128].rearrange("p (c b) -> p c b", c=8)

        for a_ in range(2):
            for h_ in range(2):
                P.dma("pool", ropes[:, a_, h_ * 1032:(h_ + 1) * 1032], rope_in[:, a_, h_ * 1032:(h_ + 1) * 1032],
                      w=["rope"])
        P.dma("pool", CM, cm_in, w=["cm"])
        P.dma("sp", acm, acm_in, w=["acm"])
        P.dma("sp", pti, pt_in.partition_broadcast(128), w=["pti"])
        vcp(Pm_b, acm[:, 0, :], r=["acm"], w=["pmb"])
        vcp(ptf, pti, r=["pti"], w=["ptf"])
        vts(ptf, ptf, 128.0, acm[:, 3, 0:1], ALU.mult, ALU.add, r=["ptf", "acm"], w=["ptf"])
        vcp(idx_i, ptf, r=["ptf"], w=["idx"])
        P.add("pool", lambda e: e.memset(VA[:, :, :, 64:65], 1.0), w=["va"])
        SEL = []
        for hl_, tns in enumerate((dt_tok, dtA_tok)):
            sv_ = tns[:, :, :].rearrange("p a b -> p (a b)")[:, 0:512].bitcast(BF16).rearrange("p (r m) -> p r m", r=8)
            SEL.append(sv_)
            vcp(sv_[0:16, :, :], ident_b[0:16, 8 * hl_:8 * hl_ + 8].unsqueeze(2).broadcast_to([16, 8, 128]),
                r=["ident_b"], w=["sel"])

        import os
        for g in range(8 if os.environ.get('SK_GRP') is None else 0):
            for which, (dst, dtok) in enumerate(((QT, "qt"), (KT, "kt"), (VT, "vt"))):
                if os.environ.get('SK_W%d' % which) is not None:
                    continue
                slot, stok = ws_ring.next()
                load_w(w_qkv, list(range(8)), which * D + g * 128, 128, slot, 0, stok)
                wv = wview(slot, 0, 8, 128)
                for ti, (t0, w) in enumerate(TT):
                    bank = ti % 2
                    for c in range(8):
                        mm(ps[bank][:, 0:w], wv[:, c, :], hT[:, c, PADL + t0:PADL + t0 + w], c == 0, c == 7,
                           r=[stok, ("h", c)], w=[("ps", bank)])
                    pq = ps[bank]
                    if which == 2:
                        vf, vftok = tmp_ring.next()
                        acp(vf[:, 0:w], pq[:, 0:w], r=[("ps", bank)], w=[vftok])
                        P.dma("sp", vT_out[:, g, t0:t0 + w], vf[:, 0:w], r=[vftok])
                        vcp(VT[:, t0:t0 + w], vf[:, 0:w], r=[vftok], w=["vt"], eng="pool")
                        if t0 == NP:
                            vcp(VS[:, g, :], vf[:, 0:NS], r=[vftok], w=["ss"], eng="pool")
                        continue
                    qs, qstok = tmp_ring.next()
                    qsb = qs[:, :].bitcast(BF16)
                    acp(qsb[:, 0:w], pq[:, 0:w], r=[("ps", bank)], w=[qstok])
                    ROT = os.environ.get('SK_ROT') is None
                    if ROT:
                        mm(ps[2 + bank][:, 0:w], Pm_b, qsb[:, 0:w], True, True, r=["pmb", qstok], w=[("ps", 2 + bank)])
                    t1, t1tok = tmp_ring.next()
                    acp(t1[:, 0:w], pq[:, 0:w], r=[("ps", bank)], w=[t1tok])
                    vtt(t1[:, 0:w], t1[:, 0:w], cosT[:, t0:t0 + w], ALU.mult, r=[t1tok, "rope"], w=[t1tok])
                    t2, t2tok = tmp_ring.next()
                    if ROT:
                        acp(t2[:, 0:w], ps[2 + bank][:, 0:w], r=[("ps", 2 + bank)], w=[t2tok])
                        vtt(t2[:, 0:w], t2[:, 0:w], sinT[:, t0:t0 + w], ALU.mult, r=[t2tok, "rope"], w=[t2tok])
                        vtt(t1[:, 0:w], t1[:, 0:w], t2[:, 0:w], ALU.add, r=[t1tok, t2tok], w=[t1tok])
                    acp(dst[:, t0:t0 + w], t1[:, 0:w], r=[t1tok], w=[dtok])
                    if which == 1:
                        P.dma("sp", kT_out[:, g, t0:t0 + w], t1[:, 0:w], r=[t1tok])
                        if t0 < NP and os.environ.get('SK_RED') is None:
                            P.add("dve", lambda e, ti=ti, t1=t1: e.tensor_reduce(
                                out=kmf[:, 2 * ti:2 * ti + 2], in_=t1[:, 0:512].rearrange("p (j k) -> p j k", j=2),
                                axis=AX.X, op=ALU.add), r=[t1tok], w=["kmf"])
                    if t0 == NP:
                        vcp((QS if which == 0 else KS)[:, g, :], t1[:, 0:NS], r=[t1tok], w=["ss"], eng="pool")
            vts(kmf, kmf, 1.0 / 256.0, None, ALU.mult, None, r=["kmf"], w=["kmf"])
            vtt(kmb2.rearrange("p (h j) -> p h j", h=2), kmf.unsqueeze(1).broadcast_to([128, 2, 8]),
                acm[:, 3, 1:3].unsqueeze(2).broadcast_to([128, 2, 8]), ALU.mult, r=["kmf", "acm"], w=["kmb"])
            for tc in range(16 if os.environ.get('SK_VTOK') is None else 0):
                bank = 4 + tc % 2
                mm(ps[bank][:, 0:128], VT[:, tc * 128:(tc + 1) * 128], ident_b[:], True, True, r=["vt", "ident_b"],
                   w=[("ps", bank)])
                acp(VA[:, tc, :, 0:64], ps[bank][:, 0:128].rearrange("p (h d) -> p h d", h=2), r=[("ps", bank)],
                    w=["va"])
            vts(VT, KT, acm[:, 3, 2:3], None, ALU.mult, None, r=["kt", "vt", "acm"], w=["vt"])
            vts(KT, KT, acm[:, 3, 1:2], None, ALU.mult, None, r=["kt", "acm"], w=["kt"])
            KTh = [KT, VT]
            for qi in range(14):
                qc = qi + 2
                mm(ps[6][:, qi * 16:(qi + 1) * 16], QT[:, qc * 128:(qc + 1) * 128], kmb2, True, True,
                   r=["qt", "kmb"], w=[("ps", 6)])
            ga, gatok = tmp_ring.next()
            gb, gbtok = tmp_ring.next()
            gm = ga[:, 0:224]
            m8 = ga[:, 224:448]
            sel = gb[:, 0:224]
            nmb = gb[:, 256:368].bitcast(BF16)

            def v4(ap):
                return ap.rearrange("p (o f j) -> p o f j", o=7, f=4)

            def c4(mat):
                return acm[:, mat, 8:64].rearrange("p (o j) -> p o j", o=7).unsqueeze(2).broadcast_to([128, 7, 4, 8])
            for o_ in range(7):
                vtt(gm[:, o_ * 32:(o_ + 1) * 32].rearrange("p (f j) -> p f j", f=4),
                    ps[6][:, o_ * 32:(o_ + 1) * 32].rearrange("p (f j) -> p f j", f=4),
                    acm[:, 1, (o_ + 1) * 8:(o_ + 2) * 8].unsqueeze(1).broadcast_to([128, 4, 8]), ALU.add,
                    r=[("ps", 6), "acm"], w=[gatok])
            g3 = gm.rearrange("p (r j) -> p r j", j=8)
            t3 = m8.rearrange("p (r j) -> p r j", j=8)
            c3 = sel.rearrange("p (r j) -> p r j", j=8)
            for i_ in range(8):
                if i_ == 0:
                    vtt(c3, g3[:, :, 0:1].broadcast_to([128, 28, 8]), g3, ALU.is_gt, r=[gatok], w=[gbtok])
                else:
                    vtt(t3, g3[:, :, i_:i_ + 1].broadcast_to([128, 28, 8]), g3, ALU.is_gt, r=[gatok], w=[gatok])
                    vtt(c3, c3, t3, ALU.add, r=[gbtok, gatok], w=[gbtok])
            vts(sel, sel, -1.0, 2.5, ALU.mult, ALU.add, r=[gbtok], w=[gbtok])
            vts(sel, sel, 0.5, 0.0, ALU.min, ALU.max, r=[gbtok], w=[gbtok])
            for o_ in range(7):
                vtt(sel[:, o_ * 32:(o_ + 1) * 32].rearrange("p (f j) -> p f j", f=4),
                    sel[:, o_ * 32:(o_ + 1) * 32].rearrange("p (f j) -> p f j", f=4),
                    acm[:, 2, (o_ + 1) * 8:(o_ + 2) * 8].unsqueeze(1).broadcast_to([128, 4, 8]), ALU.mult,
                    r=[gbtok, "acm"], w=[gbtok])
            vts(nmb, sel, 60000.0, -30000.0, ALU.mult, ALU.add, r=[gbtok], w=[gbtok])
            for qi in range(14):
                pb = 6 + qi % 2
                mm(ps[pb][0:16, 0:128], nmb[:, qi * 16:(qi + 1) * 16], ident_b[:],
                   True, True, r=[gbtok, "ident_b"], w=[("ps", pb)])
                acp(NMT[0:16, 256 + qi * 128:256 + qi * 128 + 128], ps[pb][0:16, 0:128], r=[("ps", pb)], w=["nmt"])
            sbi = 0
            import os
            for qb in range(8 if os.environ.get('SK_ATT') is None else 0):
                q0 = qb * 256
                nkt = 2 * (qb + 1)
                ottok = "otb"
                otv = reg(o0 + 2720, 256).rearrange("p (a f) -> p a f", a=2)
                rz = reg(o0 + 2976, 8).bitcast(F32)
                for hl in range(2):
                    rows = slice(64 * hl, 64 * hl + 64)
                    accp = ((4, 5), (3, 7))[(qb * 2 + hl) % 2]
                    for kt in range(nkt):
                        j = kt // 2
                        sbank = sbi % 3
                        sbi += 1
                        mm(ps[sbank][:, 0:256], KTh[hl][:, kt * 128:(kt + 1) * 128], QT[:, q0:q0 + 256], True, False,
                           r=["kt", "vt", "qt"], w=[("ps", sbank)])
                        if j < qb:
                            r_ = hl * 8 + j
                            mm(ps[sbank][:, 0:256], SEL[hl][0:16, j, :],
                               NMT[0:16, q0:q0 + 256], False, True, r=["sel", "nmt"], w=[("ps", sbank)])
                        else:
                            mm(ps[sbank][:, 0:256], ident_b[:], CM[:, kt % 2, :], False, True, r=["ident_b", "cm"],
                               w=[("ps", sbank)])
                        pt_, pttok = tmp_ring.next()
                        ptb = pt_[:, :].bitcast(BF16)
                        actf(ptb[:, 0:256], ps[sbank][:, 0:256], AF.Exp, r=[("ps", sbank)], w=[pttok], scale=0.125)
                        for qh in range(2):
                            mm(ps[accp[qh]][:, 0:65], ptb[:, qh * 128:(qh + 1) * 128], VA[:, kt, hl, :],
                               kt == 0, kt == nkt - 1, r=[pttok, "va"], w=[("ps", accp[qh])])
                    for qh in range(2):
                        P.add("dve", lambda e, qh=qh, hl=hl, rz=rz, accp=accp: e.reciprocal(
                            out=rz[:, 2 * hl + qh:2 * hl + qh + 1], in_=ps[accp[qh]][:, 64:65]),
                              r=[("ps", accp[qh])], w=[ottok])
                        vts(otv[:, qh, 64 * hl:64 * hl + 64], ps[accp[qh]][:, 0:64],
                            rz[:, 2 * hl + qh:2 * hl + qh + 1], None, ALU.mult, None, r=[("ps", accp[qh]), ottok],
                            w=[ottok])
                for qh in range(2):
                    mm(ps[6][:, qh * 128:(qh + 1) * 128], otv[:, qh, :], ident_b[:], True, True,
                       r=[ottok, "ident_b"], w=[("ps", 6)])
                acp(OT[:, q0:q0 + 256], ps[6][:, 0:256], r=[("ps", 6)], w=["ot"])
            if os.environ.get('SK_OPJ') is None:
              out_proj(W["attn_w_o"][0], [g], OT.rearrange("p (c t) -> p c t", c=1), lambda ci: "ot", [6, 7],
                     tiles=TT[0:4])

        barrier()
        kt_b = [reg(0, 1024), reg(1024, 1024), reg(2048, 1024)]
        vt_b = [reg(3072, 1024), reg(4096, 1024), reg(5120, 1024)]
        qbc = reg(6144, 1024)
        vnb = reg(7168, 1024)
        prod = reg(8192, 1024)
        prodv = reg(9216, 1024)
        Qd = prodv
        Lall = reg(10240, 512).bitcast(F32)
        Pm_ = reg(10752, 512).bitcast(F32)
        gate = reg(11264, 256).bitcast(F32)
        bias = reg(11520, 256).bitcast(F32)
        m8s = reg(11776, 256).bitcast(F32)
        sm2 = reg(12032, 256).bitcast(F32)
        own, pown, Zt, rZ = sm2[:, 0:16], sm2[:, 16:32], sm2[:, 32:48], sm2[:, 48:64]
        qk, qk2 = sm2[:, 64:72], sm2[:, 72:88]
        kt_ring = Ring("ktile", kt_b)
        vt_ring = Ring("vtile", vt_b)
        import os
        def gather(cache_ap, ring, col):
            tile_, tok_ = ring.next()
            P.add("pool", lambda e: e.indirect_dma_start(
                out=tile_, out_offset=None, in_=cache_ap,
                in_offset=bass.IndirectOffsetOnAxis(ap=idx_i[:, col:col + 1], axis=0)),
                  r=["idx"], w=[tok_], dma=True)
            return tile_, tok_
        kq, vq = [], []
        if os.environ.get('SK_DEC') is None:
            for c3 in range(3):
                kq.append(gather(cache_k_in, kt_ring, c3))
        for b in range(NS if os.environ.get('SK_DEC') is None else 0):
            for (src, dstb, dtok) in ((QS, qbc, "qbc"), (VS, vnb, "vnb")):
                vtt(Qd.rearrange("p (c n) -> p c n", c=8), ident_b[:].unsqueeze(1).broadcast_to([128, 8, 128]),
                    src[:, :, b:b + 1].broadcast_to([128, 8, 128]), ALU.mult, r=["ident_b", "ss"], w=["prodv"])
                for k2 in range(2):
                    mm(ps[k2][:, :], ones_b[:], Qd[:, 512 * k2:512 * k2 + 512], True, True, r=["ones_b", "prodv"],
                       w=[("ps", k2)])
                    acp(dstb[:, 512 * k2:512 * k2 + 512], ps[k2][:, :], r=[("ps", k2)], w=[dtok])
            vtt(qk, QS[:, :, b], KS[:, :, b], ALU.mult, r=["ss"], w=["sm2"])
            vtt(qk2.rearrange("p (c h) -> p c h", c=8), qk.unsqueeze(2).broadcast_to([128, 8, 2]),
                acm[:, 3, 1:3].unsqueeze(1).broadcast_to([128, 8, 2]), ALU.mult, r=["sm2", "acm"], w=["sm2"])
            mm(ps[2][:, 0:16], ones_f, qk2, True, True, r=["cmat", "sm2"], w=[("ps", 2)])
            acp(own, ps[2][:, 0:16], r=[("ps", 2)], w=["sm2"])
            for pg in range(16):
                ktile, kttok = kq.pop(0)
                vtt(prod, ktile, qbc, ALU.mult, r=[kttok, "qbc"], w=["prod"])
                P.add("dve", lambda e, pg=pg: e.tensor_reduce(
                    out=Lall[:, pg * 16:(pg + 1) * 16], in_=prod.rearrange("p (h d) -> p h d", h=16),
                    axis=AX.X, op=ALU.add), r=["prod"], w=["lall"])
                nxt = b * 16 + pg + 3
                if nxt < NS * 16:
                    kq.append(gather(cache_k_in, kt_ring, nxt))
                if pg == 12:
                    for c3 in range(3):
                        vq.append(gather(cache_v_in, vt_ring, b * 16 + c3))
            mm(ps[3][:, 0:256], ones_f, Lall, True, True, r=["cmat", "lall"], w=[("ps", 3)])
            acp(Pm_, ps[3][:, 0:256], r=[("ps", 3)], w=["pm"])
            g4 = Pm_.rearrange("p (j two h) -> p j two h", two=2, h=16)
            vtt(gate.rearrange("p (j h) -> p j h", j=8), g4[:, :, 0, :], g4[:, :, 1, :], ALU.add,
                r=["pm"], w=["gate"])
            for h in range(16):
                P.add("dve", lambda e, h=h: e.max(out=m8s[:, h * 8:(h + 1) * 8],
                                                   in_=gate.rearrange("p (j h) -> p h j", j=8)[:, h, :]),
                      r=["gate"], w=["m8s"])
            vtt(bias.rearrange("p (j h) -> p j h", j=8), gate.rearrange("p (j h) -> p j h", j=8),
                m8s.rearrange("p (h k) -> p h k", k=8)[:, :, 2].unsqueeze(1).broadcast_to([128, 8, 16]), ALU.is_ge,
                r=["gate", "m8s"], w=["bias"])
            vts(bias, bias, 30000.0, -30000.0, ALU.mult, ALU.add, r=["bias"], w=["bias"])
            vtt(Lall.rearrange("p (j two h) -> p j two h", two=2, h=16),
                Lall.rearrange("p (j two h) -> p j two h", two=2, h=16),
                bias.rearrange("p (j h) -> p j h", j=8).unsqueeze(2).broadcast_to([128, 8, 2, 16]), ALU.add,
                r=["lall", "bias"], w=["lall"])
            actf(Pm_, Lall, AF.Exp, r=["lall"], w=["pm"], scale=0.125)
            actf(pown, own, AF.Exp, r=["sm2"], w=["sm2"], scale=0.125)
            mm(ps[3][:, 256:512], ones_f, Pm_, True, True, r=["cmat", "pm"], w=[("ps", 3)])
            P.add("dve", lambda e: e.tensor_reduce(out=Zt, in_=ps[3][:, 256:512].rearrange("p (g h) -> p h g", h=16),
                                                   axis=AX.X, op=ALU.add), r=[("ps", 3)], w=["sm2"])
            vtt(Zt, Zt, pown, ALU.add, r=["sm2"], w=["sm2"])
            P.add("dve", lambda e: e.reciprocal(out=rZ, in_=Zt), r=["sm2"], w=["sm2"])
            for pg in range(16):
                vtile, vttok = vq.pop(0)
                vtt(prodv.rearrange("p (h d) -> p h d", h=16), vtile.rearrange("p (h d) -> p h d", h=16),
                    Pm_[:, pg * 16:(pg + 1) * 16].unsqueeze(2).broadcast_to([128, 16, 64]), ALU.mult,
                    r=[vttok, "pm"], w=["prodv"])
                for k2 in range(2):
                    mm(ps[4 + k2][:, :], ones_b[:], prodv[:, 512 * k2:512 * k2 + 512], pg == 0, pg == 15,
                       r=["ones_b", "prodv"], w=[("ps", 4 + k2)])
                if pg + 3 < 16:
                    vq.append(gather(cache_v_in, vt_ring, b * 16 + pg + 3))
            for k2 in range(2):
                tq, tqtok = tmp_ring.next()
                tq3 = tq[:, :].rearrange("p (h d) -> p h d", h=8)
                vtt(tq3, vnb[:, 512 * k2:512 * k2 + 512].rearrange("p (h d) -> p h d", h=8),
                    pown[:, 8 * k2:8 * k2 + 8].unsqueeze(2).broadcast_to([128, 8, 64]), ALU.mult,
                    r=["vnb", "sm2"], w=[tqtok])
                vtt(tq[:, :], tq[:, :], ps[4 + k2][:, :], ALU.add, r=[tqtok, ("ps", 4 + k2)], w=[tqtok])
                vtt(tq3, tq3, rZ[:, 8 * k2:8 * k2 + 8].unsqueeze(2).broadcast_to([128, 8, 64]), ALU.mult,
                    r=[tqtok, "sm2"], w=[tqtok])
                tq4 = tq[:, :].rearrange("p (c n) -> p c n", c=4)
                vtt(tq4, tq4, ident_f.unsqueeze(1).broadcast_to([128, 4, 128]), ALU.mult, r=[tqtok, "cmat"], w=[tqtok])
                P.add("dve", lambda e, tq4=tq4, k2=k2, b=b: e.tensor_reduce(
                    out=OS[:, 4 * k2:4 * k2 + 4, b], in_=tq4, axis=AX.X, op=ALU.add), r=[tqtok], w=["os"])
        vcp(OSb, OS, r=["os"], w=["osb"])
        out_proj(W["attn_w_o"][0], list(range(8)), OSb, lambda ci: "osb", [6, 7], tiles=[TT[4]], src_t0=NP)
        barrier()

    for li in range(nlayers):
        kind, j = li % 3, li // 3
        if kind == 0:
            sconv_layer(li, j)
        elif kind == 1:
            ssd_layer(li)
        else:
            attn_layer(li)
        ffn_layer(li)

    rmsnorm("norm_final", (), final=True)

    P.emit(stack)
    stack.close()
    return nc


def make_cmat():
    i = np.arange(128)
    ident = (i[:, None] == i[None, :]).astype(np.float32)
    tri = (i[:, None] <= i[None, :]).astype(np.float32)
    maskneg = np.where(i[None, :] >= i[:, None], 0.0, -30000.0).astype(np.float32)
    ones = np.ones((128, 128), np.float32)
    return np.ascontiguousarray(np.stack([ident, tri, maskneg, ones], axis=1))


def make_rope():
    theta, rot = 500000.0, 16
    half = rot // 2
    inv_freq = (np.float32(theta) ** (-(np.arange(half, dtype=np.float32) * np.float32(2.0)) / np.float32(rot))).astype(np.float32)
    pos = np.concatenate([np.arange(NP), np.full(NS, NP)]).astype(np.float32)
    out = np.zeros((128, 2, T), np.float32)
    out[:, 0, :] = 1.0
    for p in range(128):
        d = p % 64
        if d < rot:
            ang = (pos * inv_freq[d % half]).astype(np.float32)
            out[p, 0] = np.cos(ang)
            out[p, 1] = np.sin(ang)
    return out


def make_cm():
    k = np.arange(128)[:, None, None]
    kt = np.arange(2)[None, :, None]
    q = np.arange(256)[None, None, :]
    return np.ascontiguousarray(np.where(kt * 128 + k <= q, 0.0, -30000.0).astype(np.float32))


def make_acm():
    a = np.zeros((128, 4, 128), np.float32)
    for m in range(128):
        d = m % 64
        if d < 8:
            a[m + 8, 0, m] = -1.0
        elif d < 16:
            a[m - 8, 0, m] = 1.0
    for ob in range(8):
        for j in range(8):
            a[:, 1, ob * 8 + j] = 0.0 if j < ob else -1e9
            a[:, 2, ob * 8 + j] = 1.0 if j < ob else 0.0
    a[:, 3, 0] = np.arange(128)
    a[:64, 3, 1] = 1.0
    a[64:, 3, 2] = 1.0
    return a


def fm_tokens(a):
    r, F = a.shape
    return np.ascontiguousarray(a.reshape(r, F // 128, 128).transpose(2, 1, 0))


_CACHE = {}


def kernel(**inp):
    nlayers = int(inp.pop("_nlayers", 4))
    small_cache = bool(inp.pop("_small_cache", False))
    f32 = lambda k: np.asarray(inp[k], dtype=np.float32)
    pk = pack_params(inp)
    params = pk.build()
    nc = build_program(pk.items, pk.cols, nlayers=nlayers, npool=(2 if small_cache else NPOOL))

    x_prompt = f32("x_prompt")
    x_sample = f32("x_sample")
    st_sconv = f32("state_sconv")
    st_ffn = f32("state_ffn_conv")
    shared = {
        "params": params,
        "sconv_w_in": f32("sconv_w_in"), "sconv_w_out": f32("sconv_w_out"),
        "ffn_w_up": f32("ffn_w_up"), "ffn_w_down": f32("ffn_w_down"),
        "ssd_w_in": f32("ssd_w_in"), "ssd_w_out": f32("ssd_w_out"),
        "cmat": make_cmat(),
        "attn_w_qkv": f32("attn_w_qkv"), "attn_w_o": f32("attn_w_o"),
        "rope": make_rope(), "cm_mask": make_cm(), "acm": make_acm(),
    }
    if small_cache:
        shared["cache_k"] = np.zeros((256, D), np.float32)
        shared["cache_v"] = np.zeros((256, D), np.float32)
    else:
        shared["cache_k"] = f32("cache_k")[0].reshape(NPOOL * 128, D)
        shared["cache_v"] = f32("cache_v")[0].reshape(NPOOL * 128, D)
    page_table = np.asarray(inp["page_table"]).astype(np.int32)
    if small_cache:
        page_table = np.zeros_like(page_table)
    st_ssm = f32("state_ssm")
    st_ssmc = f32("state_ssm_conv")
    in_maps = []
    for core in range(NCORES):
        sl = slice(core * NS, (core + 1) * NS)
        xcat = np.concatenate([x_prompt[core], x_sample[sl, 0]], axis=0)
        m = dict(shared)
        m["xT_in"] = fm_tokens(xcat)
        sp = st_sconv[:, sl].reshape(2, NS, 2, 8, 128).transpose(0, 4, 3, 1, 2)
        m["sconv_past"] = np.ascontiguousarray(sp)
        fp = st_ffn[:, sl].reshape(4, NS, 2, 44, 128).transpose(0, 4, 3, 1, 2)
        m["ffn_past"] = np.ascontiguousarray(fp)
        m["ssmc_past"] = np.ascontiguousarray(st_ssmc[0, sl].reshape(NS, 3, 24, 128).transpose(3, 2, 0, 1))
        m["ssm_state"] = np.ascontiguousarray(st_ssm[0, sl].reshape(NS * 2048, 128))
        m["page_tab"] = np.ascontiguousarray(page_table[sl].reshape(1, NS * 16))
        in_maps.append(m)
    res = run_bass_kernel_spmd(nc, in_maps, core_ids=list(range(NCORES)))
    R = res.results
    global _DBG
    _DBG = R

    y_prompt = np.zeros((8, NP, D), np.float32)
    y_sample = np.zeros((128, 1, D), np.float32)
    sconv_p = np.zeros((2, 8, 2, D), np.float32)
    sconv_s = np.zeros((2, 128, 2, D), np.float32)
    ffn_p = np.zeros((4, 8, 2, 2 * DFF), np.float32)
    ffn_s = np.zeros((4, 128, 2, 2 * DFF), np.float32)
    ssm_p = np.zeros((1, 8, 32, 64, 128), np.float32)
    ssm_s = np.zeros((1, 128, 32, 64, 128), np.float32)
    ssmc_p = np.zeros((1, 8, 3, 3072), np.float32)
    ssmc_s = np.zeros((1, 128, 3, 3072), np.float32)
    k_p = np.zeros((1, 8, NP, 16, 64), np.float32)
    v_p = np.zeros((1, 8, NP, 16, 64), np.float32)
    k_s = np.zeros((1, 128, 1, 16, 64), np.float32)
    v_s = np.zeros((1, 128, 1, 16, 64), np.float32)
    for core in range(NCORES):
        sl = slice(core * NS, (core + 1) * NS)
        r = R[core]
        yT = r["yT_out"]
        yt = yT.transpose(2, 1, 0).reshape(T, D)
        y_prompt[core] = yt[:NP]
        y_sample[sl, 0] = yt[NP:]
        so = r["sconv_out"]
        sconv_p[:, core] = so[:, :, :, 0:2].transpose(0, 3, 2, 1).reshape(2, 2, D)
        ss = so[:, :, :, 2:].reshape(2, 128, 8, NS, 2).transpose(0, 3, 4, 2, 1).reshape(2, NS, 2, D)
        sconv_s[:, sl] = ss
        fo = r["ffn_out"]
        ffn_p[:, core] = fo[:, :, :, 0:2].transpose(0, 3, 2, 1).reshape(4, 2, 2 * DFF)
        fs = fo[:, :, :, 2:].reshape(4, 128, 44, NS, 2).transpose(0, 3, 4, 2, 1).reshape(4, NS, 2, 2 * DFF)
        ffn_s[:, sl] = fs
        if nlayers >= 2:
            ssm_p[0, core] = r["ssm_p_out"].reshape(4, 128, 8, 64).transpose(0, 2, 3, 1).reshape(32, 64, 128)
            ssm_s[0, sl] = r["ssm_s_out"].reshape(NS, 32, 64, 128)
            co = r["ssmc_out"]
            ssmc_p[0, core] = co[:, :, 0:3].transpose(2, 1, 0).reshape(3, 3072)
            ssmc_s[0, sl] = co[:, :, 3:].reshape(128, 24, NS, 3).transpose(2, 3, 1, 0).reshape(NS, 3, 3072)
        if nlayers >= 3:
            for (dstp, dsts, nm) in ((k_p, k_s, "kT_out"), (v_p, v_s, "vT_out")):
                kt_ = r[nm].transpose(2, 1, 0).reshape(T, D)
                dstp[0, core] = kt_[:NP].reshape(NP, 16, 64)
                dsts[0, sl, 0] = kt_[NP:].reshape(NS, 16, 64)
    H, Pd, N = 32, 64, 128
    return (y_prompt, y_sample, sconv_p, sconv_s,
            ssm_p, ssm_s, ssmc_p, ssmc_s,
            k_p, v_p, k_s, v_s,
            ffn_p, ffn_s)
```
